# Optimizing a Trainium2 kernel written in Bass

```python
import jax, jax.numpy as jnp
from jax import lax
import numpy as np

D_MODEL = 1024
BATCH = 8
SEQ = 2048
DEPTH = 2
DEC_BATCH = 128
DEC_SEQ = 1
PAST_LEN = 2048
PAGE_SIZE = 128

HEAD_DIM = 64
ROT_DIM = HEAD_DIM // 4
ROPE_THETA = 500000.0
Q_BLOCK = 128
EPS = 1e-6
FOX_HEADS = 8
FOX_WIDTH = FOX_HEADS * HEAD_DIM
RNN_WIDTH = D_MODEL // 2
RNN_BLOCKS = 8
RNN_BLOCK_W = RNN_WIDTH // RNN_BLOCKS
CONV_W = 4
LRU_C = 8.0
DSA_HEADS = D_MODEL // HEAD_DIM
DSA_KV_HEADS = 4
DSA_GROUP = DSA_HEADS // DSA_KV_HEADS
IDX_HEADS = 8
IDX_DIM = 64
TOPK_MAX = 256
D_FF = 4 * D_MODEL
N_FOX_LAYERS = (DEPTH + 1) // 2
N_DSA_LAYERS = DEPTH // 2
AB_SIZES = (FOX_WIDTH, FOX_WIDTH, FOX_WIDTH, FOX_HEADS, RNN_WIDTH, RNN_WIDTH)
C_SIZES = (DSA_HEADS * HEAD_DIM, DSA_KV_HEADS * HEAD_DIM, DSA_KV_HEADS * HEAD_DIM, IDX_HEADS * IDX_DIM, IDX_DIM, IDX_HEADS)

kernel_name = 'hybrid_fox_rglru_dsa_step'


def split_cols(x, sizes):
    return jnp.split(x, [int(c) for c in np.cumsum(sizes)[:-1]], axis=-1)


def rms_norm(x, g):
    xf = x.astype(jnp.float32)
    y = xf * lax.rsqrt(jnp.mean(xf * xf, axis=-1, keepdims=True) + EPS)
    return (y * g.astype(jnp.float32)).astype(x.dtype)


def layer_norm(x, g, b):
    xf = x.astype(jnp.float32)
    mu = jnp.mean(xf, axis=-1, keepdims=True)
    var = jnp.mean(jnp.square(xf - mu), axis=-1, keepdims=True)
    return ((xf - mu) * lax.rsqrt(var + EPS) * g.astype(jnp.float32) + b.astype(jnp.float32)).astype(x.dtype)


def partial_rope(x, pos):
    half = ROT_DIM // 2
    inv = ROPE_THETA ** (-jnp.arange(half, dtype=jnp.float32) / half)
    ang = pos.astype(jnp.float32)[:, None] * inv[None, :]
    cos = jnp.cos(ang)[None, :, None, :]
    sin = jnp.sin(ang)[None, :, None, :]
    xr = x[..., :ROT_DIM].astype(jnp.float32)
    x1, x2 = xr[..., :half], xr[..., half:]
    rot = jnp.concatenate([x1 * cos - x2 * sin, x2 * cos + x1 * sin], axis=-1)
    return jnp.concatenate([rot.astype(x.dtype), x[..., ROT_DIM:]], axis=-1)


def gather_pages(pool, page_table):
    g = pool[page_table]
    return g.reshape((g.shape[0], g.shape[1] * g.shape[2]) + g.shape[3:])


def map_query_blocks(fn, qpos, *qs):
    nb = qpos.shape[0] // Q_BLOCK
    blocks = tuple(jnp.swapaxes(a.reshape((a.shape[0], nb, Q_BLOCK) + a.shape[2:]), 0, 1) for a in qs)
    out = lax.map(lambda args: fn(*args), (qpos.reshape(nb, Q_BLOCK),) + blocks)
    out = jnp.swapaxes(out, 0, 1)
    return out.reshape((out.shape[0], nb * Q_BLOCK) + out.shape[3:])


def fox_attend(qpos, q, cq, k, v, ck, kpos):
    s = jnp.einsum('bqhd,bkhd->bhqk', q, k).astype(jnp.float32) * (HEAD_DIM ** -0.5)
    s = s + (jnp.swapaxes(cq, 1, 2)[..., :, None] - jnp.swapaxes(ck, 1, 2)[..., None, :])
    s = jnp.where(kpos[None, None, None, :] <= qpos[None, None, :, None], s, -jnp.inf)
    p = jax.nn.softmax(s, axis=-1).astype(v.dtype)
    return jnp.einsum('bhqk,bkhd->bqhd', p, v)


def rglru_branch(xr, conv_prev, h0, conv_w, conv_b, ga_w, ga_b, gx_w, gx_b, lam):
    B, T, _ = xr.shape
    xp = jnp.concatenate([conv_prev.astype(xr.dtype), xr], axis=1)
    xc = conv_b + sum(xp[:, j:j + T] * conv_w[j] for j in range(CONV_W))
    xb = xc.reshape(B, T, RNN_BLOCKS, RNN_BLOCK_W)
    r = jax.nn.sigmoid(jnp.einsum('btnd,nde->btne', xb, ga_w).reshape(B, T, RNN_WIDTH) + ga_b)
    i = jax.nn.sigmoid(jnp.einsum('btnd,nde->btne', xb, gx_w).reshape(B, T, RNN_WIDTH) + gx_b)
    log_a = -LRU_C * r.astype(jnp.float32) * jax.nn.softplus(-lam.astype(jnp.float32))
    a = jnp.exp(log_a)
    u = jnp.sqrt(-jnp.expm1(2.0 * log_a)) * (i * xc).astype(jnp.float32)

    def step(h, au):
        a_t, u_t = au
        h = a_t * h + u_t
        return h, h

    hT, hs = lax.scan(step, h0.astype(jnp.float32), (jnp.swapaxes(a, 0, 1), jnp.swapaxes(u, 0, 1)))
    return jnp.swapaxes(hs, 0, 1).astype(xr.dtype), hT, xp[:, T:]


def fox_lru_mixer(xn, pos, conv_prev, h0, past_kv, past_logf, page_table,
                  w_in, b_f, conv_w, conv_b, ga_w, ga_b, gx_w, gx_b, lam, w_out):
    B, T, _ = xn.shape
    q, k, v, f, xr, gate = split_cols(xn @ w_in, AB_SIZES)
    shp = (B, T, FOX_HEADS, HEAD_DIM)
    q, k, v = q.reshape(shp), k.reshape(shp), v.reshape(shp)
    logf = jax.nn.log_sigmoid((f + b_f).astype(jnp.float32))
    if page_table is None:
        c = jnp.cumsum(logf, axis=1)
        attn = map_query_blocks(lambda pb, qb, cb: fox_attend(pb, qb, cb, k, v, c, pos), pos, q, c)
    else:
        past = gather_pages(past_kv, page_table)
        k_all = jnp.concatenate([past[:, :, 0], k], axis=1)
        v_all = jnp.concatenate([past[:, :, 1], v], axis=1)
        logf_all = jnp.concatenate([gather_pages(past_logf, page_table).astype(jnp.float32), logf], axis=1)
        c = jnp.cumsum(logf_all, axis=1)
        attn = fox_attend(pos, q, c[:, -T:], k_all, v_all, c, jnp.arange(c.shape[1]))
    y, hT, conv_new = rglru_branch(xr, conv_prev, h0, conv_w, conv_b, ga_w, ga_b, gx_w, gx_b, lam)
    mixed = jnp.concatenate([attn.reshape(B, T, FOX_WIDTH), y * jax.nn.gelu(gate)], axis=-1)
    state = (jnp.stack([k, v], axis=2), logf.astype(xn.dtype), conv_new, hT.astype(xn.dtype))
    return mixed @ w_out, state


def dsa_attend(qpos, q, qi, wi, k, v, ki, topk):
    B, Tq = q.shape[:2]
    kpos = jnp.arange(k.shape[1])
    dots = jnp.einsum('bqhd,bkd->bqhk', qi, ki).astype(jnp.float32)
    score = jnp.einsum('bqhk,bqh->bqk', jax.nn.relu(dots), wi.astype(jnp.float32))
    causal = kpos[None, None, :] <= qpos[None, :, None]
    _, idx = lax.top_k(jnp.where(causal, score, -jnp.inf), topk)
    gather = jax.vmap(lambda rows, ids: rows[ids])
    k_sel = gather(k, idx)
    v_sel = gather(v, idx)
    qg = q.reshape(B, Tq, DSA_KV_HEADS, DSA_GROUP, HEAD_DIM)
    s = jnp.einsum('bqngd,bqknd->bqngk', qg, k_sel).astype(jnp.float32) * (HEAD_DIM ** -0.5)
    valid = (idx <= qpos[None, :, None])[:, :, None, None, :]
    p = jax.nn.softmax(jnp.where(valid, s, -jnp.inf), axis=-1).astype(v.dtype)
    o = jnp.einsum('bqngk,bqknd->bqngd', p, v_sel)
    return o.reshape(B, Tq, DSA_HEADS * HEAD_DIM)


def dsa_mixer(xn, pos, past_kv, past_idx_k, page_table, w_in, idx_g, idx_b, w_out):
    B, T, _ = xn.shape
    q, k, v, qi, ki, wi = split_cols(xn @ w_in, C_SIZES)
    q = partial_rope(q.reshape(B, T, DSA_HEADS, HEAD_DIM), pos)
    k = partial_rope(k.reshape(B, T, DSA_KV_HEADS, HEAD_DIM), pos)
    v = v.reshape(B, T, DSA_KV_HEADS, HEAD_DIM)
    qi = partial_rope(qi.reshape(B, T, IDX_HEADS, IDX_DIM), pos)
    ki = partial_rope(layer_norm(ki, idx_g, idx_b)[:, :, None, :], pos)[:, :, 0]
    wi = wi * ((IDX_HEADS * IDX_DIM) ** -0.5)
    if page_table is None:
        topk = min(TOPK_MAX, T // 4)
        o = map_query_blocks(lambda pb, qb, qib, wib: dsa_attend(pb, qb, qib, wib, k, v, ki, topk), pos, q, qi, wi)
    else:
        past = gather_pages(past_kv, page_table)
        k_all = jnp.concatenate([past[:, :, 0], k], axis=1)
        v_all = jnp.concatenate([past[:, :, 1], v], axis=1)
        ki_all = jnp.concatenate([gather_pages(past_idx_k, page_table), ki], axis=1)
        topk = min(TOPK_MAX, k_all.shape[1] // 4)
        o = dsa_attend(pos, q, qi, wi, k_all, v_all, ki_all, topk)
    return o @ w_out, (jnp.stack([k, v], axis=2), ki)


def sq_relu_ffn(x, w1, w2):
    return jnp.square(jax.nn.relu(x @ w1)) @ w2


def setup_inputs(seed: int = 0) -> dict:
    key = jax.random.key(seed)
    ks = jax.random.split(key, 40)

    def nrm(i, shape, scale):
        return jax.random.normal(ks[i], shape, jnp.float32) * scale

    n_pages = PAST_LEN // PAGE_SIZE
    n_used = DEC_BATCH * n_pages
    n_phys = n_used + n_used // 4
    page_table = jax.random.permutation(ks[0], n_phys)[:n_used].reshape(DEC_BATCH, n_pages).astype(jnp.int32)
    ab_in = sum(AB_SIZES)
    c_in = sum(C_SIZES)
    u = jax.random.uniform(ks[1], (N_FOX_LAYERS, RNN_WIDTH), jnp.float32, 0.9, 0.999)
    a0 = u ** (1.0 / LRU_C)
    lam = jnp.log(a0) - jnp.log1p(-a0)
    return {
        'x_prompt': nrm(2, (BATCH, SEQ, D_MODEL), 1.0),
        'x_sample': nrm(3, (DEC_BATCH, DEC_SEQ, D_MODEL), 1.0),
        'cache_fox_kv': nrm(4, (N_FOX_LAYERS, n_phys, PAGE_SIZE, 2, FOX_HEADS, HEAD_DIM), 1.0),
        'cache_fox_logf': jax.nn.log_sigmoid(nrm(5, (N_FOX_LAYERS, n_phys, PAGE_SIZE, FOX_HEADS), 1.0)),
        'state_lru_conv': nrm(6, (N_FOX_LAYERS, DEC_BATCH, CONV_W - 1, RNN_WIDTH), 1.0),
        'state_lru_h': nrm(7, (N_FOX_LAYERS, DEC_BATCH, RNN_WIDTH), 0.5),
        'cache_dsa_kv': nrm(8, (N_DSA_LAYERS, n_phys, PAGE_SIZE, 2, DSA_KV_HEADS, HEAD_DIM), 1.0),
        'cache_dsa_idx_k': nrm(9, (N_DSA_LAYERS, n_phys, PAGE_SIZE, IDX_DIM), 1.0),
        'page_table': page_table,
        'norm_mix': 1.0 + nrm(10, (DEPTH, D_MODEL), 0.02),
        'norm_ffn': 1.0 + nrm(11, (DEPTH, D_MODEL), 0.02),
        'norm_final': 1.0 + nrm(12, (D_MODEL,), 0.02),
        'ab_w_in': nrm(13, (N_FOX_LAYERS, D_MODEL, ab_in), D_MODEL ** -0.5),
        'ab_b_f': nrm(14, (N_FOX_LAYERS, FOX_HEADS), 0.02),
        'ab_conv_w': nrm(15, (N_FOX_LAYERS, CONV_W, RNN_WIDTH), CONV_W ** -0.5),
        'ab_conv_b': nrm(16, (N_FOX_LAYERS, RNN_WIDTH), 0.02),
        'ab_gate_a_w': nrm(17, (N_FOX_LAYERS, RNN_BLOCKS, RNN_BLOCK_W, RNN_BLOCK_W), RNN_BLOCK_W ** -0.5),
        'ab_gate_a_b': nrm(18, (N_FOX_LAYERS, RNN_WIDTH), 0.02),
        'ab_gate_x_w': nrm(19, (N_FOX_LAYERS, RNN_BLOCKS, RNN_BLOCK_W, RNN_BLOCK_W), RNN_BLOCK_W ** -0.5),
        'ab_gate_x_b': nrm(20, (N_FOX_LAYERS, RNN_WIDTH), 0.02),
        'ab_lambda': lam,
        'ab_w_out': nrm(21, (N_FOX_LAYERS, FOX_WIDTH + RNN_WIDTH, D_MODEL), (FOX_WIDTH + RNN_WIDTH) ** -0.5),
        'c_w_in': nrm(22, (N_DSA_LAYERS, D_MODEL, c_in), D_MODEL ** -0.5),
        'c_idx_norm_g': 1.0 + nrm(23, (N_DSA_LAYERS, IDX_DIM), 0.02),
        'c_idx_norm_b': nrm(24, (N_DSA_LAYERS, IDX_DIM), 0.02),
        'c_w_out': nrm(25, (N_DSA_LAYERS, DSA_HEADS * HEAD_DIM, D_MODEL), (DSA_HEADS * HEAD_DIM) ** -0.5),
        'ffn_w1': nrm(26, (DEPTH, D_MODEL, D_FF), D_MODEL ** -0.5),
        'ffn_w2': nrm(27, (DEPTH, D_FF, D_MODEL), D_FF ** -0.5),
    }


def reference(x_prompt, x_sample, cache_fox_kv, cache_fox_logf, state_lru_conv, state_lru_h,
              cache_dsa_kv, cache_dsa_idx_k, page_table, norm_mix, norm_ffn, norm_final,
              ab_w_in, ab_b_f, ab_conv_w, ab_conv_b, ab_gate_a_w, ab_gate_a_b, ab_gate_x_w, ab_gate_x_b,
              ab_lambda, ab_w_out, c_w_in, c_idx_norm_g, c_idx_norm_b, c_w_out, ffn_w1, ffn_w2):
    pos_p = jnp.arange(SEQ, dtype=jnp.int32)
    pos_s = PAST_LEN + jnp.arange(DEC_SEQ, dtype=jnp.int32)
    hp, hs = x_prompt, x_sample
    fox_p, fox_s, dsa_p, dsa_s = [], [], [], []
    for layer in range(DEPTH):
        li = layer // 2
        if layer % 2 == 0:
            w = (ab_w_in[li], ab_b_f[li], ab_conv_w[li], ab_conv_b[li], ab_gate_a_w[li], ab_gate_a_b[li],
                 ab_gate_x_w[li], ab_gate_x_b[li], ab_lambda[li], ab_w_out[li])
            conv0 = jnp.zeros((hp.shape[0], CONV_W - 1, RNN_WIDTH), hp.dtype)
            h0 = jnp.zeros((hp.shape[0], RNN_WIDTH), jnp.float32)
            out, st = fox_lru_mixer(rms_norm(hp, norm_mix[layer]), pos_p, conv0, h0, None, None, None, *w)
            hp = hp + out
            fox_p.append(st)
            out, st = fox_lru_mixer(rms_norm(hs, norm_mix[layer]), pos_s, state_lru_conv[li], state_lru_h[li],
                                    cache_fox_kv[li], cache_fox_logf[li], page_table, *w)
            hs = hs + out
            fox_s.append(st)
        else:
            w = (c_w_in[li], c_idx_norm_g[li], c_idx_norm_b[li], c_w_out[li])
            out, st = dsa_mixer(rms_norm(hp, norm_mix[layer]), pos_p, None, None, None, *w)
            hp = hp + out
            dsa_p.append(st)
            out, st = dsa_mixer(rms_norm(hs, norm_mix[layer]), pos_s, cache_dsa_kv[li], cache_dsa_idx_k[li],
                                page_table, *w)
            hs = hs + out
            dsa_s.append(st)
        hp = hp + sq_relu_ffn(rms_norm(hp, norm_ffn[layer]), ffn_w1[layer], ffn_w2[layer])
        hs = hs + sq_relu_ffn(rms_norm(hs, norm_ffn[layer]), ffn_w1[layer], ffn_w2[layer])
    y_prompt = rms_norm(hp, norm_final)
    y_sample = rms_norm(hs, norm_final)
    fox_kv_p, fox_logf_p, lru_conv_p, lru_h_p = (jnp.stack(a) for a in zip(*fox_p))
    fox_kv_s, fox_logf_s, lru_conv_s, lru_h_s = (jnp.stack(a) for a in zip(*fox_s))
    dsa_kv_p, dsa_idxk_p = (jnp.stack(a) for a in zip(*dsa_p))
    dsa_kv_s, dsa_idxk_s = (jnp.stack(a) for a in zip(*dsa_s))
    return (y_prompt, y_sample, fox_kv_p, fox_logf_p, lru_conv_p, lru_h_p, dsa_kv_p, dsa_idxk_p,
            fox_kv_s, fox_logf_s, lru_conv_s, lru_h_s, dsa_kv_s, dsa_idxk_s)
```

```python
import bisect
import math
import numpy as np
from contextlib import ExitStack
import concourse.bass as bass
import concourse.mybir as mybir
from concourse.bass_utils import run_bass_kernel_spmd

F32 = mybir.dt.float32
BF16 = mybir.dt.bfloat16
I32 = mybir.dt.int32
ALU = mybir.AluOpType
AF = mybir.ActivationFunctionType
AX = mybir.AxisListType

T = 2048
D = 1024
N = 256
NCH = T // N
TPC = N // 128
NT = T // 128
STAGE = 5


class Buf:
    def __init__(self, name, t):
        self.name = name
        self.t = t
        self.lw = None
        self.rd = {}

    def __getitem__(self, idx):
        return self.t[idx]


class BufView(Buf):
    def __init__(self, base, ap):
        self.base = base
        self.name = base.name + "_v"
        self.t = ap

    lw = property(lambda s: s.base.lw, lambda s, v: setattr(s.base, "lw", v))
    rd = property(lambda s: s.base.rd, lambda s, v: setattr(s.base, "rd", v))


class Rec:
    __slots__ = ("fn", "waits", "inc", "seq", "grp")

    def __init__(self, fn, waits, grp=None):
        self.fn = fn
        self.waits = waits
        self.inc = False
        self.seq = 0
        self.grp = grp


class Grp:
    def __init__(self, sem):
        self.sem = sem
        self.count = 0


class K:
    def __init__(self, nc, stack):
        self.nc = nc
        self.stack = stack
        self.engs = ["pe", "act", "dve", "pool", "sp"]
        self.recs = {e: [] for e in self.engs}
        self.esem = {e: stack.enter_context(nc.semaphore("es_" + e)) for e in self.engs}
        self.ecnt = {e: 0 for e in self.engs}
        self.incidx = {e: [] for e in self.engs}
        self.incseq = {e: [] for e in self.engs}
        self.known = {e: {} for e in self.engs}
        self.grps = []

    def sb(self, name, shape, dt):
        return Buf(name, self.stack.enter_context(self.nc.sbuf_tensor(name, list(shape), dt)))

    def ps(self, name, shape, dt=F32):
        return Buf(name, self.stack.enter_context(self.nc.psum_tensor(name, list(shape), dt)))

    def grp(self, name):
        g = Grp(self.stack.enter_context(self.nc.semaphore("g_" + name)))
        self.grps.append(g)
        return g

    def _resolve(self, d):
        if d[0] == "e":
            f, idx = d[1], d[2]
            pos = bisect.bisect_left(self.incidx[f], idx)
            if pos < len(self.incidx[f]):
                return self.esem[f], ("e", f), self.incseq[f][pos]
            rec = self.recs[f][-1]
            rec.inc = True
            self.ecnt[f] += 1
            rec.seq = self.ecnt[f]
            self.incidx[f].append(len(self.recs[f]) - 1)
            self.incseq[f].append(rec.seq)
            return self.esem[f], ("e", f), rec.seq
        g, cnt = d[1], d[2]
        return g.sem, ("d", id(g)), cnt

    def _deps(self, e, reads, writes):
        deps = []
        for b in reads:
            if b.lw is not None:
                deps.append(b.lw)
        for b in writes:
            if b.lw is not None:
                deps.append(b.lw)
            deps.extend(b.rd.values())
        waits = {}
        for d in deps:
            if d[0] == "e" and d[1] == e and e == "pe":
                continue
            sem, key, val = self._resolve(d)
            if self.known[e].get(key, 0) >= val:
                continue
            if key not in waits or waits[key][1] < val:
                waits[key] = (sem, val)
        for key, (sem, val) in waits.items():
            self.known[e][key] = val
        return list(waits.values())

    def emit(self, e, fn, reads=(), writes=()):
        waits = self._deps(e, reads, writes)
        self.recs[e].append(Rec(fn, waits))
        idx = len(self.recs[e]) - 1
        for b in reads:
            b.rd[e] = ("e", e, idx)
        for b in writes:
            b.lw = ("e", e, idx)
            b.rd = {}

    def dmaf(self, e, fn, grp, reads=(), writes=()):
        waits = self._deps(e, reads, writes)
        self.recs[e].append(Rec(fn, waits, grp))
        grp.count += 16
        for b in reads:
            b.rd[("d", id(grp))] = ("d", grp, grp.count)
        for b in writes:
            b.lw = ("d", grp, grp.count)
            b.rd = {}

    def dma(self, e, out, in_, grp, reads=(), writes=(), **kw):
        self.dmaf(e, lambda eng: eng.dma_start(out=out, in_=in_, **kw), grp, reads, writes)

    def finish(self):
        waits = [(g.sem, g.count) for g in self.grps if g.count > 0]
        self.recs["sp"].append(Rec(None, waits))

    def replay(self):
        me = self

        def run(e, eng):
            for rec in me.recs[e]:
                for sem, val in rec.waits:
                    eng.wait_ge(sem, val)
                if rec.fn is None:
                    continue
                ins = rec.fn(eng)
                if rec.grp is not None:
                    ins.then_inc(rec.grp.sem, 16)
                if rec.inc:
                    ins.then_inc(me.esem[e], 1)

        with self.nc.Block() as block:
            @block.tensor
            def _(eng):
                run("pe", eng)

            @block.scalar
            def _(eng):
                run("act", eng)

            @block.vector
            def _(eng):
                run("dve", eng)

            @block.gpsimd
            def _(eng):
                run("pool", eng)

            @block.sync
            def _(eng):
                run("sp", eng)


def build_nc(n_phys=2560, do_prompt=True, do_dec=True):
    nc = bass.Bass("TRN2", target_bir_lowering=False)

    def din(name, shape, dt=F32):
        return nc.dram_tensor(name, list(shape), dt, kind="ExternalInput").ap()

    def dout(name, shape):
        return nc.dram_tensor(name, list(shape), F32, kind="ExternalOutput").ap()

    x_p = din("x_p", [T, D])
    norm_mix = din("norm_mix", [2, D])
    norm_ffn = din("norm_ffn", [2, D])
    norm_final = din("norm_final", [1, D])
    ab_w_in = din("ab_w_in", [D, 2568])
    ab_b_f = din("ab_b_f", [1, 8])
    ab_conv_w = din("ab_conv_w", [4, 512])
    ab_conv_b = din("ab_conv_b", [1, 512])
    ab_ga_w = din("ab_ga_w", [8, 64, 64])
    ab_ga_b = din("ab_ga_b", [1, 512])
    ab_gx_w = din("ab_gx_w", [8, 64, 64])
    ab_gx_b = din("ab_gx_b", [1, 512])
    ab_lambda = din("ab_lambda", [1, 512])
    ab_w_out = din("ab_w_out", [D, D])
    c_w_in = din("c_w_in", [D, 2120])
    c_idx_g = din("c_idx_g", [1, 64])
    c_idx_b = din("c_idx_b", [1, 64])
    c_w_out = din("c_w_out", [D, D])
    ffn_w1 = din("ffn_w1", [2, D, 4096])
    ffn_w2 = din("ffn_w2", [2, 4096, D])

    x_s = din("x_s", [16, D])
    pt_in = din("pt", [1, 256], I32)
    conv_s = din("conv_s", [48, 512])
    h_s = din("h_s", [16, 512])
    c_fkv = din("c_fkv", [n_phys * 128, 1024])
    c_flf = din("c_flf", [n_phys * 128, 8])
    c_dkv = din("c_dkv", [n_phys * 128, 512])
    c_dik = din("c_dik", [n_phys * 128, 64])
    y_s = dout("y_s", [16, D])
    fox_kv_s = dout("fox_kv_s", [16, 1024])
    fox_logf_s = dout("fox_logf_s", [16, 8])
    lru_conv_s = dout("lru_conv_s", [48, 512])
    lru_h_s = dout("lru_h_s", [16, 512])
    dsa_kv_s = dout("dsa_kv_s", [16, 512])
    dsa_idxk_s = dout("dsa_idxk_s", [16, 64])
    h1s_scr = nc.dram_tensor("h1s_scr", [128, 8 * 16], F32, kind="Internal").ap()
    y_p = dout("y_p", [T, D])
    fox_kv_p = dout("fox_kv_p", [T, 1024])
    fox_logf_p = dout("fox_logf_p", [T, 8])
    lru_conv_p = dout("lru_conv_p", [3, 512])
    lru_h_p = dout("lru_h_p", [1, 512])
    dsa_kv_p = dout("dsa_kv_p", [T, 512])
    dsa_idxk_p = dout("dsa_idxk_p", [T, 64])
    h1_scr = nc.dram_tensor("h1_scr", [NCH, 128, 8 * N], F32, kind="Internal").ap()

    with ExitStack() as st:
        k = K(nc, st)

        def mm(out, lhsT, rhs, start, stop, R, W):
            k.emit("pe", lambda e: e.matmul(out=out, lhsT=lhsT, rhs=rhs, start=start, stop=stop), R, W)

        def act(out, in_, func, R, W, bias=None, scale=None, accum=None, eng="act"):
            kw = {}
            if bias is not None:
                kw["bias"] = bias
            if scale is not None:
                kw["scale"] = scale
            if accum is not None:
                kw["accum_out"] = accum
            k.emit(eng, lambda e: e.activation(out=out, in_=in_, func=func, **kw), R, W)

        def tt(eng, out, in0, in1, op, R, W):
            k.emit(eng, lambda e: e.tensor_tensor(out=out, in0=in0, in1=in1, op=op), R, W)

        def ts(eng, out, in0, s1, s2, op0, op1, R, W):
            if s2 is None:
                k.emit(eng, lambda e: e.tensor_scalar(out=out, in0=in0, scalar1=s1, scalar2=None, op0=op0), R, W)
            else:
                k.emit(eng, lambda e: e.tensor_scalar(out=out, in0=in0, scalar1=s1, scalar2=s2, op0=op0, op1=op1), R, W)

        def stt(eng, out, in0, scalar, in1, op0, op1, R, W):
            k.emit(eng, lambda e: e.scalar_tensor_tensor(out=out, in0=in0, scalar=scalar, in1=in1, op0=op0, op1=op1), R, W)

        def cp(eng, out, in_, R, W):
            if eng == "act":
                k.emit(eng, lambda e: e.activation(out=out, in_=in_, func=AF.Copy), R, W)
            else:
                k.emit(eng, lambda e: e.tensor_copy(out=out, in_=in_), R, W)

        def memset(eng, ap, val, W):
            k.emit(eng, lambda e: e.memset(ap, val), [], W)

        def asel(out, in_, op, fill, base, pattern, cm, R, W):
            k.emit("pool", lambda e: e.affine_select(out=out, in_=in_, compare_op=op, fill=fill, base=base,
                                                     pattern=pattern, channel_multiplier=cm), R, W)

        pmm = [k.ps("pmm%d" % i, [128, 512]) for i in range(2)]
        pS = [k.ps("pS%d" % i, [128, 512]) for i in range(2)]
        pO = [k.ps("pO%d" % i, [128, 512]) for i in range(2)]
        pT = [k.ps("pT%d" % i, [128, 512]) for i in range(2)]
        rr = {"mm": 0, "S": 0, "O": 0, "T": 0}

        def nxt(kind):
            lst = {"mm": pmm, "S": pS, "O": pO, "T": pT}[kind]
            rr[kind] = (rr[kind] + 1) % len(lst)
            return lst[rr[kind]]

        gc = k.grp("const")
        ident = k.sb("ident", [128, 128], F32)
        identb = k.sb("identb", [128, 128], BF16)
        memset("pool", ident[:, :], 0.0, [ident])
        asel(ident[:, :], ident[:, :], ALU.not_equal, 1.0, 0, [[-1, 128]], 1, [ident], [ident])
        cp("dve", identb[:, :], ident[:, :], [ident], [identb])
        cmaskf = k.sb("cmaskf", [128, 128], F32)
        cmask = k.sb("cmask", [128, 128], BF16)
        memset("pool", cmaskf[:, :], 0.0, [cmaskf])
        asel(cmaskf[:, :], cmaskf[:, :], ALU.is_ge, -30000.0, 0, [[1, 128]], -1, [cmaskf], [cmaskf])
        cp("dve", cmask[:, :], cmaskf[:, :], [cmaskf], [cmask])
        cmq = k.sb("cmq", [128, 128], F32)
        memset("pool", cmq[:, :], 0.0, [cmq])
        asel(cmq[:, :], cmq[:, :], ALU.is_ge, -1e30, 0, [[-1, 128]], 1, [cmq], [cmq])
        onesb = k.sb("onesb", [128, 128], BF16)
        memset("dve", onesb[:, :], 1.0, [onesb])
        onesf = k.sb("onesf", [128, 256], F32)
        memset("dve", onesf[:, :], 1.0, [onesf])
        gmix = k.sb("gmix", [128, 2, 8], F32)
        gffn = k.sb("gffn", [128, 2, 8], F32)
        gfin = k.sb("gfin", [128, 8], F32)
        convw = k.sb("convw", [128, 4, 4], F32)
        convb = k.sb("convb", [128, 4], F32)
        gab = k.sb("gab", [128, 4], F32)
        gxb = k.sb("gxb", [128, 4], F32)
        lam = k.sb("lam", [128, 4], F32)
        clam = k.sb("clam", [128, 4], F32)
        nbf3 = k.sb("nbf3", [24, 1], F32)
        NCD = dict(allow_slow_non_contiguous=True)
        if True:
            for l in range(2):
                k.dma("sp", gmix[:, l, :], norm_mix[l, :].rearrange("(c p) -> p c", p=128), gc, writes=[gmix], **NCD)
                k.dma("sp", gffn[:, l, :], norm_ffn[l, :].rearrange("(c p) -> p c", p=128), gc, writes=[gffn], **NCD)
            k.dma("sp", gfin[:, :], norm_final[0, :].rearrange("(c p) -> p c", p=128), gc, writes=[gfin], **NCD)
            for j in range(4):
                k.dma("sp", convw[:, j, :], ab_conv_w[j, :].rearrange("(c p) -> p c", p=128), gc, writes=[convw], **NCD)
            k.dma("sp", convb[:, :], ab_conv_b[0, :].rearrange("(c p) -> p c", p=128), gc, writes=[convb], **NCD)
            k.dma("sp", gab[:, :], ab_ga_b[0, :].rearrange("(c p) -> p c", p=128), gc, writes=[gab], **NCD)
            k.dma("sp", gxb[:, :], ab_gx_b[0, :].rearrange("(c p) -> p c", p=128), gc, writes=[gxb], **NCD)
            k.dma("sp", lam[:, :], ab_lambda[0, :].rearrange("(c p) -> p c", p=128), gc, writes=[lam], **NCD)
            for j in range(3):
                k.dma("sp", nbf3[8 * j:8 * j + 8, :], ab_b_f[0, :].rearrange("(p o) -> p o", o=1), gc, writes=[nbf3], **NCD)
        sct = [k.sb("sct%d" % i, [128, 512], F32) for i in range(2)]
        gaf = BufView(sct[0], sct[0][:, :].rearrange("p (a b) -> p a b", a=4))
        gxf = BufView(sct[1], sct[1][:, :].rearrange("p (a b) -> p a b", a=4))
        gabd = k.sb("gabd", [128, 4, 128], BF16)
        gxbd = k.sb("gxbd", [128, 4, 128], BF16)
        memset("dve", gaf[:, :, :], 0.0, [gaf])
        memset("dve", gxf[:, :, :], 0.0, [gxf])
        for rb in range(4):
            for i in range(2):
                k.dma("sp", gaf[64 * i:64 * i + 64, rb, 64 * i:64 * i + 64], ab_ga_w[2 * rb + i, :, :], gc, writes=[gaf])
                k.dma("sp", gxf[64 * i:64 * i + 64, rb, 64 * i:64 * i + 64], ab_gx_w[2 * rb + i, :, :], gc, writes=[gxf])
        for b_ in (gmix, gffn, gfin, convw, convb, gab, gxb, lam, nbf3, gaf, gxf):
            if b_.lw is not None and b_.lw[0] == "d":
                b_.lw = ("d", gc, gc.count)
        ts("dve", nbf3[:, :], nbf3[:, :], -1.0, None, ALU.mult, None, [nbf3], [nbf3])
        act(clam[:, :], lam[:, :], AF.Exp, [lam], [clam], scale=-1.0)
        act(clam[:, :], clam[:, :], AF.Ln, [clam], [clam], bias=1.0)
        ts("dve", clam[:, :], clam[:, :], -8.0, None, ALU.mult, None, [clam], [clam])
        cp("dve", gabd[:, :, :], gaf[:, :, :], [gaf], [gabd])
        cp("dve", gxbd[:, :, :], gxf[:, :, :], [gxf], [gxbd])
        m3 = k.sb("m3", [24, 3], F32)
        memset("pool", m3[:, :], 1.0, [m3])
        asel(m3[:, :], m3[:, :], ALU.is_ge, 0.0, 0, [[-8, 3]], 1, [m3], [m3])
        asel(m3[:, :], m3[:, :], ALU.is_ge, 0.0, 7, [[8, 3]], -1, [m3], [m3])
        xtok = k.sb("xtok", [128, TPC, D], F32)
        indf = BufView(xtok, xtok[0:24, 0, :].rearrange("p (h k) -> p h k", h=8))
        ind = k.sb("ind", [24, 8, 128], BF16)
        memset("pool", indf[:, :, :], 0.0, [indf])
        for j in range(3):
            asel(indf[:, :, :], indf[:, :, :], ALU.not_equal, 1.0, -8 * j, [[-1, 8], [0, 128]], 1, [indf], [indf])
        cp("dve", ind[:, :, :], indf[:, :, :], [indf], [ind])


        NTX = NT + 1
        posi = k.sb("posi", [128, NTX], I32)
        posf = k.sb("posf", [128, NTX], F32)
        k.emit("pool", lambda e: e.iota(posi[:, 0:NT], pattern=[[128, NT]], base=0, channel_multiplier=1), [], [posi])
        k.emit("pool", lambda e: e.iota(posi[:, NT:NTX], pattern=[[0, 1]], base=2048, channel_multiplier=0), [posi], [posi])
        cp("dve", posf[:, :], posi[:, :], [posi], [posf])
        ang = k.sb("ang", [128, NTX, 8], F32)
        angi = k.sb("angi", [128, NTX, 8], I32)
        angn = k.sb("angn", [128, NTX, 8], F32)
        cor = k.sb("cor", [128, NTX, 8], F32)
        cosT = k.sb("cosT", [128, NTX, 8], F32)
        sinT = k.sb("sinT", [128, NTX, 8], F32)
        TWO_PI = 2.0 * math.pi
        for phase, dst in ((0.0, sinT), (math.pi / 2.0, cosT)):
            for i in range(8):
                inv = 500000.0 ** (-i / 8.0)
                ts("dve", ang[:, :, i], posf[:, :], inv, phase, ALU.mult, ALU.add, [posf], [ang])
            ts("dve", angn[:, :, :], ang[:, :, :], 1.0 / TWO_PI, None, ALU.mult, None, [ang], [angn])
            cp("dve", angi[:, :, :], angn[:, :, :], [angn], [angi])
            cp("dve", angn[:, :, :], angi[:, :, :], [angi], [angn])
            stt("dve", ang[:, :, :], angn[:, :, :], -TWO_PI, ang[:, :, :], ALU.mult, ALU.add, [angn, ang], [ang])
            ts("dve", cor[:, :, :], ang[:, :, :], math.pi, -TWO_PI, ALU.is_gt, ALU.mult, [ang], [cor])
            tt("dve", ang[:, :, :], ang[:, :, :], cor[:, :, :], ALU.add, [ang, cor], [ang])
            ts("dve", cor[:, :, :], ang[:, :, :], -math.pi, TWO_PI, ALU.is_lt, ALU.mult, [ang], [cor])
            tt("dve", ang[:, :, :], ang[:, :, :], cor[:, :, :], ALU.add, [ang, cor], [ang])
            act(dst[:, :, :], ang[:, :, :], AF.Sin, [ang], [dst])
        lnrow = k.sb("lnrow", [1, 128], F32)
        gln = k.grp("ln")
        k.dma("sp", lnrow[0:1, 0:64], c_idx_g[0:1, :], gln, writes=[lnrow])
        k.dma("sp", lnrow[0:1, 64:128], c_idx_b[0:1, :], gln, writes=[lnrow])
        lnrow.lw = ("d", gln, gln.count)
        lngb = k.sb("lngb", [128, 128], F32)
        pl_ = nxt("T")
        mm(pl_[:, 0:128], onesf[0:1, 0:128], lnrow[0:1, :], True, True, [onesf, lnrow], [pl_])
        cp("dve", lngb[:, :], pl_[:, 0:128], [pl_], [lngb])

        NSLOT = 3
        wslots = [k.sb("wslot%d" % i, [128, 8192], BF16) for i in range(NSLOT)]
        wgrps = [k.grp("w%d" % i) for i in range(NSLOT)]
        wplan = []

        def piece(view, npart, a, b):
            wplan.append((view, npart, a, b))

        win0 = ab_w_in.rearrange("(kc p) n -> p kc n", p=128)
        win1 = c_w_in.rearrange("(kc p) n -> p kc n", p=128)

        def plan_layer0():
            piece(win0[:, :, 0:1024], 128, 8, 1024)
            piece(win0[:, :, 1024:1544], 128, 8, 520)
            piece(win0[:, :, 1544:2568], 128, 8, 1024)
            piece(ab_w_out[0:512, :].rearrange("(h p) n -> p h n", p=64), 64, 8, 1024)
            piece(ab_w_out[512:1024, :].rearrange("(h p) n -> p h n", p=128), 128, 4, 1024)
            plan_ffn(0)

        def plan_ffn(l):
            w1v = ffn_w1[l].rearrange("(kc p) n -> p kc n", p=128)
            w2v = ffn_w2[l].rearrange("(fc p) n -> p fc n", p=128)
            for s in range(4):
                piece(w1v[:, :, 1024 * s:1024 * s + 1024], 128, 8, 1024)
                piece(w2v[:, 8 * s:8 * s + 8, :], 128, 8, 1024)

        def plan_layer1():
            piece(win1[:, :, 0:1024], 128, 8, 1024)
            piece(win1[:, :, 1024:2048], 128, 8, 1024)
            piece(win1[:, :, 2048:2120], 128, 8, 72)
            piece(c_w_out[0:512, :].rearrange("(h p) n -> p h n", p=64), 64, 8, 1024)
            piece(c_w_out[512:1024, :].rearrange("(h p) n -> p h n", p=64), 64, 8, 1024)
            plan_ffn(1)

        if do_prompt:
            for c in range(NCH):
                plan_layer0()
        if do_dec:
            plan_layer0()
        if do_prompt:
            for c in range(NCH):
                plan_layer1()
        if do_dec:
            plan_layer1()
        wstate = {"issued": 0, "used": 0}

        def w_issue():
            i = wstate["issued"]
            if i >= len(wplan):
                return
            view, npart, a, b = wplan[i]
            slot = wslots[i % NSLOT]
            dst = slot[0:npart, 0:a * b].rearrange("p (a b) -> p a b", a=a)
            k.dma("pool", dst, view, wgrps[i % NSLOT], writes=[slot])
            wstate["issued"] += 1

        def w_next(keep_prev=False):
            i = wstate["used"]
            oldest = i - 1 if keep_prev else i
            while wstate["issued"] < min(len(wplan), oldest + NSLOT) or wstate["issued"] <= i:
                w_issue()
            view, npart, a, b = wplan[i]
            slot = wslots[i % NSLOT]
            wstate["used"] += 1
            return slot, slot[0:npart, 0:a * b].rearrange("p (a b) -> p a b", a=a)

        def w_prefetch():
            i = wstate["used"]
            while wstate["issued"] < min(len(wplan), i + NSLOT - 1):
                w_issue()

        kTc = k.sb("kTc", [128, 4, T], BF16)
        Vc = k.sb("Vc", [128, NT, 8, 65], BF16)
        negck = k.sb("negck", [128, NT, 8], F32)
        memset("dve", Vc[:, :, :, 64:65], 1.0, [Vc])
        hcar = k.sb("hcar", [128, 4], F32)
        memset("dve", hcar[:, :], 0.0, [hcar])
        ccar = k.sb("ccar", [24, 1], F32)
        memset("dve", ccar[:, :], 0.0, [ccar])
        xrext = k.sb("xrext", [128, 4, N + 3], F32)
        memset("dve", xrext[:, :, :], 0.0, [xrext])

        hT = k.sb("hT", [128, 8, N], F32)
        xnT = k.sb("xnT", [128, 8, N], BF16)
        rstd = k.sb("rstd", [128, N], F32)
        qT = k.sb("qT", [128, 4, N], BF16)
        wf3 = k.sb("wf3", [128, 8, 24], BF16)
        kvst = k.sb("kvst", [128, 1024], F32)
        lfT = k.sb("lfT", [24, N], F32)
        cT = k.sb("cT", [24, N], F32)
        r1 = k.sb("r1", [24, N], F32)
        hib = k.sb("hib", [24, N], BF16)
        midb = k.sb("midb", [24, N], BF16)
        lob = k.sb("lob", [24, N], BF16)
        cq3f = k.sb("cq3f", [24, N], F32)
        cq3 = k.sb("cq3", [24, N], BF16)
        lfst = k.sb("lfst", [128, TPC, 8], F32)
        PTb = [k.sb("PT%d" % i, [128, N], BF16) for i in range(2)]
        rl = k.sb("rl", [128, N], F32)
        bcs = k.sb("bcs", [64, N], F32)
        attnT = k.sb("attnT", [64, 16, N], BF16)
        lruT = k.sb("lruT", [128, 4, N], BF16)
        tmp = [k.sb("tmp%d" % i, [128, N], F32) for i in range(8)]
        xcb = k.sb("xcb", [128, N], BF16)
        hid = k.sb("hid", [128, 8, N], BF16)
        sq = hid
        rtmp = [k.sb("rtmp%d" % i, [128, N], F32) for i in range(2)]
        tok1 = k.sb("tok1", [128, TPC, 2120], F32)
        kin_t = [k.sb("kin%d" % i, [128, 64], F32) for i in range(TPC)]
        lst = k.sb("lst", [128, 8], F32)
        rt1 = k.sb("rt1", [128, 20, 8], F32)
        rt2 = k.sb("rt2", [128, 20, 8], F32)
        rt3 = k.sb("rt3", [128, 20, 8], F32)
        rt4 = k.sb("rt4", [128, 20, 8], F32)
        qperm = kvst
        qT1 = k.sb("qT1", [128, 8, N], BF16)
        qiT = k.sb("qiT", [128, 4, N], BF16)
        kblk = k.sb("kblk", [128, 128], F32)
        score = BufView(tok1, tok1[:, :, :].rearrange("p a b -> p (a b)")[:, 0:T])
        wk = BufView(xtok, xtok[:, :, :].rearrange("p a b -> p (a b)"))
        m8 = k.sb("m8", [128, 8], F32)
        maskT = k.sb("maskT", [128, NT, 128], BF16)
        wis = k.sb("wis", [128, TPC, 8], F32)
        absw = k.sb("absw", [128, TPC, 8], F32)
        sgn = k.sb("sgn", [128, TPC, 8], F32)
        gdkv = k.grp("dkv")
        gdik = k.grp("dik")
        gio = k.grp("xin")
        gkv = k.grp("kvout")
        glf = k.grp("lfout")
        gh1 = k.grp("h1w")
        gh1r = k.grp("h1r")
        scrb = [Buf("scr%d" % i, None) for i in range(NCH)]
        gy = k.grp("yout")
        gsm = k.grp("small")

        def rmsnorm(gbuf, gsel, n=N):
            for dc in range(8):
                act(sq[:, dc, 0:n], hT[:, dc, 0:n], AF.Square, [hT], [sq])
            p = nxt("mm")
            for dc in range(8):
                mm(p[:, 0:n], onesb[:, :], sq[:, dc, 0:n], dc == 0, dc == 7, [onesb, sq], [p])
            act(rstd[:, 0:n], p[:, 0:n], AF.Sqrt, [p], [rstd], scale=1.0 / D, bias=1e-6)
            k.emit("dve", lambda e: e.reciprocal(out=rstd[:, 0:n], in_=rstd[:, 0:n]), [rstd], [rstd])
            for dc in range(8):
                stt("dve", xnT[:, dc, 0:n], hT[:, dc, 0:n], gsel(dc), rstd[:, 0:n], ALU.mult, ALU.mult, [hT, gbuf, rstd], [xnT])

        def ffn(n=N):
            for s in range(4):
                wb1, w1 = w_next()
                for fj in range(8):
                    p = nxt("mm")
                    for kc in range(8):
                        mm(p[:, 0:n], w1[:, kc, 128 * fj:128 * fj + 128], xnT[:, kc, 0:n], kc == 0, kc == 7, [wb1, xnT], [p])
                    rt = rtmp[fj % 2]
                    act(rt[:, 0:n], p[:, 0:n], AF.Relu, [p], [rt])
                    tt("dve", hid[:, fj, 0:n], rt[:, 0:n], rt[:, 0:n], ALU.mult, [rt], [hid])
                wb2, w2 = w_next()
                for ob in range(8):
                    p = nxt("mm")
                    for fj in range(8):
                        mm(p[:, 0:n], w2[:, fj, 128 * ob:128 * ob + 128], hid[:, fj, 0:n], fj == 0, fj == 7, [wb2, hid], [p])
                    tt("dve", hT[:, ob, 0:n], hT[:, ob, 0:n], p[:, 0:n], ALU.add, [hT, p], [hT])

        ND = 16
        tri = k.sb("tri", [128, 128], F32)
        memset("pool", tri[:, :], 1.0, [tri])
        asel(tri[:, :], tri[:, :], ALU.is_ge, 0.0, -1, [[-1, 128]], 1, [tri], [tri])
        Mmat = k.sb("Mmat", [128, 128], F32)
        memset("pool", Mmat[:, :], 0.0, [Mmat])
        for m_ in range(1, 16):
            asel(Mmat[:, :], Mmat[:, :], ALU.not_equal, 1.0, -8 * m_, [[-1, 128]], 1, [Mmat], [Mmat])
        pt_sb = k.sb("pt_sb", [1, 256], I32)
        gpt = k.grp("pt")
        k.dma("sp", pt_sb[:, :], pt_in[:, :], gpt, writes=[pt_sb])
        kTn = k.sb("kTn", [128, 4, ND], BF16)
        Vn = k.sb("Vn", [ND, 8, 65], BF16)
        memset("dve", Vn[:, :, 64:65], 1.0, [Vn])
        lfpg = k.sb("lfpg", [128, 16, 8], F32)
        Tsb = k.sb("Tsb", [128, 128], F32)
        sml = k.sb("sml", [ND, 64], F32)
        pnm = k.sb("pnm", [ND, ND, 16], BF16)
        cvT = k.sb("cvT", [128, 4, ND, 3], F32)
        cvn = k.sb("cvn", [128, 4, ND, 3], F32)
        h0T = k.sb("h0T", [128, 4, ND], F32)
        hsd = k.sb("hsd", [128, 4, ND], F32)
        hsave = k.sb("hsave", [128, 8, ND], F32)
        sgnm = k.sb("sgnm", [ND, ND, 8], F32)
        maskTd = k.sb("maskTd", [128, 16, ND], F32)
        stage = BufView(tok1, tok1[:, :, :].rearrange("p a b -> p (a b)")[:, 0:4096].rearrange("p (i f) -> p i f", i=4))
        negflat = BufView(negck, negck[:, :, :].rearrange("p a b -> p (a b)"))
        gst = k.grp("stage")
        glp = k.grp("lfpg")
        gds = k.grp("decsmall")
        gdo = k.grp("decout")
        scrs = Buf("scrs", None)
        regs = {}

        ptf = k.sb("ptf", [1, 256], F32)
        cp("dve", ptf[:, :], pt_sb[:, :], [pt_sb], [ptf])
        pidx = k.sb("pidx", [128, 1], I32)
        pidf = k.sb("pidf", [128, 1], F32)
        k.emit("pool", lambda e: e.iota(pidx[:, :], pattern=[[0, 1]], base=0, channel_multiplier=1), [], [pidx])
        cp("dve", pidf[:, :], pidx[:, :], [pidx], [pidf])
        pix = nxt("T")
        mm(pix[:, 0:256], onesf[0:1, 0:128], ptf[0:1, :], True, True, [onesf, ptf], [pix])
        idxf = rtmp[0]
        ts("dve", idxf[:, 0:256], pix[:, 0:256], 128.0, pidf[:, 0:1], ALU.mult, ALU.add, [pix, pidf], [idxf])
        idx_all = k.sb("idx_all", [128, 256], I32)
        cp("dve", idx_all[:, :], idxf[:, 0:256], [idxf], [idx_all])

        def dyn_dma(out_ap, cache, col, grp, writes):
            k.dmaf("pool", lambda e: e.indirect_dma_start(out=out_ap, out_offset=None, in_=cache[:, :],
                                                           in_offset=bass.IndirectOffsetOnAxis(ap=idx_all[:, col:col + 1], axis=0)),
                   grp, reads=[idx_all], writes=writes)

        def load_x_dec():
            k.dma("sp", xtok[0:ND, 0, :], x_s[:, :], gio, writes=[xtok])
            p = nxt("T")
            for dc in range(8):
                k.emit("pe", lambda e, p=p, dc=dc: e.transpose(out=p[:, ND * dc:ND * dc + ND], in_=xtok[0:ND, 0, 128 * dc:128 * dc + 128], identity=ident[0:ND, 0:ND]), [xtok, ident], [p])
            cp("act", hT[:, :, 0:ND], p[:, 0:8 * ND].rearrange("p (a t) -> p a t", a=8), [p], [hT])

        def fox_decode_seq(b):
            for g in range(4):
                for i in range(4):
                    dyn_dma(stage[:, i, :], c_fkv, 16 * b + 4 * g + i, gst, [stage] if i in (0, 3) else [])
                for i in range(4):
                    pg = 4 * g + i
                    p = nxt("T")
                    for hp in range(4):
                        k.emit("pe", lambda e, p=p, i=i, hp=hp: e.transpose(out=p[:, 128 * hp:128 * hp + 128], in_=stage[:, i, 128 * hp:128 * hp + 128], identity=ident[:, :]), [stage, ident], [p])
                    cp("act", kTc[:, :, 128 * pg:128 * pg + 128], p[:, :].rearrange("p (i t) -> p i t", i=4), [p], [kTc])
                    cp("dve", Vc[:, pg, :, 0:64], stage[:, i, 512:1024].rearrange("p (h d) -> p h d", h=8), [stage], [Vc])
            for pg in range(16):
                dyn_dma(lfpg[:, pg, :], c_flf, 16 * b + pg, glp, [lfpg] if pg in (0, 15) else [])
            lff = lfpg[:, :, :].rearrange("p a b -> p (a b)")
            pb_ = nxt("mm")
            pt_ = nxt("T")
            mm(pt_[:, 0:128], lff, onesf[:, 0:128], True, True, [lfpg, onesf], [pt_])
            cp("act", Tsb[:, :], pt_[:, 0:128], [pt_], [Tsb])
            mm(pb_[:, 0:128], tri[:, :], lff, True, False, [tri, lfpg], [pb_])
            mm(pb_[:, 0:128], Tsb[:, :], Mmat[:, :], False, True, [Tsb, Mmat], [pb_])
            cp("act", negflat[:, :], pb_[:, 0:128], [pb_], [negflat])
            S = nxt("S")
            for j in range(16):
                for h in range(8):
                    hp, pbs = h // 2, 64 * (h % 2)
                    mm(S[:, 8 * j + h:8 * j + h + 1], kTc[pbs:pbs + 64, hp, 128 * j:128 * j + 128], qT[pbs:pbs + 64, hp, b:b + 1], True, True, [kTc, qT], [S])
            sd = rtmp[0]
            tt("dve", sd[:, 0:128], S[:, 0:128], negflat[:, :], ALU.add, [S, negflat], [sd])
            PT = PTb[0]
            act(PT[:, 0:128], sd[:, 0:128], AF.Exp, [sd], [PT])
            O = nxt("O")
            for h in range(8):
                for j in range(16):
                    mm(O[0:65, h:h + 1], Vc[:, j, h, :], PT[:, 8 * j + h:8 * j + h + 1], j == 0, False, [Vc, PT], [O])
                mm(O[0:65, h:h + 1], Vn[:, h, :], pnm[:, b, h:h + 1], False, True, [Vn, pnm], [O])
            k.emit("dve", lambda e, O=O: e.reciprocal(out=rl[64:65, 0:8], in_=O[64:65, 0:8]), [O], [rl])
            B = nxt("T")
            mm(B[0:64, 0:8], onesf[64:65, 0:64], rl[64:65, 0:8], True, True, [onesf, rl], [B])
            cp("act", bcs[:, 0:8], B[0:64, 0:8], [B], [bcs])
            tt("dve", attnT[:, 0:8, b:b + 1].rearrange("p h o -> p (h o)"), O[0:64, 0:8], bcs[:, 0:8], ALU.mult, [O, bcs], [attnT])

        def layer0_decode():
            n = ND
            load_x_dec()
            rmsnorm(gmix, lambda dc: gmix[:, 0, dc:dc + 1], n)
            wbA, wA = w_next()
            for hp in range(4):
                p = nxt("mm")
                for kc in range(8):
                    mm(p[:, 0:n], wA[:, kc, 128 * hp:128 * hp + 128], xnT[:, kc, 0:n], kc == 0, kc == 7, [wbA, xnT], [p])
                ts("dve", qT[:, hp, 0:n], p[:, 0:n], 0.125, None, ALU.mult, None, [p], [qT])
                p2 = nxt("mm")
                for kc in range(8):
                    mm(p2[:, 0:n], wA[:, kc, 512 + 128 * hp:512 + 128 * hp + 128], xnT[:, kc, 0:n], kc == 0, kc == 7, [wbA, xnT], [p2])
                cp("dve", kTn[:, hp, :], p2[:, 0:n], [p2], [kTn])
            wbB, wB = w_next(keep_prev=True)
            for j in range(3):
                cp("dve", wf3[:, :, 8 * j:8 * j + 8], wB[:, :, 512:520], [wbB], [wf3])
            qtok = sct[0]
            pq = nxt("mm")
            for kc in range(8):
                mm(pq[0:n, 0:512], xnT[:, kc, 0:n], wA[:, kc, 0:512], kc == 0, kc == 7, [wbA, xnT], [pq])
            cp("act", qtok[0:n, :], pq[0:n, 0:512], [pq], [qtok])
            pk = nxt("mm")
            for kc in range(8):
                mm(pk[0:n, 0:512], xnT[:, kc, 0:n], wA[:, kc, 512:1024], kc == 0, kc == 7, [wbA, xnT], [pk])
            cp("act", kvst[0:n, 0:512], pk[0:n, 0:512], [pk], [kvst])
            pv = nxt("mm")
            for kc in range(8):
                mm(pv[0:n, 0:512], xnT[:, kc, 0:n], wB[:, kc, 0:512], kc == 0, kc == 7, [wbB, xnT], [pv])
            cp("act", kvst[0:n, 512:1024], pv[0:n, 0:512], [pv], [kvst])
            cp("dve", Vn[:, :, 0:64], kvst[0:n, 512:1024].rearrange("p (h d) -> p h d", h=8), [kvst], [Vn])
            k.dma("sp", fox_kv_s[:, :], kvst[0:n, :], gdo, reads=[kvst])
            prod = sct[1]
            tt("dve", prod[0:n, :], qtok[0:n, :], kvst[0:n, 0:512], ALU.mult, [qtok, kvst], [prod])
            k.emit("dve", lambda e: e.reduce_sum(out=sml[:, 0:8], in_=prod[0:n, :].rearrange("p (h d) -> p h d", h=8), axis=AX.X), [prod], [sml])
            pf = nxt("mm")
            for kc in range(8):
                mm(pf[0:24, 0:n], wf3[:, kc, :], xnT[:, kc, 0:n], kc == 0, kc == 7, [wf3, xnT], [pf])
            act(lfT[:, 0:n], pf[0:24, 0:n], AF.Exp, [pf], [lfT], scale=-1.0, bias=nbf3[:, 0:1])
            act(lfT[:, 0:n], lfT[:, 0:n], AF.Ln, [lfT], [lfT], bias=1.0)
            ts("dve", lfT[:, 0:n], lfT[:, 0:n], -1.0, None, ALU.mult, None, [lfT], [lfT])
            p = nxt("T")
            k.emit("pe", lambda e, p=p: e.transpose(out=p[0:n, 0:8], in_=lfT[0:8, 0:n], identity=ident[0:8, 0:8]), [lfT, ident], [p])
            cp("dve", sml[:, 8:16], p[0:n, 0:8], [p], [sml])
            k.dma("sp", fox_logf_s[:, :], sml[:, 8:16], gdo, reads=[sml], **NCD)
            stt("dve", sml[:, 16:24], sml[:, 0:8], 0.125, sml[:, 8:16], ALU.mult, ALU.subtract, [sml], [sml])
            act(sml[:, 24:32], sml[:, 16:24], AF.Exp, [sml], [sml])
            for b in range(ND):
                ts("dve", pnm[:, b, 0:8], sml[:, 24:32], ident[0:ND, b:b + 1], None, ALU.mult, None, [sml, ident], [pnm])
            for b in range(ND):
                fox_decode_seq(b)
            k.dma("sp", sct[0][0:48, :], conv_s[:, :], gds, writes=[sct[0]])
            k.dma("sp", sct[1][0:ND, :], h_s[:, :], gds, writes=[sct[1]])
            for rb in range(4):
                p = nxt("T")
                k.emit("pe", lambda e, p=p, rb=rb: e.transpose(out=p[:, 0:48], in_=sct[0][0:48, 128 * rb:128 * rb + 128], identity=ident[0:48, 0:48]), [sct[0], ident], [p])
                k.emit("pe", lambda e, p=p, rb=rb: e.transpose(out=p[:, 64:64 + ND], in_=sct[1][0:ND, 128 * rb:128 * rb + 128], identity=ident[0:ND, 0:ND]), [sct[1], ident], [p])
                cp("act", cvT[:, rb, :, :].rearrange("p b j -> p (b j)"), p[:, 0:48], [p], [cvT])
                cp("act", h0T[:, rb, :], p[:, 64:64 + ND], [p], [h0T])
            wbC, wC = w_next()
            for rb in range(4):
                xc, gt_, rr_, ii_, aa_, uu_, xr_, gl_ = tmp
                p = nxt("mm")
                for kc in range(8):
                    mm(p[:, 0:n], wC[:, kc, 128 * rb:128 * rb + 128], xnT[:, kc, 0:n], kc == 0, kc == 7, [wbC, xnT], [p])
                cp("act", xr_[:, 0:n], p[:, 0:n], [p], [xr_])
                pg = nxt("mm")
                for kc in range(8):
                    mm(pg[:, 0:n], wC[:, kc, 512 + 128 * rb:512 + 128 * rb + 128], xnT[:, kc, 0:n], kc == 0, kc == 7, [wbC, xnT], [pg])
                cp("act", gt_[:, 0:n], pg[:, 0:n], [pg], [gt_])
                ts("dve", xc[:, 0:n], xr_[:, 0:n], convw[:, 3, rb:rb + 1], convb[:, rb:rb + 1], ALU.mult, ALU.add, [xr_, convw, convb], [xc])
                for j in range(3):
                    stt("dve", xc[:, 0:n], cvT[:, rb, :, j], convw[:, j, rb:rb + 1], xc[:, 0:n], ALU.mult, ALU.add, [cvT, convw, xc], [xc])
                cp("dve", cvn[:, rb, :, 0], cvT[:, rb, :, 1], [cvT], [cvn])
                cp("dve", cvn[:, rb, :, 1], cvT[:, rb, :, 2], [cvT], [cvn])
                cp("dve", cvn[:, rb, :, 2], xr_[:, 0:n], [xr_], [cvn])
                cp("dve", xcb[:, 0:n], xc[:, 0:n], [xc], [xcb])
                pr = nxt("mm")
                mm(pr[:, 0:n], gabd[:, rb, :], xcb[:, 0:n], True, True, [gabd, xcb], [pr])
                pi = nxt("mm")
                mm(pi[:, 0:n], gxbd[:, rb, :], xcb[:, 0:n], True, True, [gxbd, xcb], [pi])
                act(rr_[:, 0:n], pr[:, 0:n], AF.Sigmoid, [pr, gab], [rr_], bias=gab[:, rb:rb + 1])
                act(ii_[:, 0:n], pi[:, 0:n], AF.Sigmoid, [pi, gxb], [ii_], bias=gxb[:, rb:rb + 1])
                act(aa_[:, 0:n], rr_[:, 0:n], AF.Exp, [rr_, clam], [aa_], scale=clam[:, rb:rb + 1])
                tt("dve", uu_[:, 0:n], aa_[:, 0:n], aa_[:, 0:n], ALU.mult, [aa_], [uu_])
                ts("dve", uu_[:, 0:n], uu_[:, 0:n], -1.0, 1.0, ALU.mult, ALU.add, [uu_], [uu_])
                act(uu_[:, 0:n], uu_[:, 0:n], AF.Sqrt, [uu_], [uu_])
                tt("dve", uu_[:, 0:n], uu_[:, 0:n], ii_[:, 0:n], ALU.mult, [uu_, ii_], [uu_])
                tt("dve", uu_[:, 0:n], uu_[:, 0:n], xc[:, 0:n], ALU.mult, [uu_, xc], [uu_])
                tt("dve", aa_[:, 0:n], aa_[:, 0:n], h0T[:, rb, :], ALU.mult, [aa_, h0T], [aa_])
                tt("dve", hsd[:, rb, :], aa_[:, 0:n], uu_[:, 0:n], ALU.add, [aa_, uu_], [hsd])
                tt("dve", gl_[:, 0:n], gt_[:, 0:n], gt_[:, 0:n], ALU.mult, [gt_], [gl_])
                ts("dve", gl_[:, 0:n], gl_[:, 0:n], 0.044715, 1.0, ALU.mult, ALU.add, [gl_], [gl_])
                tt("dve", gl_[:, 0:n], gl_[:, 0:n], gt_[:, 0:n], ALU.mult, [gl_, gt_], [gl_])
                act(gl_[:, 0:n], gl_[:, 0:n], AF.Tanh, [gl_], [gl_], scale=0.7978845608028654)
                stt("dve", gl_[:, 0:n], gl_[:, 0:n], 1.0, gt_[:, 0:n], ALU.add, ALU.mult, [gl_, gt_], [gl_])
                stt("dve", lruT[:, rb, 0:n], gl_[:, 0:n], 0.5, hsd[:, rb, :], ALU.mult, ALU.mult, [gl_, hsd], [lruT])
            p = nxt("T")
            p2 = nxt("T")
            for rb in range(4):
                k.emit("pe", lambda e, p=p, rb=rb: e.transpose(out=p[0:48, 128 * rb:128 * rb + 128], in_=cvn[:, rb, :, :].rearrange("p b j -> p (b j)"), identity=ident[:, :]), [cvn, ident], [p])
                k.emit("pe", lambda e, p2=p2, rb=rb: e.transpose(out=p2[0:ND, 128 * rb:128 * rb + 128], in_=hsd[:, rb, :], identity=ident[:, :]), [hsd, ident], [p2])
            cp("act", sct[0][0:48, :], p[0:48, :], [p], [sct[0]])
            cp("act", sct[1][0:ND, :], p2[0:ND, :], [p2], [sct[1]])
            k.dma("sp", lru_conv_s[:, :], sct[0][0:48, :], gdo, reads=[sct[0]])
            k.dma("sp", lru_h_s[:, :], sct[1][0:ND, :], gdo, reads=[sct[1]])
            wbD, wD = w_next()
            wbE, wE = w_next(keep_prev=True)
            for ob in range(8):
                p = nxt("mm")
                for h in range(8):
                    mm(p[:, 0:n], wD[:, h, 128 * ob:128 * ob + 128], attnT[:, h, 0:n], h == 0, False, [wbD, attnT], [p])
                for rb in range(4):
                    mm(p[:, 0:n], wE[:, rb, 128 * ob:128 * ob + 128], lruT[:, rb, 0:n], False, rb == 3, [wbE, lruT], [p])
                tt("dve", hT[:, ob, 0:n], hT[:, ob, 0:n], p[:, 0:n], ALU.add, [hT, p], [hT])
            rmsnorm(gffn, lambda dc: gffn[:, 0, dc:dc + 1], n)
            ffn(n)
            for dc in range(8):
                cp("dve", hsave[:, dc, :], hT[:, dc, 0:n], [hT], [hsave])

        def layer1_decode():
            n = ND
            gt = NT
            for dc in range(8):
                cp("dve", hT[:, dc, 0:n], hsave[:, dc, :], [hsave], [hT])
            rmsnorm(gmix, lambda dc: gmix[:, 1, dc:dc + 1], n)
            col0 = 0
            for pi_, wcols in enumerate((1024, 1024, 72)):
                wbX, wX = w_next()
                for cb in range(0, wcols, 512):
                    wdt = min(512, wcols - cb)
                    p = nxt("mm")
                    for kc in range(8):
                        mm(p[0:n, 0:wdt], xnT[:, kc, 0:n], wX[:, kc, cb:cb + wdt], kc == 0, kc == 7, [wbX, xnT], [p])
                    cp("act", tok1[0:n, 0, col0 + cb:col0 + cb + wdt], p[0:n, 0:wdt], [p], [tok1])
                col0 += wcols
            kin = kin_t[0]
            cp("dve", Vn[:, 0:4, 0:64], tok1[0:n, 0, 1280:1536].rearrange("p (h d) -> p h d", h=4), [tok1], [Vn])
            k.emit("dve", lambda e: e.reduce_sum(out=lst[0:n, 0:1], in_=tok1[0:n, 0, 2048:2112], axis=AX.X), [tok1], [lst])
            ts("dve", lst[0:n, 0:1], lst[0:n, 0:1], 1.0 / 64.0, None, ALU.mult, None, [lst], [lst])
            ts("dve", kin[0:n, :], tok1[0:n, 0, 2048:2112], lst[0:n, 0:1], None, ALU.subtract, None, [tok1, lst], [kin])
            tt("dve", rt1[0:n, 0:8, :].rearrange("p a b -> p (a b)"), kin[0:n, :], kin[0:n, :], ALU.mult, [kin], [rt1])
            k.emit("dve", lambda e: e.reduce_sum(out=lst[0:n, 1:2], in_=rt1[0:n, 0:8, :].rearrange("p a b -> p (a b)"), axis=AX.X), [rt1], [lst])
            act(lst[0:n, 2:3], lst[0:n, 1:2], AF.Sqrt, [lst], [lst], scale=1.0 / 64.0, bias=1e-6)
            k.emit("dve", lambda e: e.reciprocal(out=lst[0:n, 3:4], in_=lst[0:n, 2:3]), [lst], [lst])
            ts("dve", kin[0:n, :], kin[0:n, :], lst[0:n, 3:4], None, ALU.mult, None, [kin, lst], [kin])
            tt("dve", kin[0:n, :], kin[0:n, :], lngb[0:n, 0:64], ALU.mult, [kin, lngb], [kin])
            tt("dve", kin[0:n, :], kin[0:n, :], lngb[0:n, 64:128], ALU.add, [kin, lngb], [kin])
            for (vw, H, bufv) in ((tok1[0:n, 0, 0:1280].rearrange("p (h d) -> p h d", d=64), 20, tok1),
                                  (tok1[0:n, 0, 1536:2048].rearrange("p (h d) -> p h d", d=64), 8, tok1),
                                  (kin[0:n, :].rearrange("p (h d) -> p h d", d=64), 1, kin)):
                cs = cosT[0:n, gt:gt + 1, :].to_broadcast([n, H, 8])
                sn = sinT[0:n, gt:gt + 1, :].to_broadcast([n, H, 8])
                x1 = vw[:, :, 0:8]
                x2 = vw[:, :, 8:16]
                tt("dve", rt1[0:n, 0:H, :], x1, cs, ALU.mult, [bufv, cosT], [rt1])
                tt("dve", rt2[0:n, 0:H, :], x2, sn, ALU.mult, [bufv, sinT], [rt2])
                tt("dve", rt3[0:n, 0:H, :], x2, cs, ALU.mult, [bufv, cosT], [rt3])
                tt("dve", rt4[0:n, 0:H, :], x1, sn, ALU.mult, [bufv, sinT], [rt4])
                tt("dve", x1, rt1[0:n, 0:H, :], rt2[0:n, 0:H, :], ALU.subtract, [rt1, rt2], [bufv])
                tt("dve", x2, rt3[0:n, 0:H, :], rt4[0:n, 0:H, :], ALU.add, [rt3, rt4], [bufv])
            k.dma("sp", dsa_kv_s[:, :], tok1[0:n, 0, 1024:1536], gdo, reads=[tok1])
            k.dma("sp", dsa_idxk_s[:, :], kin[0:n, :], gdo, reads=[kin])
            for a in range(2):
                src = tok1[0:n, 0, 512 * a:512 * a + 512].rearrange("p (b cc d) -> p cc b d", b=2, cc=4)
                dst = qperm[0:n, 512 * a:512 * a + 512].rearrange("p (cc b d) -> p cc b d", cc=4, b=2)
                ts("dve", dst, src, 0.125, None, ALU.mult, None, [tok1], [qperm])
            p = nxt("T")
            for blk in range(8):
                k.emit("pe", lambda e, p=p, blk=blk: e.transpose(out=p[:, ND * blk:ND * blk + ND], in_=qperm[0:ND, 128 * blk:128 * blk + 128], identity=ident[0:ND, 0:ND]), [qperm, ident], [p])
            cp("act", qT1[:, :, 0:n], p[:, 0:8 * ND].rearrange("p (i t) -> p i t", i=8), [p], [qT1])
            p = nxt("T")
            for i in range(4):
                k.emit("pe", lambda e, p=p, i=i: e.transpose(out=p[:, ND * i:ND * i + ND], in_=tok1[0:ND, 0, 1536 + 128 * i:1536 + 128 * i + 128], identity=ident[0:ND, 0:ND]), [tok1, ident], [p])
            cp("act", qiT[:, :, 0:n], p[:, 0:4 * ND].rearrange("p (i t) -> p i t", i=4), [p], [qiT])
            ts("dve", wis[0:n, 0, :], tok1[0:n, 0, 2112:2120], 512.0 ** -0.5, None, ALU.mult, None, [tok1], [wis])
            act(absw[0:n, 0, :], wis[0:n, 0, :], AF.Abs, [wis], [absw])
            k.emit("act", lambda e: e.sign(out=sgn[0:n, 0, :], in_=wis[0:n, 0, :]), [wis], [sgn])
            for b in range(ND):
                ts("dve", sgnm[:, b, :], sgn[0:n, 0, :], ident[0:ND, b:b + 1], None, ALU.mult, None, [sgn, ident], [sgnm])
            qiv = tok1[0:n, 0, 1536:2048].rearrange("p (h d) -> p h d", h=8)
            kib = kin[0:n, :].rearrange("p (o d) -> p o d", o=1).to_broadcast([n, 8, 64])
            prod = sct[1]
            tt("dve", prod[0:n, :].rearrange("p (h d) -> p h d", h=8), qiv, kib, ALU.mult, [tok1, kin], [prod])
            k.emit("dve", lambda e: e.reduce_sum(out=sml[:, 0:8], in_=prod[0:n, :].rearrange("p (h d) -> p h d", h=8), axis=AX.X), [prod], [sml])
            ts("dve", sml[:, 0:8], sml[:, 0:8], 0.0, None, ALU.max, None, [sml], [sml])
            tt("dve", sml[:, 0:8], sml[:, 0:8], wis[0:n, 0, :], ALU.mult, [sml, wis], [sml])
            k.emit("dve", lambda e: e.reduce_sum(out=sml[:, 8:9], in_=sml[:, 0:8], axis=AX.X), [sml], [sml])
            qv = tok1[0:n, 0, 0:1024].rearrange("p (kv g d) -> p kv g d", kv=4, g=4)
            kv_ = tok1[0:n, 0, 1024:1280].rearrange("p (kv d) -> p kv d", kv=4)
            prod2 = qperm
            for g in range(4):
                tt("dve", prod2[0:n, :].rearrange("p (kv g d) -> p kv g d", kv=4, g=4)[:, :, g, :], qv[:, :, g, :], kv_, ALU.mult, [tok1], [prod2])
            k.emit("dve", lambda e: e.reduce_sum(out=sml[:, 16:32], in_=prod2[0:n, :].rearrange("p (h d) -> p h d", h=16), axis=AX.X), [prod2], [sml])
            scd = BufView(tok1, tok1[:, :, :].rearrange("p a b -> p (a b)")[0:ND, 0:2049])
            wkd = BufView(tok1, tok1[:, :, :].rearrange("p a b -> p (a b)")[0:ND, 2100:4149])
            memset("dve", scd[:, :], 0.0, [scd])
            cp("dve", scd[:, 2048:2049], sml[:, 8:9], [sml], [scd])
            stg = BufView(xtok, xtok[:, :, :].rearrange("p a b -> p (a b)").rearrange("p (i f) -> p i f", i=4))
            for b in range(ND):
                for g in range(4):
                    for i in range(4):
                        dyn_dma(stg[:, i, 0:64], c_dik, 16 * b + 4 * g + i, gst, [stg] if i in (0, 3) else [])
                    p = nxt("T")
                    for i in range(4):
                        cp("dve", kblk[:, 0:64], stg[:, i, 0:64], [stg], [kblk])
                        cp("dve", kblk[:, 64:128], stg[:, i, 0:64], [stg], [kblk])
                        k.emit("pe", lambda e, p=p, i=i: e.transpose(out=p[:, 128 * i:128 * i + 128], in_=kblk[:, :], identity=ident[:, :]), [kblk, ident], [p])
                    cp("act", kTc[:, 2, 512 * g:512 * g + 512], p[:, :], [p], [kTc])
                for h in range(8):
                    half, blk = h % 2, h // 2
                    for kb0 in range(0, 2048, 512):
                        p = nxt("S")
                        mm(p[0:n, 0:512], qiT[64 * half:64 * half + 64, blk, 0:n], kTc[64 * half:64 * half + 64, 2, kb0:kb0 + 512], True, True, [qiT, kTc], [p])
                        sc_ = sct[(h + kb0 // 512) % 2]
                        act(sc_[0:n, :], p[0:n, 0:512], AF.Relu, [p, absw], [sc_], scale=absw[0:n, 0, h:h + 1])
                        stt("dve", scd[:, kb0:kb0 + 512], sc_[0:n, :], sgnm[:, b, h:h + 1], scd[:, kb0:kb0 + 512], ALU.mult, ALU.add, [sc_, sgnm, scd], [scd])
            srcb = scd
            for r in range(32):
                k.emit("dve", lambda e, srcb=srcb: e.max(out=m8[0:n, :], in_=srcb[:, :]), [srcb], [m8])
                if r < 31:
                    k.emit("dve", lambda e, srcb=srcb: e.match_replace(out=wkd[:, :], in_to_replace=m8[0:n, :], in_values=srcb[:, :], imm_value=-1e30), [srcb, m8], [wkd])
                    srcb = wkd
            ts("dve", wkd[:, :], scd[:, :], m8[0:n, 7:8], None, ALU.is_ge, None, [scd, m8], [wkd])
            ts("dve", wkd[:, :], wkd[:, :], -1.0, 30000.0, ALU.add, ALU.mult, [wkd], [wkd])
            for g in range(4):
                p = nxt("T")
                for i in range(4):
                    pg = 4 * g + i
                    k.emit("pe", lambda e, p=p, i=i, pg=pg: e.transpose(out=p[:, ND * i:ND * i + ND], in_=wkd[:, 128 * pg:128 * pg + 128], identity=ident[0:ND, 0:ND]), [wkd, ident], [p])
                cp("act", maskTd[:, 4 * g:4 * g + 4, :], p[:, 0:4 * ND].rearrange("p (i t) -> p i t", i=4), [p], [maskTd])
            for h in range(16):
                stt("dve", sml[:, 32 + h:33 + h], sml[:, 16 + h:17 + h], 0.125, wkd[:, 2048:2049], ALU.mult, ALU.add, [sml, wkd], [sml])
            act(sml[:, 48:64], sml[:, 32:48], AF.Exp, [sml], [sml])
            for b in range(ND):
                ts("dve", pnm[:, b, :], sml[:, 48:64], ident[0:ND, b:b + 1], None, ALU.mult, None, [sml, ident], [pnm])
            stg2 = BufView(xtok, xtok[:, :, :].rearrange("p a b -> p (a b)").rearrange("p (i f) -> p i f", i=4))
            for b in range(ND):
                for g in range(4):
                    for i in range(4):
                        dyn_dma(stg2[:, i, :], c_dkv, 16 * b + 4 * g + i, gst, [stg2] if i in (0, 3) else [])
                    for i in range(4):
                        pg = 4 * g + i
                        p = nxt("T")
                        for blk in range(2):
                            k.emit("pe", lambda e, p=p, i=i, blk=blk: e.transpose(out=p[:, 128 * blk:128 * blk + 128], in_=stg2[:, i, 128 * blk:128 * blk + 128], identity=ident[:, :]), [stg2, ident], [p])
                        cp("act", kTc[:, 0:2, 128 * pg:128 * pg + 128], p[:, 0:256].rearrange("p (i t) -> p i t", i=2), [p], [kTc])
                        cp("dve", Vc[:, pg, 0:4, 0:64], stg2[:, i, 256:512].rearrange("p (h d) -> p h d", h=4), [stg2], [Vc])
                S = nxt("S")
                for j in range(16):
                    for a in range(2):
                        for b2 in range(2):
                            for cc in range(4):
                                h = 8 * a + 4 * b2 + cc
                                blkq, pbs = 4 * a + cc, 64 * b2
                                mm(S[:, 16 * j + h:16 * j + h + 1], kTc[pbs:pbs + 64, a, 128 * j:128 * j + 128], qT1[pbs:pbs + 64, blkq, b:b + 1], True, True, [kTc, qT1], [S])
                PT = PTb[0]
                for j in range(16):
                    act(PT[:, 16 * j:16 * j + 16], S[:, 16 * j:16 * j + 16], AF.Exp, [S, maskTd], [PT], bias=maskTd[:, j, b:b + 1])
                O = nxt("O")
                for h in range(16):
                    kvh = 2 * (h // 8) + (h // 4) % 2
                    for j in range(16):
                        mm(O[0:65, h:h + 1], Vc[:, j, kvh, :], PT[:, 16 * j + h:16 * j + h + 1], j == 0, False, [Vc, PT], [O])
                    mm(O[0:65, h:h + 1], Vn[:, kvh, :], pnm[:, b, h:h + 1], False, True, [Vn, pnm], [O])
                k.emit("dve", lambda e, O=O: e.reciprocal(out=rl[64:65, 0:16], in_=O[64:65, 0:16]), [O], [rl])
                B = nxt("T")
                mm(B[0:64, 0:16], onesf[64:65, 0:64], rl[64:65, 0:16], True, True, [onesf, rl], [B])
                cp("act", bcs[:, 0:16], B[0:64, 0:16], [B], [bcs])
                tt("dve", attnT[:, 0:16, b:b + 1].rearrange("p h o -> p (h o)"), O[0:64, 0:16], bcs[:, 0:16], ALU.mult, [O, bcs], [attnT])
            wbD, wD = w_next()
            wbE, wE = w_next(keep_prev=True)
            for ob in range(8):
                p = nxt("mm")
                for h in range(16):
                    wsl, wbuf = (wD, wbD) if h < 8 else (wE, wbE)
                    mm(p[:, 0:n], wsl[:, h % 8, 128 * ob:128 * ob + 128], attnT[:, h, 0:n], h == 0, h == 15, [wbuf, attnT], [p])
                tt("dve", hT[:, ob, 0:n], hT[:, ob, 0:n], p[:, 0:n], ALU.add, [hT, p], [hT])
            rmsnorm(gffn, lambda dc: gffn[:, 1, dc:dc + 1], n)
            ffn(n)
            rmsnorm(gfin, lambda dc: gfin[:, dc:dc + 1], n)
            p = nxt("T")
            p2 = nxt("T")
            for dc in range(8):
                yt = tmp[dc % 4]
                stt("dve", yt[:, 0:n], hT[:, dc, 0:n], gfin[:, dc:dc + 1], rstd[:, 0:n], ALU.mult, ALU.mult, [hT, gfin, rstd], [yt])
                pp = p if dc < 4 else p2
                k.emit("pe", lambda e, pp=pp, dc=dc, yt=yt: e.transpose(out=pp[0:ND, 128 * (dc % 4):128 * (dc % 4) + 128], in_=yt[:, 0:ND], identity=ident[:, :]), [yt, ident], [pp])
            cp("act", kvst[0:n, 0:512], p[0:n, :], [p], [kvst])
            cp("act", kvst[0:n, 512:1024], p2[0:n, :], [p2], [kvst])
            k.dma("sp", y_s[:, :], kvst[0:n, :], gdo, reads=[kvst])

        for c in (range(NCH) if do_prompt else []):
            t0 = c * N
            k.dma("sp", xtok[:, :, :], x_p[t0:t0 + N, :].rearrange("(j p) d -> p j d", p=128), gio, writes=[xtok])
            for j in range(TPC):
                for g in range(2):
                    p = nxt("T")
                    for i in range(4):
                        dc = 4 * g + i
                        k.emit("pe", lambda e, p=p, i=i, j=j, dc=dc: e.transpose(out=p[:, 128 * i:128 * i + 128], in_=xtok[:, j, 128 * dc:128 * dc + 128], identity=ident[:, :]), [xtok, ident], [p])
                    cp("act", hT[:, 4 * g:4 * g + 4, 128 * j:128 * j + 128], p[:, :].rearrange("p (i t) -> p i t", i=4), [p], [hT])
            rmsnorm(gmix, lambda dc: gmix[:, 0, dc:dc + 1])
            wbA, wA = w_next()
            for hp in range(4):
                p = nxt("mm")
                for kc in range(8):
                    mm(p[:, 0:N], wA[:, kc, 128 * hp:128 * hp + 128], xnT[:, kc, :], kc == 0, kc == 7, [wbA, xnT], [p])
                ts("dve", qT[:, hp, :], p[:, 0:N], 0.125, None, ALU.mult, None, [p], [qT])
                p2 = nxt("mm")
                for kc in range(8):
                    mm(p2[:, 0:N], wA[:, kc, 512 + 128 * hp:512 + 128 * hp + 128], xnT[:, kc, :], kc == 0, kc == 7, [wbA, xnT], [p2])
                cp("dve", kTc[:, hp, t0:t0 + N], p2[:, 0:N], [p2], [kTc])
            wbB, wB = w_next(keep_prev=True)
            for j in range(3):
                cp("dve", wf3[:, :, 8 * j:8 * j + 8], wB[:, :, 512:520], [wbB], [wf3])
            for j in range(TPC):
                gt = c * TPC + j
                pk = nxt("mm")
                for kc in range(8):
                    mm(pk[:, 0:512], xnT[:, kc, 128 * j:128 * j + 128], wA[:, kc, 512:1024], kc == 0, kc == 7, [wbA, xnT], [pk])
                pv = nxt("mm")
                for kc in range(8):
                    mm(pv[:, 0:512], xnT[:, kc, 128 * j:128 * j + 128], wB[:, kc, 0:512], kc == 0, kc == 7, [wbB, xnT], [pv])
                cp("act", kvst[:, 0:512], pk[:, 0:512], [pk], [kvst])
                cp("dve", kvst[:, 512:1024], pv[:, 0:512], [pv], [kvst])
                cp("dve", Vc[:, gt, :, 0:64], kvst[:, 512:1024].rearrange("p (h d) -> p h d", h=8), [kvst], [Vc])
                k.dma("sp", fox_kv_p[t0 + 128 * j:t0 + 128 * j + 128, :], kvst[:, :], gkv, reads=[kvst])
            pf = nxt("mm")
            for kc in range(8):
                mm(pf[0:24, 0:N], wf3[:, kc, :], xnT[:, kc, :], kc == 0, kc == 7, [wf3, xnT], [pf])
            act(lfT[:, :], pf[0:24, 0:N], AF.Exp, [pf], [lfT], scale=-1.0, bias=nbf3[:, 0:1])
            act(lfT[:, :], lfT[:, :], AF.Ln, [lfT], [lfT], bias=1.0)
            ts("dve", lfT[:, :], lfT[:, :], -1.0, None, ALU.mult, None, [lfT], [lfT])
            k.emit("dve", lambda e: e.tensor_tensor_scan(out=cT[:, :], data0=onesf[0:24, 0:N], data1=lfT[:, :], initial=ccar[:, 0:1], op0=ALU.mult, op1=ALU.add), [onesf, lfT, ccar], [cT])
            cp("dve", ccar[:, 0:1], cT[:, N - 1:N], [cT], [ccar])
            cp("dve", hib[:, :], cT[:, :], [cT], [hib])
            tt("dve", r1[:, :], cT[:, :], hib[:, :], ALU.subtract, [cT, hib], [r1])
            cp("dve", midb[:, :], r1[:, :], [r1], [midb])
            tt("dve", r1[:, :], r1[:, :], midb[:, :], ALU.subtract, [r1, midb], [r1])
            cp("dve", lob[:, :], r1[:, :], [r1], [lob])
            ts("dve", cq3f[:, :], hib[:, :], m3[:, 0:1], None, ALU.mult, None, [hib, m3], [cq3f])
            stt("dve", cq3f[:, :], midb[:, :], m3[:, 1:2], cq3f[:, :], ALU.mult, ALU.add, [midb, m3, cq3f], [cq3f])
            stt("dve", cq3f[:, :], lob[:, :], m3[:, 2:3], cq3f[:, :], ALU.mult, ALU.add, [lob, m3, cq3f], [cq3f])
            cp("dve", cq3[:, :], cq3f[:, :], [cq3f], [cq3])
            for j in range(TPC):
                gt = c * TPC + j
                p = nxt("T")
                k.emit("pe", lambda e, p=p, j=j: e.transpose(out=p[:, 0:8], in_=lfT[0:8, 128 * j:128 * j + 128], identity=ident[0:8, 0:8]), [lfT, ident], [p])
                k.emit("pe", lambda e, p=p, j=j: e.transpose(out=p[:, 8:16], in_=cT[0:8, 128 * j:128 * j + 128], identity=ident[0:8, 0:8]), [cT, ident], [p])
                cp("dve", lfst[:, j, :], p[:, 0:8], [p], [lfst])
                ts("dve", negck[:, gt, :], p[:, 8:16], -1.0, None, ALU.mult, None, [p], [negck])
            k.dma("sp", fox_logf_p[t0:t0 + N, :].rearrange("(j p) h -> p j h", p=128), lfst[:, :, :], glf, reads=[lfst], **NCD)
            for h in range(8):
                hp, pb = h // 2, 64 * (h % 2)
                O = nxt("O")
                nk = (c + 1) * TPC

                def emitS(kt, h=h, hp=hp, pb=pb):
                    j = kt - c * TPC
                    qlo = 0 if j < 0 else 128 * j
                    S = nxt("S")
                    mm(S[:, qlo:N], kTc[pb:pb + 64, hp, 128 * kt:128 * kt + 128], qT[pb:pb + 64, hp, qlo:N], True, False, [kTc, qT], [S])
                    if j >= 0:
                        mm(S[:, qlo:qlo + 128], identb[:, :], cmask[:, :], False, False, [identb, cmask], [S])
                    mm(S[:, qlo:N], ind[:, h, :], cq3[:, qlo:N], False, True, [ind, cq3], [S])
                    return S, qlo

                cur = emitS(0)
                for kt in range(nk):
                    nx = emitS(kt + 1) if kt + 1 < nk else None
                    S, qlo = cur
                    PT = PTb[kt % 2]
                    act(PT[:, qlo:N], S[:, qlo:N], AF.Exp, [S, negck], [PT], bias=negck[:, kt, h:h + 1])
                    mm(O[0:65, qlo:N], Vc[:, kt, h, :], PT[:, qlo:N], kt == 0, kt == nk - 1, [Vc, PT], [O])
                    cur = nx
                k.emit("dve", lambda e, O=O: e.reciprocal(out=rl[64:65, :], in_=O[64:65, 0:N]), [O], [rl])
                B = nxt("T")
                mm(B[0:64, 0:N], onesf[64:65, 0:64], rl[64:65, :], True, True, [onesf, rl], [B])
                cp("act", bcs[:, :], B[0:64, 0:N], [B], [bcs])
                tt("dve", attnT[:, h, :], O[0:64, 0:N], bcs[:, :], ALU.mult, [O, bcs], [attnT])
            wbC, wC = w_next()
            for rb in range(4):
                xc, gt_, rr_, ii_, aa_, uu_, hs_, gl_ = tmp
                cp("dve", xrext[:, rb, 0:3], xrext[:, rb, N:N + 3], [xrext], [xrext])
                p = nxt("mm")
                for kc in range(8):
                    mm(p[:, 0:N], wC[:, kc, 128 * rb:128 * rb + 128], xnT[:, kc, :], kc == 0, kc == 7, [wbC, xnT], [p])
                cp("act", xrext[:, rb, 3:3 + N], p[:, 0:N], [p], [xrext])
                pg = nxt("mm")
                for kc in range(8):
                    mm(pg[:, 0:N], wC[:, kc, 512 + 128 * rb:512 + 128 * rb + 128], xnT[:, kc, :], kc == 0, kc == 7, [wbC, xnT], [pg])
                cp("act", gt_[:, :], pg[:, 0:N], [pg], [gt_])
                ts("dve", xc[:, :], xrext[:, rb, 3:3 + N], convw[:, 3, rb:rb + 1], convb[:, rb:rb + 1], ALU.mult, ALU.add, [xrext, convw, convb], [xc])
                for j in range(3):
                    stt("dve", xc[:, :], xrext[:, rb, j:j + N], convw[:, j, rb:rb + 1], xc[:, :], ALU.mult, ALU.add, [xrext, convw, xc], [xc])
                cp("dve", xcb[:, :], xc[:, :], [xc], [xcb])
                pr = nxt("mm")
                mm(pr[:, 0:N], gabd[:, rb, :], xcb[:, :], True, True, [gabd, xcb], [pr])
                pi = nxt("mm")
                mm(pi[:, 0:N], gxbd[:, rb, :], xcb[:, :], True, True, [gxbd, xcb], [pi])
                act(rr_[:, :], pr[:, 0:N], AF.Sigmoid, [pr, gab], [rr_], bias=gab[:, rb:rb + 1])
                act(ii_[:, :], pi[:, 0:N], AF.Sigmoid, [pi, gxb], [ii_], bias=gxb[:, rb:rb + 1])
                act(aa_[:, :], rr_[:, :], AF.Exp, [rr_, clam], [aa_], scale=clam[:, rb:rb + 1])
                tt("dve", uu_[:, :], aa_[:, :], aa_[:, :], ALU.mult, [aa_], [uu_])
                ts("dve", uu_[:, :], uu_[:, :], -1.0, 1.0, ALU.mult, ALU.add, [uu_], [uu_])
                act(uu_[:, :], uu_[:, :], AF.Sqrt, [uu_], [uu_])
                tt("dve", uu_[:, :], uu_[:, :], ii_[:, :], ALU.mult, [uu_, ii_], [uu_])
                tt("dve", uu_[:, :], uu_[:, :], xc[:, :], ALU.mult, [uu_, xc], [uu_])
                k.emit("dve", lambda e, rb=rb, aa_=aa_, uu_=uu_, hs_=hs_: e.tensor_tensor_scan(out=hs_[:, :], data0=aa_[:, :], data1=uu_[:, :], initial=hcar[:, rb:rb + 1], op0=ALU.mult, op1=ALU.add), [aa_, uu_, hcar], [hs_])
                cp("dve", hcar[:, rb:rb + 1], hs_[:, N - 1:N], [hs_], [hcar])
                tt("dve", gl_[:, :], gt_[:, :], gt_[:, :], ALU.mult, [gt_], [gl_])
                ts("dve", gl_[:, :], gl_[:, :], 0.044715, 1.0, ALU.mult, ALU.add, [gl_], [gl_])
                tt("dve", gl_[:, :], gl_[:, :], gt_[:, :], ALU.mult, [gl_, gt_], [gl_])
                act(gl_[:, :], gl_[:, :], AF.Tanh, [gl_], [gl_], scale=0.7978845608028654)
                stt("dve", gl_[:, :], gl_[:, :], 1.0, gt_[:, :], ALU.add, ALU.mult, [gl_, gt_], [gl_])
                stt("dve", lruT[:, rb, :], gl_[:, :], 0.5, hs_[:, :], ALU.mult, ALU.mult, [gl_, hs_], [lruT])
            if c == NCH - 1:
                for rb in range(4):
                    k.dma("sp", lru_conv_p[:, 128 * rb:128 * rb + 128].rearrange("j p -> p j"), xrext[:, rb, N:N + 3], gsm, reads=[xrext], **NCD)
                k.dma("sp", lru_h_p[0, :].rearrange("(c p) -> p c", p=128), hcar[:, :], gsm, reads=[hcar], **NCD)
            wbD, wD = w_next()
            wbE, wE = w_next(keep_prev=True)
            for ob in range(8):
                p = nxt("mm")
                for h in range(8):
                    mm(p[:, 0:N], wD[:, h, 128 * ob:128 * ob + 128], attnT[:, h, :], h == 0, False, [wbD, attnT], [p])
                for rb in range(4):
                    mm(p[:, 0:N], wE[:, rb, 128 * ob:128 * ob + 128], lruT[:, rb, :], False, rb == 3, [wbE, lruT], [p])
                tt("dve", hT[:, ob, :], hT[:, ob, :], p[:, 0:N], ALU.add, [hT, p], [hT])
            rmsnorm(gffn, lambda dc: gffn[:, 0, dc:dc + 1])
            ffn()
            k.dma("sp", h1_scr[c, :, :], hT[:, :, :].rearrange("p a b -> p (a b)"), gh1, reads=[hT], writes=[scrb[c]])

        if do_dec:
            layer0_decode()
        for c in (range(NCH) if do_prompt else []):
            t0 = c * N
            k.dma("sp", hT[:, :, :].rearrange("p a b -> p (a b)"), h1_scr[c, :, :], gh1r, reads=[scrb[c]], writes=[hT])
            rmsnorm(gmix, lambda dc: gmix[:, 1, dc:dc + 1])
            col0 = 0
            for pi_, wcols in enumerate((1024, 1024, 72)):
                wbX, wX = w_next()
                for j in range(TPC):
                    for cb in range(0, wcols, 512):
                        wdt = min(512, wcols - cb)
                        p = nxt("mm")
                        for kc in range(8):
                            mm(p[:, 0:wdt], xnT[:, kc, 128 * j:128 * j + 128], wX[:, kc, cb:cb + wdt], kc == 0, kc == 7, [wbX, xnT], [p])
                        cp("act", tok1[:, j, col0 + cb:col0 + cb + wdt], p[:, 0:wdt], [p], [tok1])
                col0 += wcols
            for j in range(TPC):
                gt = c * TPC + j
                kin = kin_t[j]
                cp("dve", Vc[:, gt, 0:4, 0:64], tok1[:, j, 1280:1536].rearrange("p (h d) -> p h d", h=4), [tok1], [Vc])
                k.emit("dve", lambda e, j=j: e.reduce_sum(out=lst[:, 0:1], in_=tok1[:, j, 2048:2112], axis=AX.X), [tok1], [lst])
                ts("dve", lst[:, 0:1], lst[:, 0:1], 1.0 / 64.0, None, ALU.mult, None, [lst], [lst])
                ts("dve", kin[:, :], tok1[:, j, 2048:2112], lst[:, 0:1], None, ALU.subtract, None, [tok1, lst], [kin])
                tt("dve", rt1[:, 0:8, :].rearrange("p a b -> p (a b)"), kin[:, :], kin[:, :], ALU.mult, [kin], [rt1])
                k.emit("dve", lambda e: e.reduce_sum(out=lst[:, 1:2], in_=rt1[:, 0:8, :].rearrange("p a b -> p (a b)"), axis=AX.X), [rt1], [lst])
                act(lst[:, 2:3], lst[:, 1:2], AF.Sqrt, [lst], [lst], scale=1.0 / 64.0, bias=1e-6)
                k.emit("dve", lambda e: e.reciprocal(out=lst[:, 3:4], in_=lst[:, 2:3]), [lst], [lst])
                ts("dve", kin[:, :], kin[:, :], lst[:, 3:4], None, ALU.mult, None, [kin, lst], [kin])
                tt("dve", kin[:, :], kin[:, :], lngb[:, 0:64], ALU.mult, [kin, lngb], [kin])
                tt("dve", kin[:, :], kin[:, :], lngb[:, 64:128], ALU.add, [kin, lngb], [kin])
                for (vw, H, bufv) in ((tok1[:, j, 0:1280].rearrange("p (h d) -> p h d", d=64), 20, tok1),
                                      (tok1[:, j, 1536:2048].rearrange("p (h d) -> p h d", d=64), 8, tok1),
                                      (kin[:, :].rearrange("p (h d) -> p h d", d=64), 1, kin)):
                    cs = cosT[:, gt:gt + 1, :].to_broadcast([128, H, 8])
                    sn = sinT[:, gt:gt + 1, :].to_broadcast([128, H, 8])
                    x1 = vw[:, :, 0:8]
                    x2 = vw[:, :, 8:16]
                    tt("dve", rt1[:, 0:H, :], x1, cs, ALU.mult, [bufv, cosT], [rt1])
                    tt("dve", rt2[:, 0:H, :], x2, sn, ALU.mult, [bufv, sinT], [rt2])
                    tt("dve", rt3[:, 0:H, :], x2, cs, ALU.mult, [bufv, cosT], [rt3])
                    tt("dve", rt4[:, 0:H, :], x1, sn, ALU.mult, [bufv, sinT], [rt4])
                    tt("dve", x1, rt1[:, 0:H, :], rt2[:, 0:H, :], ALU.subtract, [rt1, rt2], [bufv])
                    tt("dve", x2, rt3[:, 0:H, :], rt4[:, 0:H, :], ALU.add, [rt3, rt4], [bufv])
                k.dma("sp", dsa_kv_p[t0 + 128 * j:t0 + 128 * j + 128, :], tok1[:, j, 1024:1536], gdkv, reads=[tok1])
                k.dma("sp", dsa_idxk_p[t0 + 128 * j:t0 + 128 * j + 128, :], kin[:, :], gdik, reads=[kin])
            for j in range(TPC):
                gt = c * TPC + j
                for a in range(2):
                    src = tok1[:, j, 512 * a:512 * a + 512].rearrange("p (b cc d) -> p cc b d", b=2, cc=4)
                    dst = qperm[:, 512 * a:512 * a + 512].rearrange("p (cc b d) -> p cc b d", cc=4, b=2)
                    ts("dve", dst, src, 0.125, None, ALU.mult, None, [tok1], [qperm])
                for g in range(2):
                    p = nxt("T")
                    for i in range(4):
                        blk = 4 * g + i
                        k.emit("pe", lambda e, p=p, i=i, blk=blk: e.transpose(out=p[:, 128 * i:128 * i + 128], in_=qperm[:, 128 * blk:128 * blk + 128], identity=ident[:, :]), [qperm, ident], [p])
                    cp("act", qT1[:, 4 * g:4 * g + 4, 128 * j:128 * j + 128], p[:, :].rearrange("p (i t) -> p i t", i=4), [p], [qT1])
                p = nxt("T")
                for i in range(4):
                    k.emit("pe", lambda e, p=p, i=i, j=j: e.transpose(out=p[:, 128 * i:128 * i + 128], in_=tok1[:, j, 1536 + 128 * i:1536 + 128 * i + 128], identity=ident[:, :]), [tok1, ident], [p])
                cp("act", qiT[:, :, 128 * j:128 * j + 128], p[:, :].rearrange("p (i t) -> p i t", i=4), [p], [qiT])
                cp("dve", kblk[:, 0:64], kin_t[j][:, :], [kin_t[j]], [kblk])
                cp("dve", kblk[:, 64:128], kin_t[j][:, :], [kin_t[j]], [kblk])
                p = nxt("T")
                for i in range(2):
                    k.emit("pe", lambda e, p=p, i=i, j=j: e.transpose(out=p[:, 128 * i:128 * i + 128], in_=tok1[:, j, 1024 + 128 * i:1024 + 128 * i + 128], identity=ident[:, :]), [tok1, ident], [p])
                k.emit("pe", lambda e, p=p: e.transpose(out=p[:, 256:384], in_=kblk[:, :], identity=ident[:, :]), [kblk, ident], [p])
                cp("act", kTc[:, 0:3, 128 * gt:128 * gt + 128], p[:, 0:384].rearrange("p (i t) -> p i t", i=3), [p], [kTc])
                ts("dve", wis[:, j, :], tok1[:, j, 2112:2120], 512.0 ** -0.5, None, ALU.mult, None, [tok1], [wis])
                act(absw[:, j, :], wis[:, j, :], AF.Abs, [wis], [absw])
                k.emit("act", lambda e, j=j: e.sign(out=sgn[:, j, :], in_=wis[:, j, :]), [wis], [sgn])
            for j in range(TPC):
                gt = c * TPC + j
                L = 128 * (gt + 1)
                for h in range(8):
                    half, blk = h % 2, h // 2
                    for kb0 in range(0, L, 512):
                        wdt = min(512, L - kb0)
                        p = nxt("S")
                        mm(p[:, 0:wdt], qiT[64 * half:64 * half + 64, blk, 128 * j:128 * j + 128], kTc[64 * half:64 * half + 64, 2, kb0:kb0 + wdt], True, True, [qiT, kTc], [p])
                        sc_ = sct[(h + kb0 // 512) % 2]
                        act(sc_[:, 0:wdt], p[:, 0:wdt], AF.Relu, [p, absw], [sc_], scale=absw[:, j, h:h + 1])
                        if h == 0:
                            ts("dve", score[:, kb0:kb0 + wdt], sc_[:, 0:wdt], sgn[:, j, 0:1], None, ALU.mult, None, [sc_, sgn], [score])
                        else:
                            stt("dve", score[:, kb0:kb0 + wdt], sc_[:, 0:wdt], sgn[:, j, h:h + 1], score[:, kb0:kb0 + wdt], ALU.mult, ALU.add, [sc_, sgn, score], [score])
                tt("dve", score[:, L - 128:L], score[:, L - 128:L], cmq[:, :], ALU.add, [score, cmq], [score])
                if gt >= 2:
                    srcb = score
                    for r in range(32):
                        k.emit("dve", lambda e, srcb=srcb, L=L: e.max(out=m8[:, :], in_=srcb[:, 0:L]), [srcb], [m8])
                        if r < 31:
                            k.emit("dve", lambda e, srcb=srcb, L=L: e.match_replace(out=wk[:, 0:L], in_to_replace=m8[:, :], in_values=srcb[:, 0:L], imm_value=-1e30), [srcb, m8], [wk])
                            srcb = wk
                    ts("dve", wk[:, 0:L], score[:, 0:L], m8[:, 7:8], None, ALU.is_ge, None, [score, m8], [wk])
                else:
                    ts("dve", wk[:, 0:L], score[:, 0:L], -1e29, None, ALU.is_gt, None, [score], [wk])
                ts("dve", wk[:, 0:L], wk[:, 0:L], -1.0, 30000.0, ALU.add, ALU.mult, [wk], [wk])
                for kb0 in range(0, gt + 1, 4):
                    nb = min(4, gt + 1 - kb0)
                    p = nxt("T")
                    for i in range(nb):
                        k.emit("pe", lambda e, p=p, i=i, kb0=kb0: e.transpose(out=p[:, 128 * i:128 * i + 128], in_=wk[:, 128 * (kb0 + i):128 * (kb0 + i) + 128], identity=ident[:, :]), [wk, ident], [p])
                    cp("act", maskT[:, kb0:kb0 + nb, :], p[:, 0:128 * nb].rearrange("p (i t) -> p i t", i=nb), [p], [maskT])
                for h in range(16):
                    a, b, cc = h // 8, (h // 4) % 2, h % 4
                    blkq, pbs, kvh = 4 * a + cc, 64 * b, 2 * a + b
                    O = nxt("O")

                    def emitS1(kb, a=a, pbs=pbs, blkq=blkq, j=j):
                        S = nxt("S")
                        mm(S[:, 0:128], kTc[pbs:pbs + 64, a, 128 * kb:128 * kb + 128], qT1[pbs:pbs + 64, blkq, 128 * j:128 * j + 128], True, False, [kTc, qT1], [S])
                        mm(S[:, 0:128], identb[:, :], maskT[:, kb, :], False, True, [identb, maskT], [S])
                        return S

                    cur = emitS1(0)
                    for kb in range(gt + 1):
                        nx = emitS1(kb + 1) if kb + 1 <= gt else None
                        S = cur
                        PT = PTb[kb % 2]
                        act(PT[:, 0:128], S[:, 0:128], AF.Exp, [S], [PT])
                        mm(O[0:65, 0:128], Vc[:, kb, kvh, :], PT[:, 0:128], kb == 0, kb == gt, [Vc, PT], [O])
                        cur = nx
                    k.emit("dve", lambda e, O=O: e.reciprocal(out=rl[64:65, 0:128], in_=O[64:65, 0:128]), [O], [rl])
                    B = nxt("T")
                    mm(B[0:64, 0:128], onesf[64:65, 0:64], rl[64:65, 0:128], True, True, [onesf, rl], [B])
                    cp("act", bcs[:, 0:128], B[0:64, 0:128], [B], [bcs])
                    tt("dve", attnT[:, h, 128 * j:128 * j + 128], O[0:64, 0:128], bcs[:, 0:128], ALU.mult, [O, bcs], [attnT])
            wbD, wD = w_next()
            wbE, wE = w_next(keep_prev=True)
            for ob in range(8):
                p = nxt("mm")
                for h in range(16):
                    wsl, wbuf = (wD, wbD) if h < 8 else (wE, wbE)
                    mm(p[:, 0:N], wsl[:, h % 8, 128 * ob:128 * ob + 128], attnT[:, h, :], h == 0, h == 15, [wbuf, attnT], [p])
                tt("dve", hT[:, ob, :], hT[:, ob, :], p[:, 0:N], ALU.add, [hT, p], [hT])
            rmsnorm(gffn, lambda dc: gffn[:, 1, dc:dc + 1])
            ffn()
            rmsnorm(gfin, lambda dc: gfin[:, dc:dc + 1])
            for j in range(TPC):
                for g in range(2):
                    p = nxt("T")
                    for i in range(4):
                        dc = 4 * g + i
                        yt = tmp[i]
                        stt("dve", yt[:, 0:128], hT[:, dc, 128 * j:128 * j + 128], gfin[:, dc:dc + 1], rstd[:, 128 * j:128 * j + 128], ALU.mult, ALU.mult, [hT, gfin, rstd], [yt])
                        k.emit("pe", lambda e, p=p, i=i, yt=yt: e.transpose(out=p[:, 128 * i:128 * i + 128], in_=yt[:, 0:128], identity=ident[:, :]), [yt, ident], [p])
                    cp("act", xtok[:, j, 512 * g:512 * g + 512], p[:, :], [p], [xtok])
            k.dma("sp", y_p[t0:t0 + N, :].rearrange("(j p) d -> p j d", p=128), xtok[:, :, :], gy, reads=[xtok])

        if do_dec:
            layer1_decode()
        k.finish()
        k.replay()
    return nc


_OUT_SHAPES = None


def kernel(x_prompt, x_sample, cache_fox_kv, cache_fox_logf, state_lru_conv, state_lru_h,
           cache_dsa_kv, cache_dsa_idx_k, page_table, norm_mix, norm_ffn, norm_final,
           ab_w_in, ab_b_f, ab_conv_w, ab_conv_b, ab_gate_a_w, ab_gate_a_b, ab_gate_x_w, ab_gate_x_b,
           ab_lambda, ab_w_out, c_w_in, c_idx_norm_g, c_idx_norm_b, c_w_out, ffn_w1, ffn_w2):
    f = lambda a: np.ascontiguousarray(np.asarray(a), dtype=np.float32)
    shared = {
        "norm_mix": f(norm_mix), "norm_ffn": f(norm_ffn), "norm_final": f(norm_final).reshape(1, D),
        "ab_w_in": f(ab_w_in)[0], "ab_b_f": f(ab_b_f).reshape(1, 8), "ab_conv_w": f(ab_conv_w)[0],
        "ab_conv_b": f(ab_conv_b).reshape(1, 512), "ab_ga_w": f(ab_gate_a_w)[0], "ab_ga_b": f(ab_gate_a_b).reshape(1, 512),
        "ab_gx_w": f(ab_gate_x_w)[0], "ab_gx_b": f(ab_gate_x_b).reshape(1, 512), "ab_lambda": f(ab_lambda).reshape(1, 512),
        "ab_w_out": f(ab_w_out)[0], "c_w_in": f(c_w_in)[0], "c_idx_g": f(c_idx_norm_g).reshape(1, 64),
        "c_idx_b": f(c_idx_norm_b).reshape(1, 64), "c_w_out": f(c_w_out)[0], "ffn_w1": f(ffn_w1), "ffn_w2": f(ffn_w2),
    }
    xp = f(x_prompt)
    xs = f(x_sample).reshape(128, D)
    pt = np.ascontiguousarray(np.asarray(page_table), dtype=np.int32)
    convs = f(state_lru_conv)[0]
    hs = f(state_lru_h)[0]
    n_phys = int(np.asarray(cache_fox_kv).shape[1])
    shared["c_fkv"] = f(cache_fox_kv)[0].reshape(n_phys * 128, 1024)
    shared["c_flf"] = f(cache_fox_logf)[0].reshape(n_phys * 128, 8)
    shared["c_dkv"] = f(cache_dsa_kv)[0].reshape(n_phys * 128, 512)
    shared["c_dik"] = f(cache_dsa_idx_k)[0].reshape(n_phys * 128, 64)
    nc = build_nc(n_phys)
    in_maps = []
    for i in range(8):
        m = dict(shared)
        m["x_p"] = xp[i]
        m["x_s"] = xs[16 * i:16 * i + 16]
        m["pt"] = pt[16 * i:16 * i + 16].reshape(1, 256)
        m["conv_s"] = convs[16 * i:16 * i + 16].reshape(48, 512)
        m["h_s"] = hs[16 * i:16 * i + 16]
        in_maps.append(m)
    res = run_bass_kernel_spmd(nc, in_maps, core_ids=list(range(8)))
    R = res.results
    return assemble(R)


def assemble(R):
    B, DB = 8, 128
    cat = lambda name: np.concatenate([R[i][name] for i in range(8)], axis=0)
    st = lambda name: np.stack([R[i][name] for i in range(8)])
    y_prompt = st("y_p").reshape(B, T, D)
    y_sample = cat("y_s").reshape(DB, 1, D)
    fox_kv_p = st("fox_kv_p").reshape(1, B, T, 2, 8, 64)
    fox_logf_p = st("fox_logf_p").reshape(1, B, T, 8)
    lru_conv_p = st("lru_conv_p").reshape(1, B, 3, 512)
    lru_h_p = st("lru_h_p").reshape(1, B, 512)
    dsa_kv_p = st("dsa_kv_p").reshape(1, B, T, 2, 4, 64)
    dsa_idxk_p = st("dsa_idxk_p").reshape(1, B, T, 64)
    fox_kv_s = cat("fox_kv_s").reshape(1, DB, 1, 2, 8, 64)
    fox_logf_s = cat("fox_logf_s").reshape(1, DB, 1, 8)
    lru_conv_s = cat("lru_conv_s").reshape(1, DB, 3, 512)
    lru_h_s = cat("lru_h_s").reshape(1, DB, 512)
    dsa_kv_s = cat("dsa_kv_s").reshape(1, DB, 1, 2, 4, 64)
    dsa_idxk_s = cat("dsa_idxk_s").reshape(1, DB, 1, 64)
    return (y_prompt, y_sample, fox_kv_p, fox_logf_p, lru_conv_p, lru_h_p, dsa_kv_p, dsa_idxk_p,
            fox_kv_s, fox_logf_s, lru_conv_s, lru_h_s, dsa_kv_s, dsa_idxk_s)
```

```python
import bisect
import math
import numpy as np
from contextlib import ExitStack
import concourse.bass as bass
import concourse.mybir as mybir
from concourse.bass_utils import run_bass_kernel_spmd

F32 = mybir.dt.float32
BF16 = mybir.dt.bfloat16
I32 = mybir.dt.int32
ALU = mybir.AluOpType
AF = mybir.ActivationFunctionType
AX = mybir.AxisListType

T = 2048
D = 1024
N = 256
NCH = T // N
TPC = N // 128
NT = T // 128
STAGE = 5


class Buf:
    def __init__(self, name, t):
        self.name = name
        self.t = t
        self.lw = None
        self.rd = {}

    def __getitem__(self, idx):
        return self.t[idx]


class BufView(Buf):
    def __init__(self, base, ap):
        self.base = base
        self.name = base.name + "_v"
        self.t = ap

    lw = property(lambda s: s.base.lw, lambda s, v: setattr(s.base, "lw", v))
    rd = property(lambda s: s.base.rd, lambda s, v: setattr(s.base, "rd", v))


class Rec:
    __slots__ = ("fn", "waits", "inc", "seq", "grp")

    def __init__(self, fn, waits, grp=None):
        self.fn = fn
        self.waits = waits
        self.inc = False
        self.seq = 0
        self.grp = grp


class Grp:
    def __init__(self, sem):
        self.sem = sem
        self.count = 0


class K:
    def __init__(self, nc, stack):
        self.nc = nc
        self.stack = stack
        self.engs = ["pe", "act", "dve", "pool", "sp"]
        self.recs = {e: [] for e in self.engs}
        self.esem = {e: stack.enter_context(nc.semaphore("es_" + e)) for e in self.engs}
        self.ecnt = {e: 0 for e in self.engs}
        self.incidx = {e: [] for e in self.engs}
        self.incseq = {e: [] for e in self.engs}
        self.known = {e: {} for e in self.engs}
        self.grps = []

    def sb(self, name, shape, dt):
        return Buf(name, self.stack.enter_context(self.nc.sbuf_tensor(name, list(shape), dt)))

    def ps(self, name, shape, dt=F32):
        return Buf(name, self.stack.enter_context(self.nc.psum_tensor(name, list(shape), dt)))

    def grp(self, name):
        g = Grp(self.stack.enter_context(self.nc.semaphore("g_" + name)))
        self.grps.append(g)
        return g

    def _resolve(self, d):
        if d[0] == "e":
            f, idx = d[1], d[2]
            pos = bisect.bisect_left(self.incidx[f], idx)
            if pos < len(self.incidx[f]):
                return self.esem[f], ("e", f), self.incseq[f][pos]
            rec = self.recs[f][-1]
            rec.inc = True
            self.ecnt[f] += 1
            rec.seq = self.ecnt[f]
            self.incidx[f].append(len(self.recs[f]) - 1)
            self.incseq[f].append(rec.seq)
            return self.esem[f], ("e", f), rec.seq
        g, cnt = d[1], d[2]
        return g.sem, ("d", id(g)), cnt

    def _deps(self, e, reads, writes):
        deps = []
        for b in reads:
            if b.lw is not None:
                deps.append(b.lw)
        for b in writes:
            if b.lw is not None:
                deps.append(b.lw)
            deps.extend(b.rd.values())
        waits = {}
        for d in deps:
            if d[0] == "e" and d[1] == e and e == "pe":
                continue
            sem, key, val = self._resolve(d)
            if self.known[e].get(key, 0) >= val:
                continue
            if key not in waits or waits[key][1] < val:
                waits[key] = (sem, val)
        for key, (sem, val) in waits.items():
            self.known[e][key] = val
        return list(waits.values())

    def emit(self, e, fn, reads=(), writes=()):
        waits = self._deps(e, reads, writes)
        self.recs[e].append(Rec(fn, waits))
        idx = len(self.recs[e]) - 1
        for b in reads:
            b.rd[e] = ("e", e, idx)
        for b in writes:
            b.lw = ("e", e, idx)
            b.rd = {}

    def dmaf(self, e, fn, grp, reads=(), writes=()):
        waits = self._deps(e, reads, writes)
        self.recs[e].append(Rec(fn, waits, grp))
        grp.count += 16
        for b in reads:
            b.rd[("d", id(grp))] = ("d", grp, grp.count)
        for b in writes:
            b.lw = ("d", grp, grp.count)
            b.rd = {}

    def dma(self, e, out, in_, grp, reads=(), writes=(), **kw):
        self.dmaf(e, lambda eng: eng.dma_start(out=out, in_=in_, **kw), grp, reads, writes)

    def finish(self):
        waits = [(g.sem, g.count) for g in self.grps if g.count > 0]
        self.recs["sp"].append(Rec(None, waits))

    def replay(self):
        me = self

        def run(e, eng):
            for rec in me.recs[e]:
                for sem, val in rec.waits:
                    eng.wait_ge(sem, val)
                if rec.fn is None:
                    continue
                ins = rec.fn(eng)
                if rec.grp is not None:
                    ins.then_inc(rec.grp.sem, 16)
                if rec.inc:
                    ins.then_inc(me.esem[e], 1)

        with self.nc.Block() as block:
            @block.tensor
            def _(eng):
                run("pe", eng)

            @block.scalar
            def _(eng):
                run("act", eng)

            @block.vector
            def _(eng):
                run("dve", eng)

            @block.gpsimd
            def _(eng):
                run("pool", eng)

            @block.sync
            def _(eng):
                run("sp", eng)


def build_nc(n_phys=2560, do_prompt=True, do_dec=True):
    nc = bass.Bass("TRN2", target_bir_lowering=False)

    def din(name, shape, dt=F32):
        return nc.dram_tensor(name, list(shape), dt, kind="ExternalInput").ap()

    def dout(name, shape):
        return nc.dram_tensor(name, list(shape), F32, kind="ExternalOutput").ap()

    x_p = din("x_p", [T, D])
    norm_mix = din("norm_mix", [2, D])
    norm_ffn = din("norm_ffn", [2, D])
    norm_final = din("norm_final", [1, D])
    ab_w_in = din("ab_w_in", [D, 2568])
    ab_b_f = din("ab_b_f", [1, 8])
    ab_conv_w = din("ab_conv_w", [4, 512])
    ab_conv_b = din("ab_conv_b", [1, 512])
    ab_ga_w = din("ab_ga_w", [8, 64, 64])
    ab_ga_b = din("ab_ga_b", [1, 512])
    ab_gx_w = din("ab_gx_w", [8, 64, 64])
    ab_gx_b = din("ab_gx_b", [1, 512])
    ab_lambda = din("ab_lambda", [1, 512])
    ab_w_out = din("ab_w_out", [D, D])
    c_w_in = din("c_w_in", [D, 2120])
    c_idx_g = din("c_idx_g", [1, 64])
    c_idx_b = din("c_idx_b", [1, 64])
    c_w_out = din("c_w_out", [D, D])
    ffn_w1 = din("ffn_w1", [2, D, 4096])
    ffn_w2 = din("ffn_w2", [2, 4096, D])

    x_s = din("x_s", [16, D])
    pt_in = din("pt", [1, 256], I32)
    conv_s = din("conv_s", [48, 512])
    h_s = din("h_s", [16, 512])
    c_fkv = din("c_fkv", [n_phys * 128, 1024])
    c_flf = din("c_flf", [n_phys * 128, 8])
    c_dkv = din("c_dkv", [n_phys * 128, 512])
    c_dik = din("c_dik", [n_phys * 128, 64])
    y_s = dout("y_s", [16, D])
    fox_kv_s = dout("fox_kv_s", [16, 1024])
    fox_logf_s = dout("fox_logf_s", [16, 8])
    lru_conv_s = dout("lru_conv_s", [48, 512])
    lru_h_s = dout("lru_h_s", [16, 512])
    dsa_kv_s = dout("dsa_kv_s", [16, 512])
    dsa_idxk_s = dout("dsa_idxk_s", [16, 64])
    h1s_scr = nc.dram_tensor("h1s_scr", [128, 8 * 16], F32, kind="Internal").ap()
    y_p = dout("y_p", [T, D])
    fox_kv_p = dout("fox_kv_p", [T, 1024])
    fox_logf_p = dout("fox_logf_p", [T, 8])
    lru_conv_p = dout("lru_conv_p", [3, 512])
    lru_h_p = dout("lru_h_p", [1, 512])
    dsa_kv_p = dout("dsa_kv_p", [T, 512])
    dsa_idxk_p = dout("dsa_idxk_p", [T, 64])
    def dscr(name, shape):
        return nc.dram_tensor(name, list(shape), BF16, kind="Internal").ap()
    sb_win0 = dscr("sb_win0", [D, 2568])
    sb_wout0 = dscr("sb_wout0", [D, D])
    sb_win1 = dscr("sb_win1", [D, 2120])
    sb_wout1 = dscr("sb_wout1", [D, D])
    sb_w1 = dscr("sb_w1", [2, D, 4096])
    sb_w2 = dscr("sb_w2", [2, 4096, D])
    h1_scr = nc.dram_tensor("h1_scr", [NCH, 128, 8 * N], F32, kind="Internal").ap()

    with ExitStack() as st:
        k = K(nc, st)

        def mm(out, lhsT, rhs, start, stop, R, W):
            k.emit("pe", lambda e: e.matmul(out=out, lhsT=lhsT, rhs=rhs, start=start, stop=stop), R, W)

        def act(out, in_, func, R, W, bias=None, scale=None, accum=None, eng="act"):
            kw = {}
            if bias is not None:
                kw["bias"] = bias
            if scale is not None:
                kw["scale"] = scale
            if accum is not None:
                kw["accum_out"] = accum
            k.emit(eng, lambda e: e.activation(out=out, in_=in_, func=func, **kw), R, W)

        def tt(eng, out, in0, in1, op, R, W):
            k.emit(eng, lambda e: e.tensor_tensor(out=out, in0=in0, in1=in1, op=op), R, W)

        def ts(eng, out, in0, s1, s2, op0, op1, R, W):
            if s2 is None:
                k.emit(eng, lambda e: e.tensor_scalar(out=out, in0=in0, scalar1=s1, scalar2=None, op0=op0), R, W)
            else:
                k.emit(eng, lambda e: e.tensor_scalar(out=out, in0=in0, scalar1=s1, scalar2=s2, op0=op0, op1=op1), R, W)

        def stt(eng, out, in0, scalar, in1, op0, op1, R, W):
            k.emit(eng, lambda e: e.scalar_tensor_tensor(out=out, in0=in0, scalar=scalar, in1=in1, op0=op0, op1=op1), R, W)

        def cp(eng, out, in_, R, W):
            if eng == "act":
                k.emit(eng, lambda e: e.activation(out=out, in_=in_, func=AF.Copy), R, W)
            else:
                k.emit(eng, lambda e: e.tensor_copy(out=out, in_=in_), R, W)

        def memset(eng, ap, val, W):
            k.emit(eng, lambda e: e.memset(ap, val), [], W)

        def asel(out, in_, op, fill, base, pattern, cm, R, W):
            k.emit("pool", lambda e: e.affine_select(out=out, in_=in_, compare_op=op, fill=fill, base=base,
                                                     pattern=pattern, channel_multiplier=cm), R, W)

        pmm = [k.ps("pmm%d" % i, [128, 512]) for i in range(2)]
        pS = [k.ps("pS%d" % i, [128, 512]) for i in range(2)]
        pO = [k.ps("pO%d" % i, [128, 512]) for i in range(2)]
        pT = [k.ps("pT%d" % i, [128, 512]) for i in range(2)]
        rr = {"mm": 0, "S": 0, "O": 0, "T": 0}

        def nxt(kind):
            lst = {"mm": pmm, "S": pS, "O": pO, "T": pT}[kind]
            rr[kind] = (rr[kind] + 1) % len(lst)
            return lst[rr[kind]]

        gc = k.grp("const")
        ident = k.sb("ident", [128, 128], F32)
        identb = k.sb("identb", [128, 128], BF16)
        memset("pool", ident[:, :], 0.0, [ident])
        asel(ident[:, :], ident[:, :], ALU.not_equal, 1.0, 0, [[-1, 128]], 1, [ident], [ident])
        cp("dve", identb[:, :], ident[:, :], [ident], [identb])
        cmaskf = k.sb("cmaskf", [128, 128], F32)
        cmask = k.sb("cmask", [128, 128], BF16)
        memset("pool", cmaskf[:, :], 0.0, [cmaskf])
        asel(cmaskf[:, :], cmaskf[:, :], ALU.is_ge, -30000.0, 0, [[1, 128]], -1, [cmaskf], [cmaskf])
        cp("dve", cmask[:, :], cmaskf[:, :], [cmaskf], [cmask])
        cmq = k.sb("cmq", [128, 128], F32)
        memset("pool", cmq[:, :], 0.0, [cmq])
        asel(cmq[:, :], cmq[:, :], ALU.is_ge, -1e30, 0, [[-1, 128]], 1, [cmq], [cmq])
        onesb = k.sb("onesb", [128, 128], BF16)
        memset("dve", onesb[:, :], 1.0, [onesb])
        onesf = k.sb("onesf", [128, 256], F32)
        memset("dve", onesf[:, :], 1.0, [onesf])
        gmix = k.sb("gmix", [128, 2, 8], F32)
        gffn = k.sb("gffn", [128, 2, 8], F32)
        gfin = k.sb("gfin", [128, 8], F32)
        convw = k.sb("convw", [128, 4, 4], F32)
        convb = k.sb("convb", [128, 4], F32)
        gab = k.sb("gab", [128, 4], F32)
        gxb = k.sb("gxb", [128, 4], F32)
        lam = k.sb("lam", [128, 4], F32)
        clam = k.sb("clam", [128, 4], F32)
        nbf3 = k.sb("nbf3", [24, 1], F32)
        NCD = dict(allow_slow_non_contiguous=True)
        if True:
            for l in range(2):
                k.dma("sp", gmix[:, l, :], norm_mix[l, :].rearrange("(c p) -> p c", p=128), gc, writes=[gmix], **NCD)
                k.dma("sp", gffn[:, l, :], norm_ffn[l, :].rearrange("(c p) -> p c", p=128), gc, writes=[gffn], **NCD)
            k.dma("sp", gfin[:, :], norm_final[0, :].rearrange("(c p) -> p c", p=128), gc, writes=[gfin], **NCD)
            for j in range(4):
                k.dma("sp", convw[:, j, :], ab_conv_w[j, :].rearrange("(c p) -> p c", p=128), gc, writes=[convw], **NCD)
            k.dma("sp", convb[:, :], ab_conv_b[0, :].rearrange("(c p) -> p c", p=128), gc, writes=[convb], **NCD)
            k.dma("sp", gab[:, :], ab_ga_b[0, :].rearrange("(c p) -> p c", p=128), gc, writes=[gab], **NCD)
            k.dma("sp", gxb[:, :], ab_gx_b[0, :].rearrange("(c p) -> p c", p=128), gc, writes=[gxb], **NCD)
            k.dma("sp", lam[:, :], ab_lambda[0, :].rearrange("(c p) -> p c", p=128), gc, writes=[lam], **NCD)
            for j in range(3):
                k.dma("sp", nbf3[8 * j:8 * j + 8, :], ab_b_f[0, :].rearrange("(p o) -> p o", o=1), gc, writes=[nbf3], **NCD)
        sct = [k.sb("sct%d" % i, [128, 512], F32) for i in range(2)]
        gaf = BufView(sct[0], sct[0][:, :].rearrange("p (a b) -> p a b", a=4))
        gxf = BufView(sct[1], sct[1][:, :].rearrange("p (a b) -> p a b", a=4))
        gabd = k.sb("gabd", [128, 4, 128], BF16)
        gxbd = k.sb("gxbd", [128, 4, 128], BF16)
        memset("dve", gaf[:, :, :], 0.0, [gaf])
        memset("dve", gxf[:, :, :], 0.0, [gxf])
        for rb in range(4):
            for i in range(2):
                k.dma("sp", gaf[64 * i:64 * i + 64, rb, 64 * i:64 * i + 64], ab_ga_w[2 * rb + i, :, :], gc, writes=[gaf])
                k.dma("sp", gxf[64 * i:64 * i + 64, rb, 64 * i:64 * i + 64], ab_gx_w[2 * rb + i, :, :], gc, writes=[gxf])
        for b_ in (gmix, gffn, gfin, convw, convb, gab, gxb, lam, nbf3, gaf, gxf):
            if b_.lw is not None and b_.lw[0] == "d":
                b_.lw = ("d", gc, gc.count)
        ts("dve", nbf3[:, :], nbf3[:, :], -1.0, None, ALU.mult, None, [nbf3], [nbf3])
        act(clam[:, :], lam[:, :], AF.Exp, [lam], [clam], scale=-1.0)
        act(clam[:, :], clam[:, :], AF.Ln, [clam], [clam], bias=1.0)
        ts("dve", clam[:, :], clam[:, :], -8.0, None, ALU.mult, None, [clam], [clam])
        cp("dve", gabd[:, :, :], gaf[:, :, :], [gaf], [gabd])
        cp("dve", gxbd[:, :, :], gxf[:, :, :], [gxf], [gxbd])
        m3 = k.sb("m3", [24, 3], F32)
        memset("pool", m3[:, :], 1.0, [m3])
        asel(m3[:, :], m3[:, :], ALU.is_ge, 0.0, 0, [[-8, 3]], 1, [m3], [m3])
        asel(m3[:, :], m3[:, :], ALU.is_ge, 0.0, 7, [[8, 3]], -1, [m3], [m3])
        xtok = k.sb("xtok", [128, TPC, D], F32)
        indf = BufView(xtok, xtok[0:24, 0, :].rearrange("p (h k) -> p h k", h=8))
        ind = k.sb("ind", [24, 8, 128], BF16)
        memset("pool", indf[:, :, :], 0.0, [indf])
        for j in range(3):
            asel(indf[:, :, :], indf[:, :, :], ALU.not_equal, 1.0, -8 * j, [[-1, 8], [0, 128]], 1, [indf], [indf])
        cp("dve", ind[:, :, :], indf[:, :, :], [indf], [ind])


        NTX = NT + 1
        posi = k.sb("posi", [128, NTX], I32)
        posf = k.sb("posf", [128, NTX], F32)
        k.emit("pool", lambda e: e.iota(posi[:, 0:NT], pattern=[[128, NT]], base=0, channel_multiplier=1), [], [posi])
        k.emit("pool", lambda e: e.iota(posi[:, NT:NTX], pattern=[[0, 1]], base=2048, channel_multiplier=0), [posi], [posi])
        cp("dve", posf[:, :], posi[:, :], [posi], [posf])
        ang = k.sb("ang", [128, NTX, 8], F32)
        angi = k.sb("angi", [128, NTX, 8], I32)
        angn = k.sb("angn", [128, NTX, 8], F32)
        cor = k.sb("cor", [128, NTX, 8], F32)
        cosT = k.sb("cosT", [128, NTX, 8], F32)
        sinT = k.sb("sinT", [128, NTX, 8], F32)
        TWO_PI = 2.0 * math.pi
        for phase, dst in ((0.0, sinT), (math.pi / 2.0, cosT)):
            for i in range(8):
                inv = 500000.0 ** (-i / 8.0)
                ts("dve", ang[:, :, i], posf[:, :], inv, phase, ALU.mult, ALU.add, [posf], [ang])
            ts("dve", angn[:, :, :], ang[:, :, :], 1.0 / TWO_PI, None, ALU.mult, None, [ang], [angn])
            cp("dve", angi[:, :, :], angn[:, :, :], [angn], [angi])
            cp("dve", angn[:, :, :], angi[:, :, :], [angi], [angn])
            stt("dve", ang[:, :, :], angn[:, :, :], -TWO_PI, ang[:, :, :], ALU.mult, ALU.add, [angn, ang], [ang])
            ts("dve", cor[:, :, :], ang[:, :, :], math.pi, -TWO_PI, ALU.is_gt, ALU.mult, [ang], [cor])
            tt("dve", ang[:, :, :], ang[:, :, :], cor[:, :, :], ALU.add, [ang, cor], [ang])
            ts("dve", cor[:, :, :], ang[:, :, :], -math.pi, TWO_PI, ALU.is_lt, ALU.mult, [ang], [cor])
            tt("dve", ang[:, :, :], ang[:, :, :], cor[:, :, :], ALU.add, [ang, cor], [ang])
            act(dst[:, :, :], ang[:, :, :], AF.Sin, [ang], [dst])
        lnrow = k.sb("lnrow", [1, 128], F32)
        gln = k.grp("ln")
        k.dma("sp", lnrow[0:1, 0:64], c_idx_g[0:1, :], gln, writes=[lnrow])
        k.dma("sp", lnrow[0:1, 64:128], c_idx_b[0:1, :], gln, writes=[lnrow])
        lnrow.lw = ("d", gln, gln.count)
        lngb = k.sb("lngb", [128, 128], F32)
        pl_ = nxt("T")
        mm(pl_[:, 0:128], onesf[0:1, 0:128], lnrow[0:1, :], True, True, [onesf, lnrow], [pl_])
        cp("dve", lngb[:, :], pl_[:, 0:128], [pl_], [lngb])

        NSLOT = 3
        wslots = [k.sb("wslot%d" % i, [128, 8192], BF16) for i in range(NSLOT)]
        wgrps = [k.grp("w%d" % i) for i in range(NSLOT)]
        wplan = []

        pieces = {}

        def piece(key, mk, npart, a, b, src, dst):
            if key not in pieces:
                cb = Buf("conv_" + key, None)
                g = k.grp("cv_" + key)
                k.dma("pool", mk(dst), mk(src), g, writes=[cb])
                pieces[key] = (mk(dst), npart, a, b, cb)
            wplan.append(pieces[key])

        KC = lambda ap: ap.rearrange("(kc p) n -> p kc n", p=128)

        def plan_layer0():
            piece("a0", lambda w: KC(w)[:, :, 0:1024], 128, 8, 1024, ab_w_in, sb_win0)
            piece("b0", lambda w: KC(w)[:, :, 1024:1544], 128, 8, 520, ab_w_in, sb_win0)
            piece("c0", lambda w: KC(w)[:, :, 1544:2568], 128, 8, 1024, ab_w_in, sb_win0)
            piece("d0", lambda w: w[0:512, :].rearrange("(h p) n -> p h n", p=64), 64, 8, 1024, ab_w_out, sb_wout0)
            piece("e0", lambda w: w[512:1024, :].rearrange("(h p) n -> p h n", p=128), 128, 4, 1024, ab_w_out, sb_wout0)
            plan_ffn(0)

        def plan_ffn(l):
            for s_ in range(4):
                piece("f%d_%d" % (l, s_), lambda w, s_=s_: KC(w[l])[:, :, 1024 * s_:1024 * s_ + 1024], 128, 8, 1024, ffn_w1, sb_w1)
                piece("g%d_%d" % (l, s_), lambda w, s_=s_: w[l].rearrange("(fc p) n -> p fc n", p=128)[:, 8 * s_:8 * s_ + 8, :], 128, 8, 1024, ffn_w2, sb_w2)

        def plan_layer1():
            piece("a1", lambda w: KC(w)[:, :, 0:1024], 128, 8, 1024, c_w_in, sb_win1)
            piece("b1", lambda w: KC(w)[:, :, 1024:2048], 128, 8, 1024, c_w_in, sb_win1)
            piece("c1", lambda w: KC(w)[:, :, 2048:2120], 128, 8, 72, c_w_in, sb_win1)
            piece("d1", lambda w: w[0:512, :].rearrange("(h p) n -> p h n", p=64), 64, 8, 1024, c_w_out, sb_wout1)
            piece("e1", lambda w: w[512:1024, :].rearrange("(h p) n -> p h n", p=64), 64, 8, 1024, c_w_out, sb_wout1)
            plan_ffn(1)

        plan_layer0()
        plan_layer1()
        wplan.clear()
        if do_prompt:
            for c in range(NCH):
                plan_layer0()
        if do_dec:
            plan_layer0()
        if do_prompt:
            for c in range(NCH):
                plan_layer1()
        if do_dec:
            plan_layer1()
        wstate = {"issued": 0, "used": 0}

        def w_issue():
            i = wstate["issued"]
            if i >= len(wplan):
                return
            view, npart, a, b, cb = wplan[i]
            slot = wslots[i % NSLOT]
            dst = slot[0:npart, 0:a * b].rearrange("p (a b) -> p a b", a=a)
            k.dma("sp", dst, view, wgrps[i % NSLOT], reads=[cb], writes=[slot])
            wstate["issued"] += 1

        def w_next(keep_prev=False):
            i = wstate["used"]
            oldest = i - 1 if keep_prev else i
            while wstate["issued"] < min(len(wplan), oldest + NSLOT) or wstate["issued"] <= i:
                w_issue()
            view, npart, a, b, cb = wplan[i]
            slot = wslots[i % NSLOT]
            wstate["used"] += 1
            return slot, slot[0:npart, 0:a * b].rearrange("p (a b) -> p a b", a=a)

        def w_prefetch():
            i = wstate["used"]
            while wstate["issued"] < min(len(wplan), i + NSLOT - 1):
                w_issue()

        kTc = k.sb("kTc", [128, 4, T], BF16)
        Vc = k.sb("Vc", [128, NT, 8, 65], BF16)
        negck = k.sb("negck", [128, NT, 8], F32)
        memset("dve", Vc[:, :, :, 64:65], 1.0, [Vc])
        hcar = k.sb("hcar", [128, 4], F32)
        memset("dve", hcar[:, :], 0.0, [hcar])
        ccar = k.sb("ccar", [24, 1], F32)
        memset("dve", ccar[:, :], 0.0, [ccar])
        xrext = k.sb("xrext", [128, 4, N + 3], F32)
        memset("dve", xrext[:, :, :], 0.0, [xrext])

        hT = k.sb("hT", [128, 8, N], F32)
        xnT = k.sb("xnT", [128, 8, N], BF16)
        rstd = k.sb("rstd", [128, N], F32)
        qT = k.sb("qT", [128, 4, N], BF16)
        wf3 = k.sb("wf3", [128, 8, 24], BF16)
        kvst = k.sb("kvst", [128, 1024], F32)
        lfT = k.sb("lfT", [24, N], F32)
        cT = k.sb("cT", [24, N], F32)
        r1 = k.sb("r1", [24, N], F32)
        hib = k.sb("hib", [24, N], BF16)
        midb = k.sb("midb", [24, N], BF16)
        lob = k.sb("lob", [24, N], BF16)
        cq3f = k.sb("cq3f", [24, N], F32)
        cq3 = k.sb("cq3", [24, N], BF16)
        lfst = k.sb("lfst", [128, TPC, 8], F32)
        PTb = [k.sb("PT%d" % i, [128, N], BF16) for i in range(2)]
        rl = k.sb("rl", [128, N], F32)
        bcs = k.sb("bcs", [64, N], F32)
        attnT = k.sb("attnT", [64, 16, N], BF16)
        lruT = k.sb("lruT", [128, 4, N], BF16)
        tmp = [k.sb("tmp%d" % i, [128, N], F32) for i in range(8)]
        xcb = k.sb("xcb", [128, N], BF16)
        hid = k.sb("hid", [128, 8, N], BF16)
        sq = hid
        rtmp = [k.sb("rtmp%d" % i, [128, N], F32) for i in range(2)]
        tok1 = k.sb("tok1", [128, TPC, 2120], F32)
        kin_t = [k.sb("kin%d" % i, [128, 64], F32) for i in range(TPC)]
        lst = k.sb("lst", [128, 8], F32)
        rt1 = k.sb("rt1", [128, 20, 8], F32)
        rt2 = k.sb("rt2", [128, 20, 8], F32)
        rt3 = k.sb("rt3", [128, 20, 8], F32)
        rt4 = k.sb("rt4", [128, 20, 8], F32)
        qperm = kvst
        qT1 = k.sb("qT1", [128, 8, N], BF16)
        qiT = k.sb("qiT", [128, 4, N], BF16)
        kblk = k.sb("kblk", [128, 128], F32)
        score = BufView(tok1, tok1[:, :, :].rearrange("p a b -> p (a b)")[:, 0:T])
        wk = BufView(xtok, xtok[:, :, :].rearrange("p a b -> p (a b)"))
        m8 = k.sb("m8", [128, 8], F32)
        maskT = k.sb("maskT", [128, NT, 128], BF16)
        wis = k.sb("wis", [128, TPC, 8], F32)
        absw = k.sb("absw", [128, TPC, 8], F32)
        sgn = k.sb("sgn", [128, TPC, 8], F32)
        gdkv = k.grp("dkv")
        gdik = k.grp("dik")
        gio = k.grp("xin")
        gkv = k.grp("kvout")
        glf = k.grp("lfout")
        gh1 = k.grp("h1w")
        gh1r = k.grp("h1r")
        scrb = [Buf("scr%d" % i, None) for i in range(NCH)]
        gy = k.grp("yout")
        gsm = k.grp("small")

        def rmsnorm(gbuf, gsel, n=N):
            for dc in range(8):
                act(sq[:, dc, 0:n], hT[:, dc, 0:n], AF.Square, [hT], [sq])
            p = nxt("mm")
            for dc in range(8):
                mm(p[:, 0:n], onesb[:, :], sq[:, dc, 0:n], dc == 0, dc == 7, [onesb, sq], [p])
            act(rstd[:, 0:n], p[:, 0:n], AF.Sqrt, [p], [rstd], scale=1.0 / D, bias=1e-6)
            k.emit("dve", lambda e: e.reciprocal(out=rstd[:, 0:n], in_=rstd[:, 0:n]), [rstd], [rstd])
            for dc in range(8):
                stt("dve", xnT[:, dc, 0:n], hT[:, dc, 0:n], gsel(dc), rstd[:, 0:n], ALU.mult, ALU.mult, [hT, gbuf, rstd], [xnT])

        def ffn(n=N):
            for s in range(4):
                wb1, w1 = w_next()
                for fj in range(8):
                    p = nxt("mm")
                    for kc in range(8):
                        mm(p[:, 0:n], w1[:, kc, 128 * fj:128 * fj + 128], xnT[:, kc, 0:n], kc == 0, kc == 7, [wb1, xnT], [p])
                    rt = rtmp[fj % 2]
                    act(rt[:, 0:n], p[:, 0:n], AF.Relu, [p], [rt])
                    tt("dve", hid[:, fj, 0:n], rt[:, 0:n], rt[:, 0:n], ALU.mult, [rt], [hid])
                wb2, w2 = w_next()
                for ob in range(8):
                    p = nxt("mm")
                    for fj in range(8):
                        mm(p[:, 0:n], w2[:, fj, 128 * ob:128 * ob + 128], hid[:, fj, 0:n], fj == 0, fj == 7, [wb2, hid], [p])
                    tt("dve", hT[:, ob, 0:n], hT[:, ob, 0:n], p[:, 0:n], ALU.add, [hT, p], [hT])

        ND = 16
        tri = k.sb("tri", [128, 128], F32)
        memset("pool", tri[:, :], 1.0, [tri])
        asel(tri[:, :], tri[:, :], ALU.is_ge, 0.0, -1, [[-1, 128]], 1, [tri], [tri])
        Mmat = k.sb("Mmat", [128, 128], F32)
        memset("pool", Mmat[:, :], 0.0, [Mmat])
        for m_ in range(1, 16):
            asel(Mmat[:, :], Mmat[:, :], ALU.not_equal, 1.0, -8 * m_, [[-1, 128]], 1, [Mmat], [Mmat])
        pt_sb = k.sb("pt_sb", [1, 256], I32)
        gpt = k.grp("pt")
        k.dma("sp", pt_sb[:, :], pt_in[:, :], gpt, writes=[pt_sb])
        kTn = k.sb("kTn", [128, 4, ND], BF16)
        Vn = k.sb("Vn", [ND, 8, 65], BF16)
        memset("dve", Vn[:, :, 64:65], 1.0, [Vn])
        lfpg = k.sb("lfpg", [128, 16, 8], F32)
        Tsb = k.sb("Tsb", [128, 128], F32)
        sml = k.sb("sml", [ND, 64], F32)
        pnm = k.sb("pnm", [ND, ND, 16], BF16)
        cvT = k.sb("cvT", [128, 4, ND, 3], F32)
        cvn = k.sb("cvn", [128, 4, ND, 3], F32)
        h0T = k.sb("h0T", [128, 4, ND], F32)
        hsd = k.sb("hsd", [128, 4, ND], F32)
        hsave = k.sb("hsave", [128, 8, ND], F32)
        sgnm = k.sb("sgnm", [ND, ND, 8], F32)
        maskTd = k.sb("maskTd", [128, 16, ND], F32)
        stage = BufView(tok1, tok1[:, :, :].rearrange("p a b -> p (a b)")[:, 0:4096].rearrange("p (i f) -> p i f", i=4))
        negflat = BufView(negck, negck[:, :, :].rearrange("p a b -> p (a b)"))
        gst = k.grp("stage")
        glp = k.grp("lfpg")
        gds = k.grp("decsmall")
        gdo = k.grp("decout")
        scrs = Buf("scrs", None)
        regs = {}

        ptf = k.sb("ptf", [1, 256], F32)
        cp("dve", ptf[:, :], pt_sb[:, :], [pt_sb], [ptf])
        pidx = k.sb("pidx", [128, 1], I32)
        pidf = k.sb("pidf", [128, 1], F32)
        k.emit("pool", lambda e: e.iota(pidx[:, :], pattern=[[0, 1]], base=0, channel_multiplier=1), [], [pidx])
        cp("dve", pidf[:, :], pidx[:, :], [pidx], [pidf])
        pix = nxt("T")
        mm(pix[:, 0:256], onesf[0:1, 0:128], ptf[0:1, :], True, True, [onesf, ptf], [pix])
        idxf = rtmp[0]
        ts("dve", idxf[:, 0:256], pix[:, 0:256], 128.0, pidf[:, 0:1], ALU.mult, ALU.add, [pix, pidf], [idxf])
        idx_all = k.sb("idx_all", [128, 256], I32)
        cp("dve", idx_all[:, :], idxf[:, 0:256], [idxf], [idx_all])

        def dyn_dma(out_ap, cache, col, grp, writes):
            k.dmaf("pool", lambda e: e.indirect_dma_start(out=out_ap, out_offset=None, in_=cache[:, :],
                                                           in_offset=bass.IndirectOffsetOnAxis(ap=idx_all[:, col:col + 1], axis=0)),
                   grp, reads=[idx_all], writes=writes)

        def load_x_dec():
            k.dma("sp", xtok[0:ND, 0, :], x_s[:, :], gio, writes=[xtok])
            p = nxt("T")
            for dc in range(8):
                k.emit("pe", lambda e, p=p, dc=dc: e.transpose(out=p[:, ND * dc:ND * dc + ND], in_=xtok[0:ND, 0, 128 * dc:128 * dc + 128], identity=ident[0:ND, 0:ND]), [xtok, ident], [p])
            cp("act", hT[:, :, 0:ND], p[:, 0:8 * ND].rearrange("p (a t) -> p a t", a=8), [p], [hT])

        def fox_decode_seq(b):
            for g in range(4):
                for i in range(4):
                    dyn_dma(stage[:, i, :], c_fkv, 16 * b + 4 * g + i, gst, [stage] if i in (0, 3) else [])
                for i in range(4):
                    pg = 4 * g + i
                    p = nxt("T")
                    for hp in range(4):
                        k.emit("pe", lambda e, p=p, i=i, hp=hp: e.transpose(out=p[:, 128 * hp:128 * hp + 128], in_=stage[:, i, 128 * hp:128 * hp + 128], identity=ident[:, :]), [stage, ident], [p])
                    cp("act", kTc[:, :, 128 * pg:128 * pg + 128], p[:, :].rearrange("p (i t) -> p i t", i=4), [p], [kTc])
                    cp("dve", Vc[:, pg, :, 0:64], stage[:, i, 512:1024].rearrange("p (h d) -> p h d", h=8), [stage], [Vc])
            for pg in range(16):
                dyn_dma(lfpg[:, pg, :], c_flf, 16 * b + pg, glp, [lfpg] if pg in (0, 15) else [])
            lff = lfpg[:, :, :].rearrange("p a b -> p (a b)")
            pb_ = nxt("mm")
            pt_ = nxt("T")
            mm(pt_[:, 0:128], lff, onesf[:, 0:128], True, True, [lfpg, onesf], [pt_])
            cp("act", Tsb[:, :], pt_[:, 0:128], [pt_], [Tsb])
            mm(pb_[:, 0:128], tri[:, :], lff, True, False, [tri, lfpg], [pb_])
            mm(pb_[:, 0:128], Tsb[:, :], Mmat[:, :], False, True, [Tsb, Mmat], [pb_])
            cp("act", negflat[:, :], pb_[:, 0:128], [pb_], [negflat])
            S = nxt("S")
            for j in range(16):
                for h in range(8):
                    hp, pbs = h // 2, 64 * (h % 2)
                    mm(S[:, 8 * j + h:8 * j + h + 1], kTc[pbs:pbs + 64, hp, 128 * j:128 * j + 128], qT[pbs:pbs + 64, hp, b:b + 1], True, True, [kTc, qT], [S])
            sd = rtmp[0]
            tt("dve", sd[:, 0:128], S[:, 0:128], negflat[:, :], ALU.add, [S, negflat], [sd])
            PT = PTb[0]
            act(PT[:, 0:128], sd[:, 0:128], AF.Exp, [sd], [PT])
            O = nxt("O")
            for h in range(8):
                for j in range(16):
                    mm(O[0:65, h:h + 1], Vc[:, j, h, :], PT[:, 8 * j + h:8 * j + h + 1], j == 0, False, [Vc, PT], [O])
                mm(O[0:65, h:h + 1], Vn[:, h, :], pnm[:, b, h:h + 1], False, True, [Vn, pnm], [O])
            k.emit("dve", lambda e, O=O: e.reciprocal(out=rl[64:65, 0:8], in_=O[64:65, 0:8]), [O], [rl])
            B = nxt("T")
            mm(B[0:64, 0:8], onesf[64:65, 0:64], rl[64:65, 0:8], True, True, [onesf, rl], [B])
            cp("act", bcs[:, 0:8], B[0:64, 0:8], [B], [bcs])
            tt("dve", attnT[:, 0:8, b:b + 1].rearrange("p h o -> p (h o)"), O[0:64, 0:8], bcs[:, 0:8], ALU.mult, [O, bcs], [attnT])

        def layer0_decode():
            n = ND
            load_x_dec()
            rmsnorm(gmix, lambda dc: gmix[:, 0, dc:dc + 1], n)
            wbA, wA = w_next()
            for hp in range(4):
                p = nxt("mm")
                for kc in range(8):
                    mm(p[:, 0:n], wA[:, kc, 128 * hp:128 * hp + 128], xnT[:, kc, 0:n], kc == 0, kc == 7, [wbA, xnT], [p])
                ts("dve", qT[:, hp, 0:n], p[:, 0:n], 0.125, None, ALU.mult, None, [p], [qT])
                p2 = nxt("mm")
                for kc in range(8):
                    mm(p2[:, 0:n], wA[:, kc, 512 + 128 * hp:512 + 128 * hp + 128], xnT[:, kc, 0:n], kc == 0, kc == 7, [wbA, xnT], [p2])
                cp("dve", kTn[:, hp, :], p2[:, 0:n], [p2], [kTn])
            wbB, wB = w_next(keep_prev=True)
            for j in range(3):
                cp("dve", wf3[:, :, 8 * j:8 * j + 8], wB[:, :, 512:520], [wbB], [wf3])
            qtok = sct[0]
            pq = nxt("mm")
            for kc in range(8):
                mm(pq[0:n, 0:512], xnT[:, kc, 0:n], wA[:, kc, 0:512], kc == 0, kc == 7, [wbA, xnT], [pq])
            cp("act", qtok[0:n, :], pq[0:n, 0:512], [pq], [qtok])
            pk = nxt("mm")
            for kc in range(8):
                mm(pk[0:n, 0:512], xnT[:, kc, 0:n], wA[:, kc, 512:1024], kc == 0, kc == 7, [wbA, xnT], [pk])
            cp("act", kvst[0:n, 0:512], pk[0:n, 0:512], [pk], [kvst])
            pv = nxt("mm")
            for kc in range(8):
                mm(pv[0:n, 0:512], xnT[:, kc, 0:n], wB[:, kc, 0:512], kc == 0, kc == 7, [wbB, xnT], [pv])
            cp("act", kvst[0:n, 512:1024], pv[0:n, 0:512], [pv], [kvst])
            cp("dve", Vn[:, :, 0:64], kvst[0:n, 512:1024].rearrange("p (h d) -> p h d", h=8), [kvst], [Vn])
            k.dma("sp", fox_kv_s[:, :], kvst[0:n, :], gdo, reads=[kvst])
            prod = sct[1]
            tt("dve", prod[0:n, :], qtok[0:n, :], kvst[0:n, 0:512], ALU.mult, [qtok, kvst], [prod])
            k.emit("dve", lambda e: e.reduce_sum(out=sml[:, 0:8], in_=prod[0:n, :].rearrange("p (h d) -> p h d", h=8), axis=AX.X), [prod], [sml])
            pf = nxt("mm")
            for kc in range(8):
                mm(pf[0:24, 0:n], wf3[:, kc, :], xnT[:, kc, 0:n], kc == 0, kc == 7, [wf3, xnT], [pf])
            act(lfT[:, 0:n], pf[0:24, 0:n], AF.Exp, [pf], [lfT], scale=-1.0, bias=nbf3[:, 0:1])
            act(lfT[:, 0:n], lfT[:, 0:n], AF.Ln, [lfT], [lfT], bias=1.0)
            ts("dve", lfT[:, 0:n], lfT[:, 0:n], -1.0, None, ALU.mult, None, [lfT], [lfT])
            p = nxt("T")
            k.emit("pe", lambda e, p=p: e.transpose(out=p[0:n, 0:8], in_=lfT[0:8, 0:n], identity=ident[0:8, 0:8]), [lfT, ident], [p])
            cp("dve", sml[:, 8:16], p[0:n, 0:8], [p], [sml])
            k.dma("sp", fox_logf_s[:, :], sml[:, 8:16], gdo, reads=[sml], **NCD)
            stt("dve", sml[:, 16:24], sml[:, 0:8], 0.125, sml[:, 8:16], ALU.mult, ALU.subtract, [sml], [sml])
            act(sml[:, 24:32], sml[:, 16:24], AF.Exp, [sml], [sml])
            for b in range(ND):
                ts("dve", pnm[:, b, 0:8], sml[:, 24:32], ident[0:ND, b:b + 1], None, ALU.mult, None, [sml, ident], [pnm])
            for b in range(ND):
                fox_decode_seq(b)
            k.dma("sp", sct[0][0:48, :], conv_s[:, :], gds, writes=[sct[0]])
            k.dma("sp", sct[1][0:ND, :], h_s[:, :], gds, writes=[sct[1]])
            for rb in range(4):
                p = nxt("T")
                k.emit("pe", lambda e, p=p, rb=rb: e.transpose(out=p[:, 0:48], in_=sct[0][0:48, 128 * rb:128 * rb + 128], identity=ident[0:48, 0:48]), [sct[0], ident], [p])
                k.emit("pe", lambda e, p=p, rb=rb: e.transpose(out=p[:, 64:64 + ND], in_=sct[1][0:ND, 128 * rb:128 * rb + 128], identity=ident[0:ND, 0:ND]), [sct[1], ident], [p])
                cp("act", cvT[:, rb, :, :].rearrange("p b j -> p (b j)"), p[:, 0:48], [p], [cvT])
                cp("act", h0T[:, rb, :], p[:, 64:64 + ND], [p], [h0T])
            wbC, wC = w_next()
            for rb in range(4):
                xc, gt_, rr_, ii_, aa_, uu_, xr_, gl_ = tmp
                p = nxt("mm")
                for kc in range(8):
                    mm(p[:, 0:n], wC[:, kc, 128 * rb:128 * rb + 128], xnT[:, kc, 0:n], kc == 0, kc == 7, [wbC, xnT], [p])
                cp("act", xr_[:, 0:n], p[:, 0:n], [p], [xr_])
                pg = nxt("mm")
                for kc in range(8):
                    mm(pg[:, 0:n], wC[:, kc, 512 + 128 * rb:512 + 128 * rb + 128], xnT[:, kc, 0:n], kc == 0, kc == 7, [wbC, xnT], [pg])
                cp("act", gt_[:, 0:n], pg[:, 0:n], [pg], [gt_])
                ts("dve", xc[:, 0:n], xr_[:, 0:n], convw[:, 3, rb:rb + 1], convb[:, rb:rb + 1], ALU.mult, ALU.add, [xr_, convw, convb], [xc])
                for j in range(3):
                    stt("dve", xc[:, 0:n], cvT[:, rb, :, j], convw[:, j, rb:rb + 1], xc[:, 0:n], ALU.mult, ALU.add, [cvT, convw, xc], [xc])
                cp("dve", cvn[:, rb, :, 0], cvT[:, rb, :, 1], [cvT], [cvn])
                cp("dve", cvn[:, rb, :, 1], cvT[:, rb, :, 2], [cvT], [cvn])
                cp("dve", cvn[:, rb, :, 2], xr_[:, 0:n], [xr_], [cvn])
                cp("dve", xcb[:, 0:n], xc[:, 0:n], [xc], [xcb])
                pr = nxt("mm")
                mm(pr[:, 0:n], gabd[:, rb, :], xcb[:, 0:n], True, True, [gabd, xcb], [pr])
                pi = nxt("mm")
                mm(pi[:, 0:n], gxbd[:, rb, :], xcb[:, 0:n], True, True, [gxbd, xcb], [pi])
                act(rr_[:, 0:n], pr[:, 0:n], AF.Sigmoid, [pr, gab], [rr_], bias=gab[:, rb:rb + 1])
                act(ii_[:, 0:n], pi[:, 0:n], AF.Sigmoid, [pi, gxb], [ii_], bias=gxb[:, rb:rb + 1])
                act(aa_[:, 0:n], rr_[:, 0:n], AF.Exp, [rr_, clam], [aa_], scale=clam[:, rb:rb + 1])
                tt("dve", uu_[:, 0:n], aa_[:, 0:n], aa_[:, 0:n], ALU.mult, [aa_], [uu_])
                ts("dve", uu_[:, 0:n], uu_[:, 0:n], -1.0, 1.0, ALU.mult, ALU.add, [uu_], [uu_])
                act(uu_[:, 0:n], uu_[:, 0:n], AF.Sqrt, [uu_], [uu_])
                tt("dve", uu_[:, 0:n], uu_[:, 0:n], ii_[:, 0:n], ALU.mult, [uu_, ii_], [uu_])
                tt("dve", uu_[:, 0:n], uu_[:, 0:n], xc[:, 0:n], ALU.mult, [uu_, xc], [uu_])
                tt("dve", aa_[:, 0:n], aa_[:, 0:n], h0T[:, rb, :], ALU.mult, [aa_, h0T], [aa_])
                tt("dve", hsd[:, rb, :], aa_[:, 0:n], uu_[:, 0:n], ALU.add, [aa_, uu_], [hsd])
                tt("dve", gl_[:, 0:n], gt_[:, 0:n], gt_[:, 0:n], ALU.mult, [gt_], [gl_])
                ts("dve", gl_[:, 0:n], gl_[:, 0:n], 0.044715, 1.0, ALU.mult, ALU.add, [gl_], [gl_])
                tt("dve", gl_[:, 0:n], gl_[:, 0:n], gt_[:, 0:n], ALU.mult, [gl_, gt_], [gl_])
                act(gl_[:, 0:n], gl_[:, 0:n], AF.Tanh, [gl_], [gl_], scale=0.7978845608028654)
                stt("dve", gl_[:, 0:n], gl_[:, 0:n], 1.0, gt_[:, 0:n], ALU.add, ALU.mult, [gl_, gt_], [gl_])
                stt("dve", lruT[:, rb, 0:n], gl_[:, 0:n], 0.5, hsd[:, rb, :], ALU.mult, ALU.mult, [gl_, hsd], [lruT])
            p = nxt("T")
            p2 = nxt("T")
            for rb in range(4):
                k.emit("pe", lambda e, p=p, rb=rb: e.transpose(out=p[0:48, 128 * rb:128 * rb + 128], in_=cvn[:, rb, :, :].rearrange("p b j -> p (b j)"), identity=ident[:, :]), [cvn, ident], [p])
                k.emit("pe", lambda e, p2=p2, rb=rb: e.transpose(out=p2[0:ND, 128 * rb:128 * rb + 128], in_=hsd[:, rb, :], identity=ident[:, :]), [hsd, ident], [p2])
            cp("act", sct[0][0:48, :], p[0:48, :], [p], [sct[0]])
            cp("act", sct[1][0:ND, :], p2[0:ND, :], [p2], [sct[1]])
            k.dma("sp", lru_conv_s[:, :], sct[0][0:48, :], gdo, reads=[sct[0]])
            k.dma("sp", lru_h_s[:, :], sct[1][0:ND, :], gdo, reads=[sct[1]])
            wbD, wD = w_next()
            wbE, wE = w_next(keep_prev=True)
            for ob in range(8):
                p = nxt("mm")
                for h in range(8):
                    mm(p[:, 0:n], wD[:, h, 128 * ob:128 * ob + 128], attnT[:, h, 0:n], h == 0, False, [wbD, attnT], [p])
                for rb in range(4):
                    mm(p[:, 0:n], wE[:, rb, 128 * ob:128 * ob + 128], lruT[:, rb, 0:n], False, rb == 3, [wbE, lruT], [p])
                tt("dve", hT[:, ob, 0:n], hT[:, ob, 0:n], p[:, 0:n], ALU.add, [hT, p], [hT])
            rmsnorm(gffn, lambda dc: gffn[:, 0, dc:dc + 1], n)
            ffn(n)
            for dc in range(8):
                cp("dve", hsave[:, dc, :], hT[:, dc, 0:n], [hT], [hsave])

        def layer1_decode():
            n = ND
            gt = NT
            for dc in range(8):
                cp("dve", hT[:, dc, 0:n], hsave[:, dc, :], [hsave], [hT])
            rmsnorm(gmix, lambda dc: gmix[:, 1, dc:dc + 1], n)
            col0 = 0
            for pi_, wcols in enumerate((1024, 1024, 72)):
                wbX, wX = w_next()
                for cb in range(0, wcols, 512):
                    wdt = min(512, wcols - cb)
                    p = nxt("mm")
                    for kc in range(8):
                        mm(p[0:n, 0:wdt], xnT[:, kc, 0:n], wX[:, kc, cb:cb + wdt], kc == 0, kc == 7, [wbX, xnT], [p])
                    cp("act", tok1[0:n, 0, col0 + cb:col0 + cb + wdt], p[0:n, 0:wdt], [p], [tok1])
                col0 += wcols
            kin = kin_t[0]
            cp("dve", Vn[:, 0:4, 0:64], tok1[0:n, 0, 1280:1536].rearrange("p (h d) -> p h d", h=4), [tok1], [Vn])
            k.emit("dve", lambda e: e.reduce_sum(out=lst[0:n, 0:1], in_=tok1[0:n, 0, 2048:2112], axis=AX.X), [tok1], [lst])
            ts("dve", lst[0:n, 0:1], lst[0:n, 0:1], 1.0 / 64.0, None, ALU.mult, None, [lst], [lst])
            ts("dve", kin[0:n, :], tok1[0:n, 0, 2048:2112], lst[0:n, 0:1], None, ALU.subtract, None, [tok1, lst], [kin])
            tt("dve", rt1[0:n, 0:8, :].rearrange("p a b -> p (a b)"), kin[0:n, :], kin[0:n, :], ALU.mult, [kin], [rt1])
            k.emit("dve", lambda e: e.reduce_sum(out=lst[0:n, 1:2], in_=rt1[0:n, 0:8, :].rearrange("p a b -> p (a b)"), axis=AX.X), [rt1], [lst])
            act(lst[0:n, 2:3], lst[0:n, 1:2], AF.Sqrt, [lst], [lst], scale=1.0 / 64.0, bias=1e-6)
            k.emit("dve", lambda e: e.reciprocal(out=lst[0:n, 3:4], in_=lst[0:n, 2:3]), [lst], [lst])
            ts("dve", kin[0:n, :], kin[0:n, :], lst[0:n, 3:4], None, ALU.mult, None, [kin, lst], [kin])
            tt("dve", kin[0:n, :], kin[0:n, :], lngb[0:n, 0:64], ALU.mult, [kin, lngb], [kin])
            tt("dve", kin[0:n, :], kin[0:n, :], lngb[0:n, 64:128], ALU.add, [kin, lngb], [kin])
            for (vw, H, bufv) in ((tok1[0:n, 0, 0:1280].rearrange("p (h d) -> p h d", d=64), 20, tok1),
                                  (tok1[0:n, 0, 1536:2048].rearrange("p (h d) -> p h d", d=64), 8, tok1),
                                  (kin[0:n, :].rearrange("p (h d) -> p h d", d=64), 1, kin)):
                cs = cosT[0:n, gt:gt + 1, :].to_broadcast([n, H, 8])
                sn = sinT[0:n, gt:gt + 1, :].to_broadcast([n, H, 8])
                x1 = vw[:, :, 0:8]
                x2 = vw[:, :, 8:16]
                tt("dve", rt1[0:n, 0:H, :], x1, cs, ALU.mult, [bufv, cosT], [rt1])
                tt("dve", rt2[0:n, 0:H, :], x2, sn, ALU.mult, [bufv, sinT], [rt2])
                tt("dve", rt3[0:n, 0:H, :], x2, cs, ALU.mult, [bufv, cosT], [rt3])
                tt("dve", rt4[0:n, 0:H, :], x1, sn, ALU.mult, [bufv, sinT], [rt4])
                tt("dve", x1, rt1[0:n, 0:H, :], rt2[0:n, 0:H, :], ALU.subtract, [rt1, rt2], [bufv])
                tt("dve", x2, rt3[0:n, 0:H, :], rt4[0:n, 0:H, :], ALU.add, [rt3, rt4], [bufv])
            k.dma("sp", dsa_kv_s[:, :], tok1[0:n, 0, 1024:1536], gdo, reads=[tok1])
            k.dma("sp", dsa_idxk_s[:, :], kin[0:n, :], gdo, reads=[kin])
            for a in range(2):
                src = tok1[0:n, 0, 512 * a:512 * a + 512].rearrange("p (b cc d) -> p cc b d", b=2, cc=4)
                dst = qperm[0:n, 512 * a:512 * a + 512].rearrange("p (cc b d) -> p cc b d", cc=4, b=2)
                ts("dve", dst, src, 0.125, None, ALU.mult, None, [tok1], [qperm])
            p = nxt("T")
            for blk in range(8):
                k.emit("pe", lambda e, p=p, blk=blk: e.transpose(out=p[:, ND * blk:ND * blk + ND], in_=qperm[0:ND, 128 * blk:128 * blk + 128], identity=ident[0:ND, 0:ND]), [qperm, ident], [p])
            cp("act", qT1[:, :, 0:n], p[:, 0:8 * ND].rearrange("p (i t) -> p i t", i=8), [p], [qT1])
            p = nxt("T")
            for i in range(4):
                k.emit("pe", lambda e, p=p, i=i: e.transpose(out=p[:, ND * i:ND * i + ND], in_=tok1[0:ND, 0, 1536 + 128 * i:1536 + 128 * i + 128], identity=ident[0:ND, 0:ND]), [tok1, ident], [p])
            cp("act", qiT[:, :, 0:n], p[:, 0:4 * ND].rearrange("p (i t) -> p i t", i=4), [p], [qiT])
            ts("dve", wis[0:n, 0, :], tok1[0:n, 0, 2112:2120], 512.0 ** -0.5, None, ALU.mult, None, [tok1], [wis])
            act(absw[0:n, 0, :], wis[0:n, 0, :], AF.Abs, [wis], [absw])
            k.emit("act", lambda e: e.sign(out=sgn[0:n, 0, :], in_=wis[0:n, 0, :]), [wis], [sgn])
            for b in range(ND):
                ts("dve", sgnm[:, b, :], sgn[0:n, 0, :], ident[0:ND, b:b + 1], None, ALU.mult, None, [sgn, ident], [sgnm])
            qiv = tok1[0:n, 0, 1536:2048].rearrange("p (h d) -> p h d", h=8)
            kib = kin[0:n, :].rearrange("p (o d) -> p o d", o=1).to_broadcast([n, 8, 64])
            prod = sct[1]
            tt("dve", prod[0:n, :].rearrange("p (h d) -> p h d", h=8), qiv, kib, ALU.mult, [tok1, kin], [prod])
            k.emit("dve", lambda e: e.reduce_sum(out=sml[:, 0:8], in_=prod[0:n, :].rearrange("p (h d) -> p h d", h=8), axis=AX.X), [prod], [sml])
            ts("dve", sml[:, 0:8], sml[:, 0:8], 0.0, None, ALU.max, None, [sml], [sml])
            tt("dve", sml[:, 0:8], sml[:, 0:8], wis[0:n, 0, :], ALU.mult, [sml, wis], [sml])
            k.emit("dve", lambda e: e.reduce_sum(out=sml[:, 8:9], in_=sml[:, 0:8], axis=AX.X), [sml], [sml])
            qv = tok1[0:n, 0, 0:1024].rearrange("p (kv g d) -> p kv g d", kv=4, g=4)
            kv_ = tok1[0:n, 0, 1024:1280].rearrange("p (kv d) -> p kv d", kv=4)
            prod2 = qperm
            for g in range(4):
                tt("dve", prod2[0:n, :].rearrange("p (kv g d) -> p kv g d", kv=4, g=4)[:, :, g, :], qv[:, :, g, :], kv_, ALU.mult, [tok1], [prod2])
            k.emit("dve", lambda e: e.reduce_sum(out=sml[:, 16:32], in_=prod2[0:n, :].rearrange("p (h d) -> p h d", h=16), axis=AX.X), [prod2], [sml])
            scd = BufView(tok1, tok1[:, :, :].rearrange("p a b -> p (a b)")[0:ND, 0:2049])
            wkd = BufView(tok1, tok1[:, :, :].rearrange("p a b -> p (a b)")[0:ND, 2100:4149])
            memset("dve", scd[:, :], 0.0, [scd])
            cp("dve", scd[:, 2048:2049], sml[:, 8:9], [sml], [scd])
            stg = BufView(xtok, xtok[:, :, :].rearrange("p a b -> p (a b)").rearrange("p (i f) -> p i f", i=4))
            for b in range(ND):
                for g in range(4):
                    for i in range(4):
                        dyn_dma(stg[:, i, 0:64], c_dik, 16 * b + 4 * g + i, gst, [stg] if i in (0, 3) else [])
                    p = nxt("T")
                    for i in range(4):
                        cp("dve", kblk[:, 0:64], stg[:, i, 0:64], [stg], [kblk])
                        cp("dve", kblk[:, 64:128], stg[:, i, 0:64], [stg], [kblk])
                        k.emit("pe", lambda e, p=p, i=i: e.transpose(out=p[:, 128 * i:128 * i + 128], in_=kblk[:, :], identity=ident[:, :]), [kblk, ident], [p])
                    cp("act", kTc[:, 2, 512 * g:512 * g + 512], p[:, :], [p], [kTc])
                for h in range(8):
                    half, blk = h % 2, h // 2
                    for kb0 in range(0, 2048, 512):
                        p = nxt("S")
                        mm(p[0:n, 0:512], qiT[64 * half:64 * half + 64, blk, 0:n], kTc[64 * half:64 * half + 64, 2, kb0:kb0 + 512], True, True, [qiT, kTc], [p])
                        sc_ = sct[(h + kb0 // 512) % 2]
                        act(sc_[0:n, :], p[0:n, 0:512], AF.Relu, [p, absw], [sc_], scale=absw[0:n, 0, h:h + 1])
                        stt("dve", scd[:, kb0:kb0 + 512], sc_[0:n, :], sgnm[:, b, h:h + 1], scd[:, kb0:kb0 + 512], ALU.mult, ALU.add, [sc_, sgnm, scd], [scd])
            srcb = scd
            for r in range(32):
                k.emit("dve", lambda e, srcb=srcb: e.max(out=m8[0:n, :], in_=srcb[:, :]), [srcb], [m8])
                if r < 31:
                    k.emit("dve", lambda e, srcb=srcb: e.match_replace(out=wkd[:, :], in_to_replace=m8[0:n, :], in_values=srcb[:, :], imm_value=-1e30), [srcb, m8], [wkd])
                    srcb = wkd
            ts("dve", wkd[:, :], scd[:, :], m8[0:n, 7:8], None, ALU.is_ge, None, [scd, m8], [wkd])
            ts("dve", wkd[:, :], wkd[:, :], -1.0, 30000.0, ALU.add, ALU.mult, [wkd], [wkd])
            for g in range(4):
                p = nxt("T")
                for i in range(4):
                    pg = 4 * g + i
                    k.emit("pe", lambda e, p=p, i=i, pg=pg: e.transpose(out=p[:, ND * i:ND * i + ND], in_=wkd[:, 128 * pg:128 * pg + 128], identity=ident[0:ND, 0:ND]), [wkd, ident], [p])
                cp("act", maskTd[:, 4 * g:4 * g + 4, :], p[:, 0:4 * ND].rearrange("p (i t) -> p i t", i=4), [p], [maskTd])
            for h in range(16):
                stt("dve", sml[:, 32 + h:33 + h], sml[:, 16 + h:17 + h], 0.125, wkd[:, 2048:2049], ALU.mult, ALU.add, [sml, wkd], [sml])
            act(sml[:, 48:64], sml[:, 32:48], AF.Exp, [sml], [sml])
            for b in range(ND):
                ts("dve", pnm[:, b, :], sml[:, 48:64], ident[0:ND, b:b + 1], None, ALU.mult, None, [sml, ident], [pnm])
            stg2 = BufView(xtok, xtok[:, :, :].rearrange("p a b -> p (a b)").rearrange("p (i f) -> p i f", i=4))
            for b in range(ND):
                for g in range(4):
                    for i in range(4):
                        dyn_dma(stg2[:, i, :], c_dkv, 16 * b + 4 * g + i, gst, [stg2] if i in (0, 3) else [])
                    for i in range(4):
                        pg = 4 * g + i
                        p = nxt("T")
                        for blk in range(2):
                            k.emit("pe", lambda e, p=p, i=i, blk=blk: e.transpose(out=p[:, 128 * blk:128 * blk + 128], in_=stg2[:, i, 128 * blk:128 * blk + 128], identity=ident[:, :]), [stg2, ident], [p])
                        cp("act", kTc[:, 0:2, 128 * pg:128 * pg + 128], p[:, 0:256].rearrange("p (i t) -> p i t", i=2), [p], [kTc])
                        cp("dve", Vc[:, pg, 0:4, 0:64], stg2[:, i, 256:512].rearrange("p (h d) -> p h d", h=4), [stg2], [Vc])
                S = nxt("S")
                for j in range(16):
                    for a in range(2):
                        for b2 in range(2):
                            for cc in range(4):
                                h = 8 * a + 4 * b2 + cc
                                blkq, pbs = 4 * a + cc, 64 * b2
                                mm(S[:, 16 * j + h:16 * j + h + 1], kTc[pbs:pbs + 64, a, 128 * j:128 * j + 128], qT1[pbs:pbs + 64, blkq, b:b + 1], True, True, [kTc, qT1], [S])
                PT = PTb[0]
                for j in range(16):
                    act(PT[:, 16 * j:16 * j + 16], S[:, 16 * j:16 * j + 16], AF.Exp, [S, maskTd], [PT], bias=maskTd[:, j, b:b + 1])
                O = nxt("O")
                for h in range(16):
                    kvh = 2 * (h // 8) + (h // 4) % 2
                    for j in range(16):
                        mm(O[0:65, h:h + 1], Vc[:, j, kvh, :], PT[:, 16 * j + h:16 * j + h + 1], j == 0, False, [Vc, PT], [O])
                    mm(O[0:65, h:h + 1], Vn[:, kvh, :], pnm[:, b, h:h + 1], False, True, [Vn, pnm], [O])
                k.emit("dve", lambda e, O=O: e.reciprocal(out=rl[64:65, 0:16], in_=O[64:65, 0:16]), [O], [rl])
                B = nxt("T")
                mm(B[0:64, 0:16], onesf[64:65, 0:64], rl[64:65, 0:16], True, True, [onesf, rl], [B])
                cp("act", bcs[:, 0:16], B[0:64, 0:16], [B], [bcs])
                tt("dve", attnT[:, 0:16, b:b + 1].rearrange("p h o -> p (h o)"), O[0:64, 0:16], bcs[:, 0:16], ALU.mult, [O, bcs], [attnT])
            wbD, wD = w_next()
            wbE, wE = w_next(keep_prev=True)
            for ob in range(8):
                p = nxt("mm")
                for h in range(16):
                    wsl, wbuf = (wD, wbD) if h < 8 else (wE, wbE)
                    mm(p[:, 0:n], wsl[:, h % 8, 128 * ob:128 * ob + 128], attnT[:, h, 0:n], h == 0, h == 15, [wbuf, attnT], [p])
                tt("dve", hT[:, ob, 0:n], hT[:, ob, 0:n], p[:, 0:n], ALU.add, [hT, p], [hT])
            rmsnorm(gffn, lambda dc: gffn[:, 1, dc:dc + 1], n)
            ffn(n)
            rmsnorm(gfin, lambda dc: gfin[:, dc:dc + 1], n)
            p = nxt("T")
            p2 = nxt("T")
            for dc in range(8):
                yt = tmp[dc % 4]
                stt("dve", yt[:, 0:n], hT[:, dc, 0:n], gfin[:, dc:dc + 1], rstd[:, 0:n], ALU.mult, ALU.mult, [hT, gfin, rstd], [yt])
                pp = p if dc < 4 else p2
                k.emit("pe", lambda e, pp=pp, dc=dc, yt=yt: e.transpose(out=pp[0:ND, 128 * (dc % 4):128 * (dc % 4) + 128], in_=yt[:, 0:ND], identity=ident[:, :]), [yt, ident], [pp])
            cp("act", kvst[0:n, 0:512], p[0:n, :], [p], [kvst])
            cp("act", kvst[0:n, 512:1024], p2[0:n, :], [p2], [kvst])
            k.dma("sp", y_s[:, :], kvst[0:n, :], gdo, reads=[kvst])

        for c in (range(NCH) if do_prompt else []):
            t0 = c * N
            k.dma("sp", xtok[:, :, :], x_p[t0:t0 + N, :].rearrange("(j p) d -> p j d", p=128), gio, writes=[xtok])
            for j in range(TPC):
                for g in range(2):
                    p = nxt("T")
                    for i in range(4):
                        dc = 4 * g + i
                        k.emit("pe", lambda e, p=p, i=i, j=j, dc=dc: e.transpose(out=p[:, 128 * i:128 * i + 128], in_=xtok[:, j, 128 * dc:128 * dc + 128], identity=ident[:, :]), [xtok, ident], [p])
                    cp("act", hT[:, 4 * g:4 * g + 4, 128 * j:128 * j + 128], p[:, :].rearrange("p (i t) -> p i t", i=4), [p], [hT])
            rmsnorm(gmix, lambda dc: gmix[:, 0, dc:dc + 1])
            wbA, wA = w_next()
            for hp in range(4):
                p = nxt("mm")
                for kc in range(8):
                    mm(p[:, 0:N], wA[:, kc, 128 * hp:128 * hp + 128], xnT[:, kc, :], kc == 0, kc == 7, [wbA, xnT], [p])
                ts("dve", qT[:, hp, :], p[:, 0:N], 0.125, None, ALU.mult, None, [p], [qT])
                p2 = nxt("mm")
                for kc in range(8):
                    mm(p2[:, 0:N], wA[:, kc, 512 + 128 * hp:512 + 128 * hp + 128], xnT[:, kc, :], kc == 0, kc == 7, [wbA, xnT], [p2])
                cp("dve", kTc[:, hp, t0:t0 + N], p2[:, 0:N], [p2], [kTc])
            wbB, wB = w_next(keep_prev=True)
            for j in range(3):
                cp("dve", wf3[:, :, 8 * j:8 * j + 8], wB[:, :, 512:520], [wbB], [wf3])
            for j in range(TPC):
                gt = c * TPC + j
                pk = nxt("mm")
                for kc in range(8):
                    mm(pk[:, 0:512], xnT[:, kc, 128 * j:128 * j + 128], wA[:, kc, 512:1024], kc == 0, kc == 7, [wbA, xnT], [pk])
                pv = nxt("mm")
                for kc in range(8):
                    mm(pv[:, 0:512], xnT[:, kc, 128 * j:128 * j + 128], wB[:, kc, 0:512], kc == 0, kc == 7, [wbB, xnT], [pv])
                cp("act", kvst[:, 0:512], pk[:, 0:512], [pk], [kvst])
                cp("dve", kvst[:, 512:1024], pv[:, 0:512], [pv], [kvst])
                cp("dve", Vc[:, gt, :, 0:64], kvst[:, 512:1024].rearrange("p (h d) -> p h d", h=8), [kvst], [Vc])
                k.dma("sp", fox_kv_p[t0 + 128 * j:t0 + 128 * j + 128, :], kvst[:, :], gkv, reads=[kvst])
            pf = nxt("mm")
            for kc in range(8):
                mm(pf[0:24, 0:N], wf3[:, kc, :], xnT[:, kc, :], kc == 0, kc == 7, [wf3, xnT], [pf])
            act(lfT[:, :], pf[0:24, 0:N], AF.Exp, [pf], [lfT], scale=-1.0, bias=nbf3[:, 0:1])
            act(lfT[:, :], lfT[:, :], AF.Ln, [lfT], [lfT], bias=1.0)
            ts("dve", lfT[:, :], lfT[:, :], -1.0, None, ALU.mult, None, [lfT], [lfT])
            k.emit("dve", lambda e: e.tensor_tensor_scan(out=cT[:, :], data0=onesf[0:24, 0:N], data1=lfT[:, :], initial=ccar[:, 0:1], op0=ALU.mult, op1=ALU.add), [onesf, lfT, ccar], [cT])
            cp("dve", ccar[:, 0:1], cT[:, N - 1:N], [cT], [ccar])
            cp("dve", hib[:, :], cT[:, :], [cT], [hib])
            tt("dve", r1[:, :], cT[:, :], hib[:, :], ALU.subtract, [cT, hib], [r1])
            cp("dve", midb[:, :], r1[:, :], [r1], [midb])
            tt("dve", r1[:, :], r1[:, :], midb[:, :], ALU.subtract, [r1, midb], [r1])
            cp("dve", lob[:, :], r1[:, :], [r1], [lob])
            ts("dve", cq3f[:, :], hib[:, :], m3[:, 0:1], None, ALU.mult, None, [hib, m3], [cq3f])
            stt("dve", cq3f[:, :], midb[:, :], m3[:, 1:2], cq3f[:, :], ALU.mult, ALU.add, [midb, m3, cq3f], [cq3f])
            stt("dve", cq3f[:, :], lob[:, :], m3[:, 2:3], cq3f[:, :], ALU.mult, ALU.add, [lob, m3, cq3f], [cq3f])
            cp("dve", cq3[:, :], cq3f[:, :], [cq3f], [cq3])
            for j in range(TPC):
                gt = c * TPC + j
                p = nxt("T")
                k.emit("pe", lambda e, p=p, j=j: e.transpose(out=p[:, 0:8], in_=lfT[0:8, 128 * j:128 * j + 128], identity=ident[0:8, 0:8]), [lfT, ident], [p])
                k.emit("pe", lambda e, p=p, j=j: e.transpose(out=p[:, 8:16], in_=cT[0:8, 128 * j:128 * j + 128], identity=ident[0:8, 0:8]), [cT, ident], [p])
                cp("dve", lfst[:, j, :], p[:, 0:8], [p], [lfst])
                ts("dve", negck[:, gt, :], p[:, 8:16], -1.0, None, ALU.mult, None, [p], [negck])
            k.dma("sp", fox_logf_p[t0:t0 + N, :].rearrange("(j p) h -> p j h", p=128), lfst[:, :, :], glf, reads=[lfst], **NCD)
            for h in range(8):
                hp, pb = h // 2, 64 * (h % 2)
                O = nxt("O")
                nk = (c + 1) * TPC

                def emitS(kt, h=h, hp=hp, pb=pb):
                    j = kt - c * TPC
                    qlo = 0 if j < 0 else 128 * j
                    S = nxt("S")
                    mm(S[:, qlo:N], kTc[pb:pb + 64, hp, 128 * kt:128 * kt + 128], qT[pb:pb + 64, hp, qlo:N], True, False, [kTc, qT], [S])
                    if j >= 0:
                        mm(S[:, qlo:qlo + 128], identb[:, :], cmask[:, :], False, False, [identb, cmask], [S])
                    mm(S[:, qlo:N], ind[:, h, :], cq3[:, qlo:N], False, True, [ind, cq3], [S])
                    return S, qlo

                cur = emitS(0)
                for kt in range(nk):
                    nx = emitS(kt + 1) if kt + 1 < nk else None
                    S, qlo = cur
                    PT = PTb[kt % 2]
                    act(PT[:, qlo:N], S[:, qlo:N], AF.Exp, [S, negck], [PT], bias=negck[:, kt, h:h + 1])
                    mm(O[0:65, qlo:N], Vc[:, kt, h, :], PT[:, qlo:N], kt == 0, kt == nk - 1, [Vc, PT], [O])
                    cur = nx
                k.emit("dve", lambda e, O=O: e.reciprocal(out=rl[64:65, :], in_=O[64:65, 0:N]), [O], [rl])
                B = nxt("T")
                mm(B[0:64, 0:N], onesf[64:65, 0:64], rl[64:65, :], True, True, [onesf, rl], [B])
                cp("act", bcs[:, :], B[0:64, 0:N], [B], [bcs])
                tt("dve", attnT[:, h, :], O[0:64, 0:N], bcs[:, :], ALU.mult, [O, bcs], [attnT])
            wbC, wC = w_next()
            for rb in range(4):
                xc, gt_, rr_, ii_, aa_, uu_, hs_, gl_ = tmp
                cp("dve", xrext[:, rb, 0:3], xrext[:, rb, N:N + 3], [xrext], [xrext])
                p = nxt("mm")
                for kc in range(8):
                    mm(p[:, 0:N], wC[:, kc, 128 * rb:128 * rb + 128], xnT[:, kc, :], kc == 0, kc == 7, [wbC, xnT], [p])
                cp("act", xrext[:, rb, 3:3 + N], p[:, 0:N], [p], [xrext])
                pg = nxt("mm")
                for kc in range(8):
                    mm(pg[:, 0:N], wC[:, kc, 512 + 128 * rb:512 + 128 * rb + 128], xnT[:, kc, :], kc == 0, kc == 7, [wbC, xnT], [pg])
                cp("act", gt_[:, :], pg[:, 0:N], [pg], [gt_])
                ts("dve", xc[:, :], xrext[:, rb, 3:3 + N], convw[:, 3, rb:rb + 1], convb[:, rb:rb + 1], ALU.mult, ALU.add, [xrext, convw, convb], [xc])
                for j in range(3):
                    stt("dve", xc[:, :], xrext[:, rb, j:j + N], convw[:, j, rb:rb + 1], xc[:, :], ALU.mult, ALU.add, [xrext, convw, xc], [xc])
                cp("dve", xcb[:, :], xc[:, :], [xc], [xcb])
                pr = nxt("mm")
                mm(pr[:, 0:N], gabd[:, rb, :], xcb[:, :], True, True, [gabd, xcb], [pr])
                pi = nxt("mm")
                mm(pi[:, 0:N], gxbd[:, rb, :], xcb[:, :], True, True, [gxbd, xcb], [pi])
                act(rr_[:, :], pr[:, 0:N], AF.Sigmoid, [pr, gab], [rr_], bias=gab[:, rb:rb + 1])
                act(ii_[:, :], pi[:, 0:N], AF.Sigmoid, [pi, gxb], [ii_], bias=gxb[:, rb:rb + 1])
                act(aa_[:, :], rr_[:, :], AF.Exp, [rr_, clam], [aa_], scale=clam[:, rb:rb + 1])
                tt("dve", uu_[:, :], aa_[:, :], aa_[:, :], ALU.mult, [aa_], [uu_])
                ts("dve", uu_[:, :], uu_[:, :], -1.0, 1.0, ALU.mult, ALU.add, [uu_], [uu_])
                act(uu_[:, :], uu_[:, :], AF.Sqrt, [uu_], [uu_])
                tt("dve", uu_[:, :], uu_[:, :], ii_[:, :], ALU.mult, [uu_, ii_], [uu_])
                tt("dve", uu_[:, :], uu_[:, :], xc[:, :], ALU.mult, [uu_, xc], [uu_])
                k.emit("dve", lambda e, rb=rb, aa_=aa_, uu_=uu_, hs_=hs_: e.tensor_tensor_scan(out=hs_[:, :], data0=aa_[:, :], data1=uu_[:, :], initial=hcar[:, rb:rb + 1], op0=ALU.mult, op1=ALU.add), [aa_, uu_, hcar], [hs_])
                cp("dve", hcar[:, rb:rb + 1], hs_[:, N - 1:N], [hs_], [hcar])
                tt("dve", gl_[:, :], gt_[:, :], gt_[:, :], ALU.mult, [gt_], [gl_])
                ts("dve", gl_[:, :], gl_[:, :], 0.044715, 1.0, ALU.mult, ALU.add, [gl_], [gl_])
                tt("dve", gl_[:, :], gl_[:, :], gt_[:, :], ALU.mult, [gl_, gt_], [gl_])
                act(gl_[:, :], gl_[:, :], AF.Tanh, [gl_], [gl_], scale=0.7978845608028654)
                stt("dve", gl_[:, :], gl_[:, :], 1.0, gt_[:, :], ALU.add, ALU.mult, [gl_, gt_], [gl_])
                stt("dve", lruT[:, rb, :], gl_[:, :], 0.5, hs_[:, :], ALU.mult, ALU.mult, [gl_, hs_], [lruT])
            if c == NCH - 1:
                for rb in range(4):
                    k.dma("sp", lru_conv_p[:, 128 * rb:128 * rb + 128].rearrange("j p -> p j"), xrext[:, rb, N:N + 3], gsm, reads=[xrext], **NCD)
                k.dma("sp", lru_h_p[0, :].rearrange("(c p) -> p c", p=128), hcar[:, :], gsm, reads=[hcar], **NCD)
            wbD, wD = w_next()
            wbE, wE = w_next(keep_prev=True)
            for ob in range(8):
                p = nxt("mm")
                for h in range(8):
                    mm(p[:, 0:N], wD[:, h, 128 * ob:128 * ob + 128], attnT[:, h, :], h == 0, False, [wbD, attnT], [p])
                for rb in range(4):
                    mm(p[:, 0:N], wE[:, rb, 128 * ob:128 * ob + 128], lruT[:, rb, :], False, rb == 3, [wbE, lruT], [p])
                tt("dve", hT[:, ob, :], hT[:, ob, :], p[:, 0:N], ALU.add, [hT, p], [hT])
            rmsnorm(gffn, lambda dc: gffn[:, 0, dc:dc + 1])
            ffn()
            k.dma("sp", h1_scr[c, :, :], hT[:, :, :].rearrange("p a b -> p (a b)"), gh1, reads=[hT], writes=[scrb[c]])

        if do_dec:
            layer0_decode()
        for c in (range(NCH) if do_prompt else []):
            t0 = c * N
            k.dma("sp", hT[:, :, :].rearrange("p a b -> p (a b)"), h1_scr[c, :, :], gh1r, reads=[scrb[c]], writes=[hT])
            rmsnorm(gmix, lambda dc: gmix[:, 1, dc:dc + 1])
            col0 = 0
            for pi_, wcols in enumerate((1024, 1024, 72)):
                wbX, wX = w_next()
                for j in range(TPC):
                    for cb in range(0, wcols, 512):
                        wdt = min(512, wcols - cb)
                        p = nxt("mm")
                        for kc in range(8):
                            mm(p[:, 0:wdt], xnT[:, kc, 128 * j:128 * j + 128], wX[:, kc, cb:cb + wdt], kc == 0, kc == 7, [wbX, xnT], [p])
                        cp("act", tok1[:, j, col0 + cb:col0 + cb + wdt], p[:, 0:wdt], [p], [tok1])
                col0 += wcols
            for j in range(TPC):
                gt = c * TPC + j
                kin = kin_t[j]
                cp("dve", Vc[:, gt, 0:4, 0:64], tok1[:, j, 1280:1536].rearrange("p (h d) -> p h d", h=4), [tok1], [Vc])
                k.emit("dve", lambda e, j=j: e.reduce_sum(out=lst[:, 0:1], in_=tok1[:, j, 2048:2112], axis=AX.X), [tok1], [lst])
                ts("dve", lst[:, 0:1], lst[:, 0:1], 1.0 / 64.0, None, ALU.mult, None, [lst], [lst])
                ts("dve", kin[:, :], tok1[:, j, 2048:2112], lst[:, 0:1], None, ALU.subtract, None, [tok1, lst], [kin])
                tt("dve", rt1[:, 0:8, :].rearrange("p a b -> p (a b)"), kin[:, :], kin[:, :], ALU.mult, [kin], [rt1])
                k.emit("dve", lambda e: e.reduce_sum(out=lst[:, 1:2], in_=rt1[:, 0:8, :].rearrange("p a b -> p (a b)"), axis=AX.X), [rt1], [lst])
                act(lst[:, 2:3], lst[:, 1:2], AF.Sqrt, [lst], [lst], scale=1.0 / 64.0, bias=1e-6)
                k.emit("dve", lambda e: e.reciprocal(out=lst[:, 3:4], in_=lst[:, 2:3]), [lst], [lst])
                ts("dve", kin[:, :], kin[:, :], lst[:, 3:4], None, ALU.mult, None, [kin, lst], [kin])
                tt("dve", kin[:, :], kin[:, :], lngb[:, 0:64], ALU.mult, [kin, lngb], [kin])
                tt("dve", kin[:, :], kin[:, :], lngb[:, 64:128], ALU.add, [kin, lngb], [kin])
                for (vw, H, bufv) in ((tok1[:, j, 0:1280].rearrange("p (h d) -> p h d", d=64), 20, tok1),
                                      (tok1[:, j, 1536:2048].rearrange("p (h d) -> p h d", d=64), 8, tok1),
                                      (kin[:, :].rearrange("p (h d) -> p h d", d=64), 1, kin)):
                    cs = cosT[:, gt:gt + 1, :].to_broadcast([128, H, 8])
                    sn = sinT[:, gt:gt + 1, :].to_broadcast([128, H, 8])
                    x1 = vw[:, :, 0:8]
                    x2 = vw[:, :, 8:16]
                    tt("dve", rt1[:, 0:H, :], x1, cs, ALU.mult, [bufv, cosT], [rt1])
                    tt("dve", rt2[:, 0:H, :], x2, sn, ALU.mult, [bufv, sinT], [rt2])
                    tt("dve", rt3[:, 0:H, :], x2, cs, ALU.mult, [bufv, cosT], [rt3])
                    tt("dve", rt4[:, 0:H, :], x1, sn, ALU.mult, [bufv, sinT], [rt4])
                    tt("dve", x1, rt1[:, 0:H, :], rt2[:, 0:H, :], ALU.subtract, [rt1, rt2], [bufv])
                    tt("dve", x2, rt3[:, 0:H, :], rt4[:, 0:H, :], ALU.add, [rt3, rt4], [bufv])
                k.dma("sp", dsa_kv_p[t0 + 128 * j:t0 + 128 * j + 128, :], tok1[:, j, 1024:1536], gdkv, reads=[tok1])
                k.dma("sp", dsa_idxk_p[t0 + 128 * j:t0 + 128 * j + 128, :], kin[:, :], gdik, reads=[kin])
            for j in range(TPC):
                gt = c * TPC + j
                for a in range(2):
                    src = tok1[:, j, 512 * a:512 * a + 512].rearrange("p (b cc d) -> p cc b d", b=2, cc=4)
                    dst = qperm[:, 512 * a:512 * a + 512].rearrange("p (cc b d) -> p cc b d", cc=4, b=2)
                    ts("dve", dst, src, 0.125, None, ALU.mult, None, [tok1], [qperm])
                for g in range(2):
                    p = nxt("T")
                    for i in range(4):
                        blk = 4 * g + i
                        k.emit("pe", lambda e, p=p, i=i, blk=blk: e.transpose(out=p[:, 128 * i:128 * i + 128], in_=qperm[:, 128 * blk:128 * blk + 128], identity=ident[:, :]), [qperm, ident], [p])
                    cp("act", qT1[:, 4 * g:4 * g + 4, 128 * j:128 * j + 128], p[:, :].rearrange("p (i t) -> p i t", i=4), [p], [qT1])
                p = nxt("T")
                for i in range(4):
                    k.emit("pe", lambda e, p=p, i=i, j=j: e.transpose(out=p[:, 128 * i:128 * i + 128], in_=tok1[:, j, 1536 + 128 * i:1536 + 128 * i + 128], identity=ident[:, :]), [tok1, ident], [p])
                cp("act", qiT[:, :, 128 * j:128 * j + 128], p[:, :].rearrange("p (i t) -> p i t", i=4), [p], [qiT])
                cp("dve", kblk[:, 0:64], kin_t[j][:, :], [kin_t[j]], [kblk])
                cp("dve", kblk[:, 64:128], kin_t[j][:, :], [kin_t[j]], [kblk])
                p = nxt("T")
                for i in range(2):
                    k.emit("pe", lambda e, p=p, i=i, j=j: e.transpose(out=p[:, 128 * i:128 * i + 128], in_=tok1[:, j, 1024 + 128 * i:1024 + 128 * i + 128], identity=ident[:, :]), [tok1, ident], [p])
                k.emit("pe", lambda e, p=p: e.transpose(out=p[:, 256:384], in_=kblk[:, :], identity=ident[:, :]), [kblk, ident], [p])
                cp("act", kTc[:, 0:3, 128 * gt:128 * gt + 128], p[:, 0:384].rearrange("p (i t) -> p i t", i=3), [p], [kTc])
                ts("dve", wis[:, j, :], tok1[:, j, 2112:2120], 512.0 ** -0.5, None, ALU.mult, None, [tok1], [wis])
                act(absw[:, j, :], wis[:, j, :], AF.Abs, [wis], [absw])
                k.emit("act", lambda e, j=j: e.sign(out=sgn[:, j, :], in_=wis[:, j, :]), [wis], [sgn])
            for j in range(TPC):
                gt = c * TPC + j
                L = 128 * (gt + 1)
                for h in range(8):
                    half, blk = h % 2, h // 2
                    for kb0 in range(0, L, 512):
                        wdt = min(512, L - kb0)
                        p = nxt("S")
                        mm(p[:, 0:wdt], qiT[64 * half:64 * half + 64, blk, 128 * j:128 * j + 128], kTc[64 * half:64 * half + 64, 2, kb0:kb0 + wdt], True, True, [qiT, kTc], [p])
                        sc_ = sct[(h + kb0 // 512) % 2]
                        act(sc_[:, 0:wdt], p[:, 0:wdt], AF.Relu, [p, absw], [sc_], scale=absw[:, j, h:h + 1])
                        if h == 0:
                            ts("dve", score[:, kb0:kb0 + wdt], sc_[:, 0:wdt], sgn[:, j, 0:1], None, ALU.mult, None, [sc_, sgn], [score])
                        else:
                            stt("dve", score[:, kb0:kb0 + wdt], sc_[:, 0:wdt], sgn[:, j, h:h + 1], score[:, kb0:kb0 + wdt], ALU.mult, ALU.add, [sc_, sgn, score], [score])
                tt("dve", score[:, L - 128:L], score[:, L - 128:L], cmq[:, :], ALU.add, [score, cmq], [score])
                if gt >= 2:
                    srcb = score
                    for r in range(32):
                        k.emit("dve", lambda e, srcb=srcb, L=L: e.max(out=m8[:, :], in_=srcb[:, 0:L]), [srcb], [m8])
                        if r < 31:
                            k.emit("dve", lambda e, srcb=srcb, L=L: e.match_replace(out=wk[:, 0:L], in_to_replace=m8[:, :], in_values=srcb[:, 0:L], imm_value=-1e30), [srcb, m8], [wk])
                            srcb = wk
                    ts("dve", wk[:, 0:L], score[:, 0:L], m8[:, 7:8], None, ALU.is_ge, None, [score, m8], [wk])
                else:
                    ts("dve", wk[:, 0:L], score[:, 0:L], -1e29, None, ALU.is_gt, None, [score], [wk])
                ts("dve", wk[:, 0:L], wk[:, 0:L], -1.0, 30000.0, ALU.add, ALU.mult, [wk], [wk])
                for kb0 in range(0, gt + 1, 4):
                    nb = min(4, gt + 1 - kb0)
                    p = nxt("T")
                    for i in range(nb):
                        k.emit("pe", lambda e, p=p, i=i, kb0=kb0: e.transpose(out=p[:, 128 * i:128 * i + 128], in_=wk[:, 128 * (kb0 + i):128 * (kb0 + i) + 128], identity=ident[:, :]), [wk, ident], [p])
                    cp("act", maskT[:, kb0:kb0 + nb, :], p[:, 0:128 * nb].rearrange("p (i t) -> p i t", i=nb), [p], [maskT])
                for h in range(16):
                    a, b, cc = h // 8, (h // 4) % 2, h % 4
                    blkq, pbs, kvh = 4 * a + cc, 64 * b, 2 * a + b
                    O = nxt("O")

                    def emitS1(kb, a=a, pbs=pbs, blkq=blkq, j=j):
                        S = nxt("S")
                        mm(S[:, 0:128], kTc[pbs:pbs + 64, a, 128 * kb:128 * kb + 128], qT1[pbs:pbs + 64, blkq, 128 * j:128 * j + 128], True, False, [kTc, qT1], [S])
                        mm(S[:, 0:128], identb[:, :], maskT[:, kb, :], False, True, [identb, maskT], [S])
                        return S

                    cur = emitS1(0)
                    for kb in range(gt + 1):
                        nx = emitS1(kb + 1) if kb + 1 <= gt else None
                        S = cur
                        PT = PTb[kb % 2]
                        act(PT[:, 0:128], S[:, 0:128], AF.Exp, [S], [PT])
                        mm(O[0:65, 0:128], Vc[:, kb, kvh, :], PT[:, 0:128], kb == 0, kb == gt, [Vc, PT], [O])
                        cur = nx
                    k.emit("dve", lambda e, O=O: e.reciprocal(out=rl[64:65, 0:128], in_=O[64:65, 0:128]), [O], [rl])
                    B = nxt("T")
                    mm(B[0:64, 0:128], onesf[64:65, 0:64], rl[64:65, 0:128], True, True, [onesf, rl], [B])
                    cp("act", bcs[:, 0:128], B[0:64, 0:128], [B], [bcs])
                    tt("dve", attnT[:, h, 128 * j:128 * j + 128], O[0:64, 0:128], bcs[:, 0:128], ALU.mult, [O, bcs], [attnT])
            wbD, wD = w_next()
            wbE, wE = w_next(keep_prev=True)
            for ob in range(8):
                p = nxt("mm")
                for h in range(16):
                    wsl, wbuf = (wD, wbD) if h < 8 else (wE, wbE)
                    mm(p[:, 0:N], wsl[:, h % 8, 128 * ob:128 * ob + 128], attnT[:, h, :], h == 0, h == 15, [wbuf, attnT], [p])
                tt("dve", hT[:, ob, :], hT[:, ob, :], p[:, 0:N], ALU.add, [hT, p], [hT])
            rmsnorm(gffn, lambda dc: gffn[:, 1, dc:dc + 1])
            ffn()
            rmsnorm(gfin, lambda dc: gfin[:, dc:dc + 1])
            for j in range(TPC):
                for g in range(2):
                    p = nxt("T")
                    for i in range(4):
                        dc = 4 * g + i
                        yt = tmp[i]
                        stt("dve", yt[:, 0:128], hT[:, dc, 128 * j:128 * j + 128], gfin[:, dc:dc + 1], rstd[:, 128 * j:128 * j + 128], ALU.mult, ALU.mult, [hT, gfin, rstd], [yt])
                        k.emit("pe", lambda e, p=p, i=i, yt=yt: e.transpose(out=p[:, 128 * i:128 * i + 128], in_=yt[:, 0:128], identity=ident[:, :]), [yt, ident], [p])
                    cp("act", xtok[:, j, 512 * g:512 * g + 512], p[:, :], [p], [xtok])
            k.dma("sp", y_p[t0:t0 + N, :].rearrange("(j p) d -> p j d", p=128), xtok[:, :, :], gy, reads=[xtok])

        if do_dec:
            layer1_decode()
        k.finish()
        k.replay()
    return nc


_OUT_SHAPES = None


def kernel(x_prompt, x_sample, cache_fox_kv, cache_fox_logf, state_lru_conv, state_lru_h,
           cache_dsa_kv, cache_dsa_idx_k, page_table, norm_mix, norm_ffn, norm_final,
           ab_w_in, ab_b_f, ab_conv_w, ab_conv_b, ab_gate_a_w, ab_gate_a_b, ab_gate_x_w, ab_gate_x_b,
           ab_lambda, ab_w_out, c_w_in, c_idx_norm_g, c_idx_norm_b, c_w_out, ffn_w1, ffn_w2):
    f = lambda a: np.ascontiguousarray(np.asarray(a), dtype=np.float32)
    shared = {
        "norm_mix": f(norm_mix), "norm_ffn": f(norm_ffn), "norm_final": f(norm_final).reshape(1, D),
        "ab_w_in": f(ab_w_in)[0], "ab_b_f": f(ab_b_f).reshape(1, 8), "ab_conv_w": f(ab_conv_w)[0],
        "ab_conv_b": f(ab_conv_b).reshape(1, 512), "ab_ga_w": f(ab_gate_a_w)[0], "ab_ga_b": f(ab_gate_a_b).reshape(1, 512),
        "ab_gx_w": f(ab_gate_x_w)[0], "ab_gx_b": f(ab_gate_x_b).reshape(1, 512), "ab_lambda": f(ab_lambda).reshape(1, 512),
        "ab_w_out": f(ab_w_out)[0], "c_w_in": f(c_w_in)[0], "c_idx_g": f(c_idx_norm_g).reshape(1, 64),
        "c_idx_b": f(c_idx_norm_b).reshape(1, 64), "c_w_out": f(c_w_out)[0], "ffn_w1": f(ffn_w1), "ffn_w2": f(ffn_w2),
    }
    xp = f(x_prompt)
    xs = f(x_sample).reshape(128, D)
    pt = np.ascontiguousarray(np.asarray(page_table), dtype=np.int32)
    convs = f(state_lru_conv)[0]
    hs = f(state_lru_h)[0]
    n_phys = int(np.asarray(cache_fox_kv).shape[1])
    shared["c_fkv"] = f(cache_fox_kv)[0].reshape(n_phys * 128, 1024)
    shared["c_flf"] = f(cache_fox_logf)[0].reshape(n_phys * 128, 8)
    shared["c_dkv"] = f(cache_dsa_kv)[0].reshape(n_phys * 128, 512)
    shared["c_dik"] = f(cache_dsa_idx_k)[0].reshape(n_phys * 128, 64)
    nc = build_nc(n_phys)
    in_maps = []
    for i in range(8):
        m = dict(shared)
        m["x_p"] = xp[i]
        m["x_s"] = xs[16 * i:16 * i + 16]
        m["pt"] = pt[16 * i:16 * i + 16].reshape(1, 256)
        m["conv_s"] = convs[16 * i:16 * i + 16].reshape(48, 512)
        m["h_s"] = hs[16 * i:16 * i + 16]
        in_maps.append(m)
    res = run_bass_kernel_spmd(nc, in_maps, core_ids=list(range(8)))
    R = res.results
    return assemble(R)


def assemble(R):
    B, DB = 8, 128
    cat = lambda name: np.concatenate([R[i][name] for i in range(8)], axis=0)
    st = lambda name: np.stack([R[i][name] for i in range(8)])
    y_prompt = st("y_p").reshape(B, T, D)
    y_sample = cat("y_s").reshape(DB, 1, D)
    fox_kv_p = st("fox_kv_p").reshape(1, B, T, 2, 8, 64)
    fox_logf_p = st("fox_logf_p").reshape(1, B, T, 8)
    lru_conv_p = st("lru_conv_p").reshape(1, B, 3, 512)
    lru_h_p = st("lru_h_p").reshape(1, B, 512)
    dsa_kv_p = st("dsa_kv_p").reshape(1, B, T, 2, 4, 64)
    dsa_idxk_p = st("dsa_idxk_p").reshape(1, B, T, 64)
    fox_kv_s = cat("fox_kv_s").reshape(1, DB, 1, 2, 8, 64)
    fox_logf_s = cat("fox_logf_s").reshape(1, DB, 1, 8)
    lru_conv_s = cat("lru_conv_s").reshape(1, DB, 3, 512)
    lru_h_s = cat("lru_h_s").reshape(1, DB, 512)
    dsa_kv_s = cat("dsa_kv_s").reshape(1, DB, 1, 2, 4, 64)
    dsa_idxk_s = cat("dsa_idxk_s").reshape(1, DB, 1, 64)
    return (y_prompt, y_sample, fox_kv_p, fox_logf_p, lru_conv_p, lru_h_p, dsa_kv_p, dsa_idxk_p,
            fox_kv_s, fox_logf_s, lru_conv_s, lru_h_s, dsa_kv_s, dsa_idxk_s)
```

```python
import bisect
import math
import numpy as np
from contextlib import ExitStack
import concourse.bass as bass
import concourse.mybir as mybir
from concourse.bass_utils import run_bass_kernel_spmd

F32 = mybir.dt.float32
BF16 = mybir.dt.bfloat16
I32 = mybir.dt.int32
ALU = mybir.AluOpType
AF = mybir.ActivationFunctionType
AX = mybir.AxisListType

T = 2048
D = 1024
N = 256
NCH = T // N
TPC = N // 128
NT = T // 128
STAGE = 5


class Buf:
    def __init__(self, name, t):
        self.name = name
        self.t = t
        self.lw = None
        self.rd = {}

    def __getitem__(self, idx):
        return self.t[idx]


class BufView(Buf):
    def __init__(self, base, ap):
        self.base = base
        self.name = base.name + "_v"
        self.t = ap

    lw = property(lambda s: s.base.lw, lambda s, v: setattr(s.base, "lw", v))
    rd = property(lambda s: s.base.rd, lambda s, v: setattr(s.base, "rd", v))


class Rec:
    __slots__ = ("fn", "waits", "inc", "seq", "grp")

    def __init__(self, fn, waits, grp=None):
        self.fn = fn
        self.waits = waits
        self.inc = False
        self.seq = 0
        self.grp = grp


class Grp:
    def __init__(self, sem):
        self.sem = sem
        self.count = 0


class K:
    def __init__(self, nc, stack):
        self.nc = nc
        self.stack = stack
        self.engs = ["pe", "act", "dve", "pool", "sp"]
        self.recs = {e: [] for e in self.engs}
        self.esem = {e: stack.enter_context(nc.semaphore("es_" + e)) for e in self.engs}
        self.ecnt = {e: 0 for e in self.engs}
        self.incidx = {e: [] for e in self.engs}
        self.incseq = {e: [] for e in self.engs}
        self.known = {e: {} for e in self.engs}
        self.grps = []

    def sb(self, name, shape, dt):
        return Buf(name, self.stack.enter_context(self.nc.sbuf_tensor(name, list(shape), dt)))

    def ps(self, name, shape, dt=F32):
        return Buf(name, self.stack.enter_context(self.nc.psum_tensor(name, list(shape), dt)))

    def grp(self, name):
        g = Grp(self.stack.enter_context(self.nc.semaphore("g_" + name)))
        self.grps.append(g)
        return g

    def _resolve(self, d):
        if d[0] == "e":
            f, idx = d[1], d[2]
            pos = bisect.bisect_left(self.incidx[f], idx)
            if pos < len(self.incidx[f]):
                return self.esem[f], ("e", f), self.incseq[f][pos]
            rec = self.recs[f][-1]
            rec.inc = True
            self.ecnt[f] += 1
            rec.seq = self.ecnt[f]
            self.incidx[f].append(len(self.recs[f]) - 1)
            self.incseq[f].append(rec.seq)
            return self.esem[f], ("e", f), rec.seq
        g, cnt = d[1], d[2]
        return g.sem, ("d", id(g)), cnt

    def _deps(self, e, reads, writes):
        deps = []
        for b in reads:
            if b.lw is not None:
                deps.append(b.lw)
        for b in writes:
            if b.lw is not None:
                deps.append(b.lw)
            deps.extend(b.rd.values())
        waits = {}
        for d in deps:
            if d[0] == "e" and d[1] == e and e == "pe":
                continue
            sem, key, val = self._resolve(d)
            if self.known[e].get(key, 0) >= val:
                continue
            if key not in waits or waits[key][1] < val:
                waits[key] = (sem, val)
        for key, (sem, val) in waits.items():
            self.known[e][key] = val
        return list(waits.values())

    def emit(self, e, fn, reads=(), writes=()):
        waits = self._deps(e, reads, writes)
        self.recs[e].append(Rec(fn, waits))
        idx = len(self.recs[e]) - 1
        for b in reads:
            b.rd[e] = ("e", e, idx)
        for b in writes:
            b.lw = ("e", e, idx)
            b.rd = {}

    def dmaf(self, e, fn, grp, reads=(), writes=()):
        waits = self._deps(e, reads, writes)
        self.recs[e].append(Rec(fn, waits, grp))
        grp.count += 16
        for b in reads:
            b.rd[("d", id(grp))] = ("d", grp, grp.count)
        for b in writes:
            b.lw = ("d", grp, grp.count)
            b.rd = {}

    def dma(self, e, out, in_, grp, reads=(), writes=(), **kw):
        self.dmaf(e, lambda eng: eng.dma_start(out=out, in_=in_, **kw), grp, reads, writes)

    def finish(self):
        waits = [(g.sem, g.count) for g in self.grps if g.count > 0]
        self.recs["sp"].append(Rec(None, waits))

    def replay(self):
        me = self

        def run(e, eng):
            for rec in me.recs[e]:
                for sem, val in rec.waits:
                    eng.wait_ge(sem, val)
                if rec.fn is None:
                    continue
                ins = rec.fn(eng)
                if rec.grp is not None:
                    ins.then_inc(rec.grp.sem, 16)
                if rec.inc:
                    ins.then_inc(me.esem[e], 1)

        with self.nc.Block() as block:
            @block.tensor
            def _(eng):
                run("pe", eng)

            @block.scalar
            def _(eng):
                run("act", eng)

            @block.vector
            def _(eng):
                run("dve", eng)

            @block.gpsimd
            def _(eng):
                run("pool", eng)

            @block.sync
            def _(eng):
                run("sp", eng)


def build_nc(n_phys=2560, do_prompt=True, do_dec=True):
    nc = bass.Bass("TRN2", target_bir_lowering=False)

    def din(name, shape, dt=F32):
        return nc.dram_tensor(name, list(shape), dt, kind="ExternalInput").ap()

    def dout(name, shape):
        return nc.dram_tensor(name, list(shape), F32, kind="ExternalOutput").ap()

    x_p = din("x_p", [T, D])
    norm_mix = din("norm_mix", [2, D])
    norm_ffn = din("norm_ffn", [2, D])
    norm_final = din("norm_final", [1, D])
    ab_w_in = din("ab_w_in", [D, 2568])
    ab_b_f = din("ab_b_f", [1, 8])
    ab_conv_w = din("ab_conv_w", [4, 512])
    ab_conv_b = din("ab_conv_b", [1, 512])
    ab_ga_w = din("ab_ga_w", [8, 64, 64])
    ab_ga_b = din("ab_ga_b", [1, 512])
    ab_gx_w = din("ab_gx_w", [8, 64, 64])
    ab_gx_b = din("ab_gx_b", [1, 512])
    ab_lambda = din("ab_lambda", [1, 512])
    ab_w_out = din("ab_w_out", [D, D])
    c_w_in = din("c_w_in", [D, 2120])
    c_idx_g = din("c_idx_g", [1, 64])
    c_idx_b = din("c_idx_b", [1, 64])
    c_w_out = din("c_w_out", [D, D])
    ffn_w1 = din("ffn_w1", [2, D, 4096])
    ffn_w2 = din("ffn_w2", [2, 4096, D])

    x_s = din("x_s", [16, D])
    pt_in = din("pt", [1, 256], I32)
    conv_s = din("conv_s", [48, 512])
    h_s = din("h_s", [16, 512])
    c_fkv = din("c_fkv", [n_phys * 128, 1024])
    c_flf = din("c_flf", [n_phys * 128, 8])
    c_dkv = din("c_dkv", [n_phys * 128, 512])
    c_dik = din("c_dik", [n_phys * 128, 64])
    y_s = dout("y_s", [16, D])
    fox_kv_s = dout("fox_kv_s", [16, 1024])
    fox_logf_s = dout("fox_logf_s", [16, 8])
    lru_conv_s = dout("lru_conv_s", [48, 512])
    lru_h_s = dout("lru_h_s", [16, 512])
    dsa_kv_s = dout("dsa_kv_s", [16, 512])
    dsa_idxk_s = dout("dsa_idxk_s", [16, 64])
    h1s_scr = nc.dram_tensor("h1s_scr", [128, 8 * 16], F32, kind="Internal").ap()
    y_p = dout("y_p", [T, D])
    fox_kv_p = dout("fox_kv_p", [T, 1024])
    fox_logf_p = dout("fox_logf_p", [T, 8])
    lru_conv_p = dout("lru_conv_p", [3, 512])
    lru_h_p = dout("lru_h_p", [1, 512])
    dsa_kv_p = dout("dsa_kv_p", [T, 512])
    dsa_idxk_p = dout("dsa_idxk_p", [T, 64])
    def dscr(name, shape):
        return nc.dram_tensor(name, list(shape), BF16, kind="Internal").ap()
    sb_win0 = dscr("sb_win0", [D, 2568])
    sb_wout0 = dscr("sb_wout0", [D, D])
    sb_win1 = dscr("sb_win1", [D, 2120])
    sb_wout1 = dscr("sb_wout1", [D, D])
    sb_w1 = dscr("sb_w1", [2, D, 4096])
    sb_w2 = dscr("sb_w2", [2, 4096, D])
    h1_scr = nc.dram_tensor("h1_scr", [NCH, 128, 8 * N], F32, kind="Internal").ap()

    with ExitStack() as st:
        k = K(nc, st)

        def mm(out, lhsT, rhs, start, stop, R, W):
            k.emit("pe", lambda e: e.matmul(out=out, lhsT=lhsT, rhs=rhs, start=start, stop=stop), R, W)

        def act(out, in_, func, R, W, bias=None, scale=None, accum=None, eng="act"):
            kw = {}
            if bias is not None:
                kw["bias"] = bias
            if scale is not None:
                kw["scale"] = scale
            if accum is not None:
                kw["accum_out"] = accum
            k.emit(eng, lambda e: e.activation(out=out, in_=in_, func=func, **kw), R, W)

        def tt(eng, out, in0, in1, op, R, W):
            k.emit(eng, lambda e: e.tensor_tensor(out=out, in0=in0, in1=in1, op=op), R, W)

        def ts(eng, out, in0, s1, s2, op0, op1, R, W):
            if s2 is None:
                k.emit(eng, lambda e: e.tensor_scalar(out=out, in0=in0, scalar1=s1, scalar2=None, op0=op0), R, W)
            else:
                k.emit(eng, lambda e: e.tensor_scalar(out=out, in0=in0, scalar1=s1, scalar2=s2, op0=op0, op1=op1), R, W)

        def stt(eng, out, in0, scalar, in1, op0, op1, R, W):
            k.emit(eng, lambda e: e.scalar_tensor_tensor(out=out, in0=in0, scalar=scalar, in1=in1, op0=op0, op1=op1), R, W)

        def cp(eng, out, in_, R, W):
            if eng == "act":
                k.emit(eng, lambda e: e.activation(out=out, in_=in_, func=AF.Copy), R, W)
            else:
                k.emit(eng, lambda e: e.tensor_copy(out=out, in_=in_), R, W)

        def memset(eng, ap, val, W):
            k.emit(eng, lambda e: e.memset(ap, val), [], W)

        def asel(out, in_, op, fill, base, pattern, cm, R, W):
            k.emit("pool", lambda e: e.affine_select(out=out, in_=in_, compare_op=op, fill=fill, base=base,
                                                     pattern=pattern, channel_multiplier=cm), R, W)

        pmm = [k.ps("pmm%d" % i, [128, 512]) for i in range(2)]
        pS = [k.ps("pS%d" % i, [128, 512]) for i in range(2)]
        pO = [k.ps("pO%d" % i, [128, 512]) for i in range(2)]
        pT = [k.ps("pT%d" % i, [128, 512]) for i in range(2)]
        rr = {"mm": 0, "S": 0, "O": 0, "T": 0}

        def nxt(kind):
            lst = {"mm": pmm, "S": pS, "O": pO, "T": pT}[kind]
            rr[kind] = (rr[kind] + 1) % len(lst)
            return lst[rr[kind]]

        gc = k.grp("const")
        ident = k.sb("ident", [128, 128], F32)
        identb = k.sb("identb", [128, 128], BF16)
        memset("pool", ident[:, :], 0.0, [ident])
        asel(ident[:, :], ident[:, :], ALU.not_equal, 1.0, 0, [[-1, 128]], 1, [ident], [ident])
        cp("dve", identb[:, :], ident[:, :], [ident], [identb])
        cmaskf = k.sb("cmaskf", [128, 128], F32)
        cmask = k.sb("cmask", [128, 128], BF16)
        memset("pool", cmaskf[:, :], 0.0, [cmaskf])
        asel(cmaskf[:, :], cmaskf[:, :], ALU.is_ge, -30000.0, 0, [[1, 128]], -1, [cmaskf], [cmaskf])
        cp("dve", cmask[:, :], cmaskf[:, :], [cmaskf], [cmask])
        cmq = k.sb("cmq", [128, 128], F32)
        memset("pool", cmq[:, :], 0.0, [cmq])
        asel(cmq[:, :], cmq[:, :], ALU.is_ge, -1e30, 0, [[-1, 128]], 1, [cmq], [cmq])
        onesb = k.sb("onesb", [128, 128], BF16)
        memset("dve", onesb[:, :], 1.0, [onesb])
        onesf = k.sb("onesf", [128, 256], F32)
        memset("dve", onesf[:, :], 1.0, [onesf])
        gmix = k.sb("gmix", [128, 2, 8], F32)
        gffn = k.sb("gffn", [128, 2, 8], F32)
        gfin = k.sb("gfin", [128, 8], F32)
        convw = k.sb("convw", [128, 4, 4], F32)
        convb = k.sb("convb", [128, 4], F32)
        gab = k.sb("gab", [128, 4], F32)
        gxb = k.sb("gxb", [128, 4], F32)
        lam = k.sb("lam", [128, 4], F32)
        clam = k.sb("clam", [128, 4], F32)
        nbf3 = k.sb("nbf3", [24, 1], F32)
        NCD = dict(allow_slow_non_contiguous=True)
        if True:
            for l in range(2):
                k.dma("sp", gmix[:, l, :], norm_mix[l, :].rearrange("(c p) -> p c", p=128), gc, writes=[gmix], **NCD)
                k.dma("sp", gffn[:, l, :], norm_ffn[l, :].rearrange("(c p) -> p c", p=128), gc, writes=[gffn], **NCD)
            k.dma("sp", gfin[:, :], norm_final[0, :].rearrange("(c p) -> p c", p=128), gc, writes=[gfin], **NCD)
            for j in range(4):
                k.dma("sp", convw[:, j, :], ab_conv_w[j, :].rearrange("(c p) -> p c", p=128), gc, writes=[convw], **NCD)
            k.dma("sp", convb[:, :], ab_conv_b[0, :].rearrange("(c p) -> p c", p=128), gc, writes=[convb], **NCD)
            k.dma("sp", gab[:, :], ab_ga_b[0, :].rearrange("(c p) -> p c", p=128), gc, writes=[gab], **NCD)
            k.dma("sp", gxb[:, :], ab_gx_b[0, :].rearrange("(c p) -> p c", p=128), gc, writes=[gxb], **NCD)
            k.dma("sp", lam[:, :], ab_lambda[0, :].rearrange("(c p) -> p c", p=128), gc, writes=[lam], **NCD)
            for j in range(3):
                k.dma("sp", nbf3[8 * j:8 * j + 8, :], ab_b_f[0, :].rearrange("(p o) -> p o", o=1), gc, writes=[nbf3], **NCD)
        sct = [k.sb("sct%d" % i, [128, 512], F32) for i in range(2)]
        gaf = BufView(sct[0], sct[0][:, :].rearrange("p (a b) -> p a b", a=4))
        gxf = BufView(sct[1], sct[1][:, :].rearrange("p (a b) -> p a b", a=4))
        gabd = k.sb("gabd", [128, 4, 128], BF16)
        gxbd = k.sb("gxbd", [128, 4, 128], BF16)
        memset("dve", gaf[:, :, :], 0.0, [gaf])
        memset("dve", gxf[:, :, :], 0.0, [gxf])
        for rb in range(4):
            for i in range(2):
                k.dma("sp", gaf[64 * i:64 * i + 64, rb, 64 * i:64 * i + 64], ab_ga_w[2 * rb + i, :, :], gc, writes=[gaf])
                k.dma("sp", gxf[64 * i:64 * i + 64, rb, 64 * i:64 * i + 64], ab_gx_w[2 * rb + i, :, :], gc, writes=[gxf])
        for b_ in (gmix, gffn, gfin, convw, convb, gab, gxb, lam, nbf3, gaf, gxf):
            if b_.lw is not None and b_.lw[0] == "d":
                b_.lw = ("d", gc, gc.count)
        ts("dve", nbf3[:, :], nbf3[:, :], -1.0, None, ALU.mult, None, [nbf3], [nbf3])
        act(clam[:, :], lam[:, :], AF.Exp, [lam], [clam], scale=-1.0)
        act(clam[:, :], clam[:, :], AF.Ln, [clam], [clam], bias=1.0)
        ts("dve", clam[:, :], clam[:, :], -8.0, None, ALU.mult, None, [clam], [clam])
        cp("dve", gabd[:, :, :], gaf[:, :, :], [gaf], [gabd])
        cp("dve", gxbd[:, :, :], gxf[:, :, :], [gxf], [gxbd])
        m3 = k.sb("m3", [24, 3], F32)
        memset("pool", m3[:, :], 1.0, [m3])
        asel(m3[:, :], m3[:, :], ALU.is_ge, 0.0, 0, [[-8, 3]], 1, [m3], [m3])
        asel(m3[:, :], m3[:, :], ALU.is_ge, 0.0, 7, [[8, 3]], -1, [m3], [m3])
        xtok = k.sb("xtok", [128, TPC, D], F32)
        indf = BufView(xtok, xtok[0:24, 0, :].rearrange("p (h k) -> p h k", h=8))
        ind = k.sb("ind", [24, 8, 128], BF16)
        memset("pool", indf[:, :, :], 0.0, [indf])
        for j in range(3):
            asel(indf[:, :, :], indf[:, :, :], ALU.not_equal, 1.0, -8 * j, [[-1, 8], [0, 128]], 1, [indf], [indf])
        cp("dve", ind[:, :, :], indf[:, :, :], [indf], [ind])


        NTX = NT + 1
        posi = k.sb("posi", [128, NTX], I32)
        posf = k.sb("posf", [128, NTX], F32)
        k.emit("pool", lambda e: e.iota(posi[:, 0:NT], pattern=[[128, NT]], base=0, channel_multiplier=1), [], [posi])
        k.emit("pool", lambda e: e.iota(posi[:, NT:NTX], pattern=[[0, 1]], base=2048, channel_multiplier=0), [posi], [posi])
        cp("dve", posf[:, :], posi[:, :], [posi], [posf])
        ang = k.sb("ang", [128, NTX, 8], F32)
        angi = k.sb("angi", [128, NTX, 8], I32)
        angn = k.sb("angn", [128, NTX, 8], F32)
        cor = k.sb("cor", [128, NTX, 8], F32)
        cosT = k.sb("cosT", [128, NTX, 8], F32)
        sinT = k.sb("sinT", [128, NTX, 8], F32)
        TWO_PI = 2.0 * math.pi
        for phase, dst in ((0.0, sinT), (math.pi / 2.0, cosT)):
            for i in range(8):
                inv = 500000.0 ** (-i / 8.0)
                ts("dve", ang[:, :, i], posf[:, :], inv, phase, ALU.mult, ALU.add, [posf], [ang])
            ts("dve", angn[:, :, :], ang[:, :, :], 1.0 / TWO_PI, None, ALU.mult, None, [ang], [angn])
            cp("dve", angi[:, :, :], angn[:, :, :], [angn], [angi])
            cp("dve", angn[:, :, :], angi[:, :, :], [angi], [angn])
            stt("dve", ang[:, :, :], angn[:, :, :], -TWO_PI, ang[:, :, :], ALU.mult, ALU.add, [angn, ang], [ang])
            ts("dve", cor[:, :, :], ang[:, :, :], math.pi, -TWO_PI, ALU.is_gt, ALU.mult, [ang], [cor])
            tt("dve", ang[:, :, :], ang[:, :, :], cor[:, :, :], ALU.add, [ang, cor], [ang])
            ts("dve", cor[:, :, :], ang[:, :, :], -math.pi, TWO_PI, ALU.is_lt, ALU.mult, [ang], [cor])
            tt("dve", ang[:, :, :], ang[:, :, :], cor[:, :, :], ALU.add, [ang, cor], [ang])
            act(dst[:, :, :], ang[:, :, :], AF.Sin, [ang], [dst])
        lnrow = k.sb("lnrow", [1, 128], F32)
        gln = k.grp("ln")
        k.dma("sp", lnrow[0:1, 0:64], c_idx_g[0:1, :], gln, writes=[lnrow])
        k.dma("sp", lnrow[0:1, 64:128], c_idx_b[0:1, :], gln, writes=[lnrow])
        lnrow.lw = ("d", gln, gln.count)
        lngb = k.sb("lngb", [128, 128], F32)
        pl_ = nxt("T")
        mm(pl_[:, 0:128], onesf[0:1, 0:128], lnrow[0:1, :], True, True, [onesf, lnrow], [pl_])
        cp("dve", lngb[:, :], pl_[:, 0:128], [pl_], [lngb])

        NSLOT = 3
        wslots = [k.sb("wslot%d" % i, [128, 8192], BF16) for i in range(NSLOT)]
        wgrps = [k.grp("w%d" % i) for i in range(NSLOT)]
        wplan = []

        pieces = {}

        def piece(key, mk, npart, a, b, src, dst):
            if key not in pieces:
                cb = Buf("conv_" + key, None)
                g = k.grp("cv_" + key)
                k.dma("pool", mk(dst), mk(src), g, writes=[cb])
                pieces[key] = (mk(dst), npart, a, b, cb)
            wplan.append(pieces[key])

        KC = lambda ap: ap.rearrange("(kc p) n -> p kc n", p=128)

        def plan_layer0():
            piece("a0", lambda w: KC(w)[:, :, 0:1024], 128, 8, 1024, ab_w_in, sb_win0)
            piece("b0", lambda w: KC(w)[:, :, 1024:1544], 128, 8, 520, ab_w_in, sb_win0)
            piece("c0", lambda w: KC(w)[:, :, 1544:2568], 128, 8, 1024, ab_w_in, sb_win0)
            piece("d0", lambda w: w[0:512, :].rearrange("(h p) n -> p h n", p=64), 64, 8, 1024, ab_w_out, sb_wout0)
            piece("e0", lambda w: w[512:1024, :].rearrange("(h p) n -> p h n", p=128), 128, 4, 1024, ab_w_out, sb_wout0)
            plan_ffn(0)

        def plan_ffn(l):
            for s_ in range(4):
                piece("f%d_%d" % (l, s_), lambda w, s_=s_: KC(w[l])[:, :, 1024 * s_:1024 * s_ + 1024], 128, 8, 1024, ffn_w1, sb_w1)
                piece("g%d_%d" % (l, s_), lambda w, s_=s_: w[l].rearrange("(fc p) n -> p fc n", p=128)[:, 8 * s_:8 * s_ + 8, :], 128, 8, 1024, ffn_w2, sb_w2)

        def plan_layer1():
            piece("a1", lambda w: KC(w)[:, :, 0:1024], 128, 8, 1024, c_w_in, sb_win1)
            piece("b1", lambda w: KC(w)[:, :, 1024:2048], 128, 8, 1024, c_w_in, sb_win1)
            piece("c1", lambda w: KC(w)[:, :, 2048:2120], 128, 8, 72, c_w_in, sb_win1)
            piece("d1", lambda w: w[0:512, :].rearrange("(h p) n -> p h n", p=64), 64, 8, 1024, c_w_out, sb_wout1)
            piece("e1", lambda w: w[512:1024, :].rearrange("(h p) n -> p h n", p=64), 64, 8, 1024, c_w_out, sb_wout1)
            plan_ffn(1)

        plan_layer0()
        plan_layer1()
        wplan.clear()
        if do_prompt:
            for c in range(NCH):
                plan_layer0()
        if do_dec:
            plan_layer0()
        if do_prompt:
            for c in range(NCH):
                plan_layer1()
        if do_dec:
            plan_layer1()
        wstate = {"issued": 0, "used": 0}

        def w_issue():
            i = wstate["issued"]
            if i >= len(wplan):
                return
            view, npart, a, b, cb = wplan[i]
            slot = wslots[i % NSLOT]
            dst = slot[0:npart, 0:a * b].rearrange("p (a b) -> p a b", a=a)
            k.dma("sp", dst, view, wgrps[i % NSLOT], reads=[cb], writes=[slot])
            wstate["issued"] += 1

        def w_next(keep_prev=False):
            i = wstate["used"]
            oldest = i - 1 if keep_prev else i
            while wstate["issued"] < min(len(wplan), oldest + NSLOT) or wstate["issued"] <= i:
                w_issue()
            view, npart, a, b, cb = wplan[i]
            slot = wslots[i % NSLOT]
            wstate["used"] += 1
            return slot, slot[0:npart, 0:a * b].rearrange("p (a b) -> p a b", a=a)

        def w_prefetch():
            i = wstate["used"]
            while wstate["issued"] < min(len(wplan), i + NSLOT - 1):
                w_issue()

        kTc = k.sb("kTc", [128, 4, T], BF16)
        Vc = k.sb("Vc", [128, NT, 8, 65], BF16)
        negck = k.sb("negck", [128, NT, 8], F32)
        memset("dve", Vc[:, :, :, 64:65], 1.0, [Vc])
        hcar = k.sb("hcar", [128, 4], F32)
        memset("dve", hcar[:, :], 0.0, [hcar])
        ccar = k.sb("ccar", [24, 1], F32)
        memset("dve", ccar[:, :], 0.0, [ccar])
        xrext = k.sb("xrext", [128, 4, N + 3], F32)
        memset("dve", xrext[:, :, :], 0.0, [xrext])

        hT = k.sb("hT", [128, 8, N], F32)
        xnT = k.sb("xnT", [128, 8, N], BF16)
        rstd = k.sb("rstd", [128, N], F32)
        qT = k.sb("qT", [128, 4, N], BF16)
        wf3 = k.sb("wf3", [128, 8, 24], BF16)
        kvst = k.sb("kvst", [128, 1024], F32)
        lfT = k.sb("lfT", [24, N], F32)
        cT = k.sb("cT", [24, N], F32)
        r1 = k.sb("r1", [24, N], F32)
        hib = k.sb("hib", [24, N], BF16)
        midb = k.sb("midb", [24, N], BF16)
        lob = k.sb("lob", [24, N], BF16)
        cq3f = k.sb("cq3f", [24, N], F32)
        cq3 = k.sb("cq3", [24, N], BF16)
        lfst = k.sb("lfst", [128, TPC, 8], F32)
        PTb = [k.sb("PT%d" % i, [128, N], BF16) for i in range(2)]
        rl = k.sb("rl", [128, N], F32)
        bcs = k.sb("bcs", [64, N], F32)
        attnT = k.sb("attnT", [64, 16, N], BF16)
        lruT = k.sb("lruT", [128, 4, N], BF16)
        tmp = [k.sb("tmp%d" % i, [128, N], F32) for i in range(8)]
        xcb = k.sb("xcb", [128, N], BF16)
        hid = k.sb("hid", [128, 8, N], BF16)
        sq = hid
        PT5 = [Buf("pt5_%d" % i, hid[:, 2 * i:2 * i + 2, :].rearrange("p a b -> p (a b)")) for i in range(2)]
        rtmp = [k.sb("rtmp%d" % i, [128, N], F32) for i in range(2)]
        tok1 = k.sb("tok1", [128, TPC, 2120], F32)
        kin_t = [k.sb("kin%d" % i, [128, 64], F32) for i in range(TPC)]
        lst = k.sb("lst", [128, 8], F32)
        rt1 = k.sb("rt1", [128, 20, 8], F32)
        rt2 = k.sb("rt2", [128, 20, 8], F32)
        rt3 = k.sb("rt3", [128, 20, 8], F32)
        rt4 = k.sb("rt4", [128, 20, 8], F32)
        qperm = kvst
        qT1 = k.sb("qT1", [128, 8, N], BF16)
        qiT = k.sb("qiT", [128, 4, N], BF16)
        kblk = k.sb("kblk", [128, 128], F32)
        score = BufView(tok1, tok1[:, :, :].rearrange("p a b -> p (a b)")[:, 0:T])
        wk = BufView(xtok, xtok[:, :, :].rearrange("p a b -> p (a b)"))
        m8 = k.sb("m8", [128, 8], F32)
        maskT = k.sb("maskT", [128, NT, 128], BF16)
        wis = k.sb("wis", [128, TPC, 8], F32)
        absw = k.sb("absw", [128, TPC, 8], F32)
        sgn = k.sb("sgn", [128, TPC, 8], F32)
        gdkv = k.grp("dkv")
        gdik = k.grp("dik")
        gio = k.grp("xin")
        gkv = k.grp("kvout")
        glf = k.grp("lfout")
        gh1 = k.grp("h1w")
        gh1r = k.grp("h1r")
        scrb = [Buf("scr%d" % i, None) for i in range(NCH)]
        gy = k.grp("yout")
        gsm = k.grp("small")

        def rmsnorm(gbuf, gsel, n=N):
            for dc in range(8):
                act(sq[:, dc, 0:n], hT[:, dc, 0:n], AF.Square, [hT], [sq])
            p = nxt("mm")
            for dc in range(8):
                mm(p[:, 0:n], onesb[:, :], sq[:, dc, 0:n], dc == 0, dc == 7, [onesb, sq], [p])
            act(rstd[:, 0:n], p[:, 0:n], AF.Sqrt, [p], [rstd], scale=1.0 / D, bias=1e-6)
            k.emit("dve", lambda e: e.reciprocal(out=rstd[:, 0:n], in_=rstd[:, 0:n]), [rstd], [rstd])
            for dc in range(8):
                stt("dve", xnT[:, dc, 0:n], hT[:, dc, 0:n], gsel(dc), rstd[:, 0:n], ALU.mult, ALU.mult, [hT, gbuf, rstd], [xnT])

        def ffn(n=N):
            for s in range(4):
                wb1, w1 = w_next()
                for fj in range(8):
                    p = nxt("mm")
                    for kc in range(8):
                        mm(p[:, 0:n], w1[:, kc, 128 * fj:128 * fj + 128], xnT[:, kc, 0:n], kc == 0, kc == 7, [wb1, xnT], [p])
                    rt = rtmp[fj % 2]
                    act(rt[:, 0:n], p[:, 0:n], AF.Relu, [p], [rt])
                    tt("dve", hid[:, fj, 0:n], rt[:, 0:n], rt[:, 0:n], ALU.mult, [rt], [hid])
                wb2, w2 = w_next()
                for ob in range(8):
                    p = nxt("mm")
                    for fj in range(8):
                        mm(p[:, 0:n], w2[:, fj, 128 * ob:128 * ob + 128], hid[:, fj, 0:n], fj == 0, fj == 7, [wb2, hid], [p])
                    tt("dve", hT[:, ob, 0:n], hT[:, ob, 0:n], p[:, 0:n], ALU.add, [hT, p], [hT])

        ND = 16
        tri = k.sb("tri", [128, 128], F32)
        memset("pool", tri[:, :], 1.0, [tri])
        asel(tri[:, :], tri[:, :], ALU.is_ge, 0.0, -1, [[-1, 128]], 1, [tri], [tri])
        Mmat = k.sb("Mmat", [128, 128], F32)
        memset("pool", Mmat[:, :], 0.0, [Mmat])
        for m_ in range(1, 16):
            asel(Mmat[:, :], Mmat[:, :], ALU.not_equal, 1.0, -8 * m_, [[-1, 128]], 1, [Mmat], [Mmat])
        pt_sb = k.sb("pt_sb", [1, 256], I32)
        gpt = k.grp("pt")
        k.dma("sp", pt_sb[:, :], pt_in[:, :], gpt, writes=[pt_sb])
        kTn = k.sb("kTn", [128, 4, ND], BF16)
        Vn = k.sb("Vn", [ND, 8, 65], BF16)
        memset("dve", Vn[:, :, 64:65], 1.0, [Vn])
        lfpg = k.sb("lfpg", [128, 16, 8], F32)
        Tsb = k.sb("Tsb", [128, 128], F32)
        sml = k.sb("sml", [ND, 64], F32)
        pnm = k.sb("pnm", [ND, ND, 16], BF16)
        cvT = k.sb("cvT", [128, 4, ND, 3], F32)
        cvn = k.sb("cvn", [128, 4, ND, 3], F32)
        h0T = k.sb("h0T", [128, 4, ND], F32)
        hsd = k.sb("hsd", [128, 4, ND], F32)
        hsave = k.sb("hsave", [128, 8, ND], F32)
        sgnm = k.sb("sgnm", [ND, ND, 8], F32)
        maskTd = k.sb("maskTd", [128, 16, ND], F32)
        stage = BufView(tok1, tok1[:, :, :].rearrange("p a b -> p (a b)")[:, 0:4096].rearrange("p (i f) -> p i f", i=4))
        negflat = BufView(negck, negck[:, :, :].rearrange("p a b -> p (a b)"))
        gst = k.grp("stage")
        glp = k.grp("lfpg")
        gds = k.grp("decsmall")
        gdo = k.grp("decout")
        scrs = Buf("scrs", None)
        regs = {}

        ptf = k.sb("ptf", [1, 256], F32)
        cp("dve", ptf[:, :], pt_sb[:, :], [pt_sb], [ptf])
        pidx = k.sb("pidx", [128, 1], I32)
        pidf = k.sb("pidf", [128, 1], F32)
        k.emit("pool", lambda e: e.iota(pidx[:, :], pattern=[[0, 1]], base=0, channel_multiplier=1), [], [pidx])
        cp("dve", pidf[:, :], pidx[:, :], [pidx], [pidf])
        pix = nxt("T")
        mm(pix[:, 0:256], onesf[0:1, 0:128], ptf[0:1, :], True, True, [onesf, ptf], [pix])
        idxf = rtmp[0]
        ts("dve", idxf[:, 0:256], pix[:, 0:256], 128.0, pidf[:, 0:1], ALU.mult, ALU.add, [pix, pidf], [idxf])
        idx_all = k.sb("idx_all", [128, 256], I32)
        cp("dve", idx_all[:, :], idxf[:, 0:256], [idxf], [idx_all])

        def dyn_dma(out_ap, cache, col, grp, writes):
            k.dmaf("pool", lambda e: e.indirect_dma_start(out=out_ap, out_offset=None, in_=cache[:, :],
                                                           in_offset=bass.IndirectOffsetOnAxis(ap=idx_all[:, col:col + 1], axis=0)),
                   grp, reads=[idx_all], writes=writes)

        def load_x_dec():
            k.dma("sp", xtok[0:ND, 0, :], x_s[:, :], gio, writes=[xtok])
            p = nxt("T")
            for dc in range(8):
                k.emit("pe", lambda e, p=p, dc=dc: e.transpose(out=p[:, ND * dc:ND * dc + ND], in_=xtok[0:ND, 0, 128 * dc:128 * dc + 128], identity=ident[0:ND, 0:ND]), [xtok, ident], [p])
            cp("act", hT[:, :, 0:ND], p[:, 0:8 * ND].rearrange("p (a t) -> p a t", a=8), [p], [hT])

        def fox_decode_seq(b):
            for g in range(4):
                for i in range(4):
                    dyn_dma(stage[:, i, :], c_fkv, 16 * b + 4 * g + i, gst, [stage] if i in (0, 3) else [])
                for i in range(4):
                    pg = 4 * g + i
                    p = nxt("T")
                    for hp in range(4):
                        k.emit("pe", lambda e, p=p, i=i, hp=hp: e.transpose(out=p[:, 128 * hp:128 * hp + 128], in_=stage[:, i, 128 * hp:128 * hp + 128], identity=ident[:, :]), [stage, ident], [p])
                    cp("act", kTc[:, :, 128 * pg:128 * pg + 128], p[:, :].rearrange("p (i t) -> p i t", i=4), [p], [kTc])
                    cp("dve", Vc[:, pg, :, 0:64], stage[:, i, 512:1024].rearrange("p (h d) -> p h d", h=8), [stage], [Vc])
            for pg in range(16):
                dyn_dma(lfpg[:, pg, :], c_flf, 16 * b + pg, glp, [lfpg] if pg in (0, 15) else [])
            lff = lfpg[:, :, :].rearrange("p a b -> p (a b)")
            pb_ = nxt("mm")
            pt_ = nxt("T")
            mm(pt_[:, 0:128], lff, onesf[:, 0:128], True, True, [lfpg, onesf], [pt_])
            cp("act", Tsb[:, :], pt_[:, 0:128], [pt_], [Tsb])
            mm(pb_[:, 0:128], tri[:, :], lff, True, False, [tri, lfpg], [pb_])
            mm(pb_[:, 0:128], Tsb[:, :], Mmat[:, :], False, True, [Tsb, Mmat], [pb_])
            cp("act", negflat[:, :], pb_[:, 0:128], [pb_], [negflat])
            S = nxt("S")
            for j in range(16):
                for h in range(8):
                    hp, pbs = h // 2, 64 * (h % 2)
                    mm(S[:, 8 * j + h:8 * j + h + 1], kTc[pbs:pbs + 64, hp, 128 * j:128 * j + 128], qT[pbs:pbs + 64, hp, b:b + 1], True, True, [kTc, qT], [S])
            sd = rtmp[0]
            tt("dve", sd[:, 0:128], S[:, 0:128], negflat[:, :], ALU.add, [S, negflat], [sd])
            PT = PTb[0]
            act(PT[:, 0:128], sd[:, 0:128], AF.Exp, [sd], [PT])
            O = nxt("O")
            for h in range(8):
                for j in range(16):
                    mm(O[0:65, h:h + 1], Vc[:, j, h, :], PT[:, 8 * j + h:8 * j + h + 1], j == 0, False, [Vc, PT], [O])
                mm(O[0:65, h:h + 1], Vn[:, h, :], pnm[:, b, h:h + 1], False, True, [Vn, pnm], [O])
            k.emit("dve", lambda e, O=O: e.reciprocal(out=rl[64:65, 0:8], in_=O[64:65, 0:8]), [O], [rl])
            B = nxt("T")
            mm(B[0:64, 0:8], onesf[64:65, 0:64], rl[64:65, 0:8], True, True, [onesf, rl], [B])
            cp("act", bcs[:, 0:8], B[0:64, 0:8], [B], [bcs])
            tt("dve", attnT[:, 0:8, b:b + 1].rearrange("p h o -> p (h o)"), O[0:64, 0:8], bcs[:, 0:8], ALU.mult, [O, bcs], [attnT])

        def layer0_decode():
            n = ND
            load_x_dec()
            rmsnorm(gmix, lambda dc: gmix[:, 0, dc:dc + 1], n)
            wbA, wA = w_next()
            for hp in range(4):
                p = nxt("mm")
                for kc in range(8):
                    mm(p[:, 0:n], wA[:, kc, 128 * hp:128 * hp + 128], xnT[:, kc, 0:n], kc == 0, kc == 7, [wbA, xnT], [p])
                ts("dve", qT[:, hp, 0:n], p[:, 0:n], 0.125, None, ALU.mult, None, [p], [qT])
                p2 = nxt("mm")
                for kc in range(8):
                    mm(p2[:, 0:n], wA[:, kc, 512 + 128 * hp:512 + 128 * hp + 128], xnT[:, kc, 0:n], kc == 0, kc == 7, [wbA, xnT], [p2])
                cp("dve", kTn[:, hp, :], p2[:, 0:n], [p2], [kTn])
            wbB, wB = w_next(keep_prev=True)
            for j in range(3):
                cp("dve", wf3[:, :, 8 * j:8 * j + 8], wB[:, :, 512:520], [wbB], [wf3])
            qtok = sct[0]
            pq = nxt("mm")
            for kc in range(8):
                mm(pq[0:n, 0:512], xnT[:, kc, 0:n], wA[:, kc, 0:512], kc == 0, kc == 7, [wbA, xnT], [pq])
            cp("act", qtok[0:n, :], pq[0:n, 0:512], [pq], [qtok])
            pk = nxt("mm")
            for kc in range(8):
                mm(pk[0:n, 0:512], xnT[:, kc, 0:n], wA[:, kc, 512:1024], kc == 0, kc == 7, [wbA, xnT], [pk])
            cp("act", kvst[0:n, 0:512], pk[0:n, 0:512], [pk], [kvst])
            pv = nxt("mm")
            for kc in range(8):
                mm(pv[0:n, 0:512], xnT[:, kc, 0:n], wB[:, kc, 0:512], kc == 0, kc == 7, [wbB, xnT], [pv])
            cp("act", kvst[0:n, 512:1024], pv[0:n, 0:512], [pv], [kvst])
            cp("dve", Vn[:, :, 0:64], kvst[0:n, 512:1024].rearrange("p (h d) -> p h d", h=8), [kvst], [Vn])
            k.dma("sp", fox_kv_s[:, :], kvst[0:n, :], gdo, reads=[kvst])
            prod = sct[1]
            tt("dve", prod[0:n, :], qtok[0:n, :], kvst[0:n, 0:512], ALU.mult, [qtok, kvst], [prod])
            k.emit("dve", lambda e: e.reduce_sum(out=sml[:, 0:8], in_=prod[0:n, :].rearrange("p (h d) -> p h d", h=8), axis=AX.X), [prod], [sml])
            pf = nxt("mm")
            for kc in range(8):
                mm(pf[0:24, 0:n], wf3[:, kc, :], xnT[:, kc, 0:n], kc == 0, kc == 7, [wf3, xnT], [pf])
            act(lfT[:, 0:n], pf[0:24, 0:n], AF.Exp, [pf], [lfT], scale=-1.0, bias=nbf3[:, 0:1])
            act(lfT[:, 0:n], lfT[:, 0:n], AF.Ln, [lfT], [lfT], bias=1.0)
            ts("dve", lfT[:, 0:n], lfT[:, 0:n], -1.0, None, ALU.mult, None, [lfT], [lfT])
            p = nxt("T")
            k.emit("pe", lambda e, p=p: e.transpose(out=p[0:n, 0:8], in_=lfT[0:8, 0:n], identity=ident[0:8, 0:8]), [lfT, ident], [p])
            cp("dve", sml[:, 8:16], p[0:n, 0:8], [p], [sml])
            k.dma("sp", fox_logf_s[:, :], sml[:, 8:16], gdo, reads=[sml], **NCD)
            stt("dve", sml[:, 16:24], sml[:, 0:8], 0.125, sml[:, 8:16], ALU.mult, ALU.subtract, [sml], [sml])
            act(sml[:, 24:32], sml[:, 16:24], AF.Exp, [sml], [sml])
            for b in range(ND):
                ts("dve", pnm[:, b, 0:8], sml[:, 24:32], ident[0:ND, b:b + 1], None, ALU.mult, None, [sml, ident], [pnm])
            for b in range(ND):
                fox_decode_seq(b)
            k.dma("sp", sct[0][0:48, :], conv_s[:, :], gds, writes=[sct[0]])
            k.dma("sp", sct[1][0:ND, :], h_s[:, :], gds, writes=[sct[1]])
            for rb in range(4):
                p = nxt("T")
                k.emit("pe", lambda e, p=p, rb=rb: e.transpose(out=p[:, 0:48], in_=sct[0][0:48, 128 * rb:128 * rb + 128], identity=ident[0:48, 0:48]), [sct[0], ident], [p])
                k.emit("pe", lambda e, p=p, rb=rb: e.transpose(out=p[:, 64:64 + ND], in_=sct[1][0:ND, 128 * rb:128 * rb + 128], identity=ident[0:ND, 0:ND]), [sct[1], ident], [p])
                cp("act", cvT[:, rb, :, :].rearrange("p b j -> p (b j)"), p[:, 0:48], [p], [cvT])
                cp("act", h0T[:, rb, :], p[:, 64:64 + ND], [p], [h0T])
            wbC, wC = w_next()
            for rb in range(4):
                xc, gt_, rr_, ii_, aa_, uu_, xr_, gl_ = tmp
                p = nxt("mm")
                for kc in range(8):
                    mm(p[:, 0:n], wC[:, kc, 128 * rb:128 * rb + 128], xnT[:, kc, 0:n], kc == 0, kc == 7, [wbC, xnT], [p])
                cp("act", xr_[:, 0:n], p[:, 0:n], [p], [xr_])
                pg = nxt("mm")
                for kc in range(8):
                    mm(pg[:, 0:n], wC[:, kc, 512 + 128 * rb:512 + 128 * rb + 128], xnT[:, kc, 0:n], kc == 0, kc == 7, [wbC, xnT], [pg])
                cp("act", gt_[:, 0:n], pg[:, 0:n], [pg], [gt_])
                ts("dve", xc[:, 0:n], xr_[:, 0:n], convw[:, 3, rb:rb + 1], convb[:, rb:rb + 1], ALU.mult, ALU.add, [xr_, convw, convb], [xc])
                for j in range(3):
                    stt("dve", xc[:, 0:n], cvT[:, rb, :, j], convw[:, j, rb:rb + 1], xc[:, 0:n], ALU.mult, ALU.add, [cvT, convw, xc], [xc])
                cp("dve", cvn[:, rb, :, 0], cvT[:, rb, :, 1], [cvT], [cvn])
                cp("dve", cvn[:, rb, :, 1], cvT[:, rb, :, 2], [cvT], [cvn])
                cp("dve", cvn[:, rb, :, 2], xr_[:, 0:n], [xr_], [cvn])
                cp("dve", xcb[:, 0:n], xc[:, 0:n], [xc], [xcb])
                pr = nxt("mm")
                mm(pr[:, 0:n], gabd[:, rb, :], xcb[:, 0:n], True, True, [gabd, xcb], [pr])
                pi = nxt("mm")
                mm(pi[:, 0:n], gxbd[:, rb, :], xcb[:, 0:n], True, True, [gxbd, xcb], [pi])
                act(rr_[:, 0:n], pr[:, 0:n], AF.Sigmoid, [pr, gab], [rr_], bias=gab[:, rb:rb + 1])
                act(ii_[:, 0:n], pi[:, 0:n], AF.Sigmoid, [pi, gxb], [ii_], bias=gxb[:, rb:rb + 1])
                act(aa_[:, 0:n], rr_[:, 0:n], AF.Exp, [rr_, clam], [aa_], scale=clam[:, rb:rb + 1])
                tt("dve", uu_[:, 0:n], aa_[:, 0:n], aa_[:, 0:n], ALU.mult, [aa_], [uu_])
                ts("dve", uu_[:, 0:n], uu_[:, 0:n], -1.0, 1.0, ALU.mult, ALU.add, [uu_], [uu_])
                act(uu_[:, 0:n], uu_[:, 0:n], AF.Sqrt, [uu_], [uu_])
                tt("dve", uu_[:, 0:n], uu_[:, 0:n], ii_[:, 0:n], ALU.mult, [uu_, ii_], [uu_])
                tt("dve", uu_[:, 0:n], uu_[:, 0:n], xc[:, 0:n], ALU.mult, [uu_, xc], [uu_])
                tt("dve", aa_[:, 0:n], aa_[:, 0:n], h0T[:, rb, :], ALU.mult, [aa_, h0T], [aa_])
                tt("dve", hsd[:, rb, :], aa_[:, 0:n], uu_[:, 0:n], ALU.add, [aa_, uu_], [hsd])
                tt("dve", gl_[:, 0:n], gt_[:, 0:n], gt_[:, 0:n], ALU.mult, [gt_], [gl_])
                ts("dve", gl_[:, 0:n], gl_[:, 0:n], 0.044715, 1.0, ALU.mult, ALU.add, [gl_], [gl_])
                tt("dve", gl_[:, 0:n], gl_[:, 0:n], gt_[:, 0:n], ALU.mult, [gl_, gt_], [gl_])
                act(gl_[:, 0:n], gl_[:, 0:n], AF.Tanh, [gl_], [gl_], scale=0.7978845608028654)
                stt("dve", gl_[:, 0:n], gl_[:, 0:n], 1.0, gt_[:, 0:n], ALU.add, ALU.mult, [gl_, gt_], [gl_])
                stt("dve", lruT[:, rb, 0:n], gl_[:, 0:n], 0.5, hsd[:, rb, :], ALU.mult, ALU.mult, [gl_, hsd], [lruT])
            p = nxt("T")
            p2 = nxt("T")
            for rb in range(4):
                k.emit("pe", lambda e, p=p, rb=rb: e.transpose(out=p[0:48, 128 * rb:128 * rb + 128], in_=cvn[:, rb, :, :].rearrange("p b j -> p (b j)"), identity=ident[:, :]), [cvn, ident], [p])
                k.emit("pe", lambda e, p2=p2, rb=rb: e.transpose(out=p2[0:ND, 128 * rb:128 * rb + 128], in_=hsd[:, rb, :], identity=ident[:, :]), [hsd, ident], [p2])
            cp("act", sct[0][0:48, :], p[0:48, :], [p], [sct[0]])
            cp("act", sct[1][0:ND, :], p2[0:ND, :], [p2], [sct[1]])
            k.dma("sp", lru_conv_s[:, :], sct[0][0:48, :], gdo, reads=[sct[0]])
            k.dma("sp", lru_h_s[:, :], sct[1][0:ND, :], gdo, reads=[sct[1]])
            wbD, wD = w_next()
            wbE, wE = w_next(keep_prev=True)
            for ob in range(8):
                p = nxt("mm")
                for h in range(8):
                    mm(p[:, 0:n], wD[:, h, 128 * ob:128 * ob + 128], attnT[:, h, 0:n], h == 0, False, [wbD, attnT], [p])
                for rb in range(4):
                    mm(p[:, 0:n], wE[:, rb, 128 * ob:128 * ob + 128], lruT[:, rb, 0:n], False, rb == 3, [wbE, lruT], [p])
                tt("dve", hT[:, ob, 0:n], hT[:, ob, 0:n], p[:, 0:n], ALU.add, [hT, p], [hT])
            rmsnorm(gffn, lambda dc: gffn[:, 0, dc:dc + 1], n)
            ffn(n)
            for dc in range(8):
                cp("dve", hsave[:, dc, :], hT[:, dc, 0:n], [hT], [hsave])

        def layer1_decode():
            n = ND
            gt = NT
            for dc in range(8):
                cp("dve", hT[:, dc, 0:n], hsave[:, dc, :], [hsave], [hT])
            rmsnorm(gmix, lambda dc: gmix[:, 1, dc:dc + 1], n)
            col0 = 0
            for pi_, wcols in enumerate((1024, 1024, 72)):
                wbX, wX = w_next()
                for cb in range(0, wcols, 512):
                    wdt = min(512, wcols - cb)
                    p = nxt("mm")
                    for kc in range(8):
                        mm(p[0:n, 0:wdt], xnT[:, kc, 0:n], wX[:, kc, cb:cb + wdt], kc == 0, kc == 7, [wbX, xnT], [p])
                    cp("act", tok1[0:n, 0, col0 + cb:col0 + cb + wdt], p[0:n, 0:wdt], [p], [tok1])
                col0 += wcols
            kin = kin_t[0]
            cp("dve", Vn[:, 0:4, 0:64], tok1[0:n, 0, 1280:1536].rearrange("p (h d) -> p h d", h=4), [tok1], [Vn])
            k.emit("dve", lambda e: e.reduce_sum(out=lst[0:n, 0:1], in_=tok1[0:n, 0, 2048:2112], axis=AX.X), [tok1], [lst])
            ts("dve", lst[0:n, 0:1], lst[0:n, 0:1], 1.0 / 64.0, None, ALU.mult, None, [lst], [lst])
            ts("dve", kin[0:n, :], tok1[0:n, 0, 2048:2112], lst[0:n, 0:1], None, ALU.subtract, None, [tok1, lst], [kin])
            tt("dve", rt1[0:n, 0:8, :].rearrange("p a b -> p (a b)"), kin[0:n, :], kin[0:n, :], ALU.mult, [kin], [rt1])
            k.emit("dve", lambda e: e.reduce_sum(out=lst[0:n, 1:2], in_=rt1[0:n, 0:8, :].rearrange("p a b -> p (a b)"), axis=AX.X), [rt1], [lst])
            act(lst[0:n, 2:3], lst[0:n, 1:2], AF.Sqrt, [lst], [lst], scale=1.0 / 64.0, bias=1e-6)
            k.emit("dve", lambda e: e.reciprocal(out=lst[0:n, 3:4], in_=lst[0:n, 2:3]), [lst], [lst])
            ts("dve", kin[0:n, :], kin[0:n, :], lst[0:n, 3:4], None, ALU.mult, None, [kin, lst], [kin])
            tt("dve", kin[0:n, :], kin[0:n, :], lngb[0:n, 0:64], ALU.mult, [kin, lngb], [kin])
            tt("dve", kin[0:n, :], kin[0:n, :], lngb[0:n, 64:128], ALU.add, [kin, lngb], [kin])
            for (vw, H, bufv) in ((tok1[0:n, 0, 0:1280].rearrange("p (h d) -> p h d", d=64), 20, tok1),
                                  (tok1[0:n, 0, 1536:2048].rearrange("p (h d) -> p h d", d=64), 8, tok1),
                                  (kin[0:n, :].rearrange("p (h d) -> p h d", d=64), 1, kin)):
                cs = cosT[0:n, gt:gt + 1, :].to_broadcast([n, H, 8])
                sn = sinT[0:n, gt:gt + 1, :].to_broadcast([n, H, 8])
                x1 = vw[:, :, 0:8]
                x2 = vw[:, :, 8:16]
                tt("dve", rt1[0:n, 0:H, :], x1, cs, ALU.mult, [bufv, cosT], [rt1])
                tt("dve", rt2[0:n, 0:H, :], x2, sn, ALU.mult, [bufv, sinT], [rt2])
                tt("dve", rt3[0:n, 0:H, :], x2, cs, ALU.mult, [bufv, cosT], [rt3])
                tt("dve", rt4[0:n, 0:H, :], x1, sn, ALU.mult, [bufv, sinT], [rt4])
                tt("dve", x1, rt1[0:n, 0:H, :], rt2[0:n, 0:H, :], ALU.subtract, [rt1, rt2], [bufv])
                tt("dve", x2, rt3[0:n, 0:H, :], rt4[0:n, 0:H, :], ALU.add, [rt3, rt4], [bufv])
            k.dma("sp", dsa_kv_s[:, :], tok1[0:n, 0, 1024:1536], gdo, reads=[tok1])
            k.dma("sp", dsa_idxk_s[:, :], kin[0:n, :], gdo, reads=[kin])
            for a in range(2):
                src = tok1[0:n, 0, 512 * a:512 * a + 512].rearrange("p (b cc d) -> p cc b d", b=2, cc=4)
                dst = qperm[0:n, 512 * a:512 * a + 512].rearrange("p (cc b d) -> p cc b d", cc=4, b=2)
                ts("dve", dst, src, 0.125, None, ALU.mult, None, [tok1], [qperm])
            p = nxt("T")
            for blk in range(8):
                k.emit("pe", lambda e, p=p, blk=blk: e.transpose(out=p[:, ND * blk:ND * blk + ND], in_=qperm[0:ND, 128 * blk:128 * blk + 128], identity=ident[0:ND, 0:ND]), [qperm, ident], [p])
            cp("act", qT1[:, :, 0:n], p[:, 0:8 * ND].rearrange("p (i t) -> p i t", i=8), [p], [qT1])
            p = nxt("T")
            for i in range(4):
                k.emit("pe", lambda e, p=p, i=i: e.transpose(out=p[:, ND * i:ND * i + ND], in_=tok1[0:ND, 0, 1536 + 128 * i:1536 + 128 * i + 128], identity=ident[0:ND, 0:ND]), [tok1, ident], [p])
            cp("act", qiT[:, :, 0:n], p[:, 0:4 * ND].rearrange("p (i t) -> p i t", i=4), [p], [qiT])
            ts("dve", wis[0:n, 0, :], tok1[0:n, 0, 2112:2120], 512.0 ** -0.5, None, ALU.mult, None, [tok1], [wis])
            act(absw[0:n, 0, :], wis[0:n, 0, :], AF.Abs, [wis], [absw])
            k.emit("act", lambda e: e.sign(out=sgn[0:n, 0, :], in_=wis[0:n, 0, :]), [wis], [sgn])
            for b in range(ND):
                ts("dve", sgnm[:, b, :], sgn[0:n, 0, :], ident[0:ND, b:b + 1], None, ALU.mult, None, [sgn, ident], [sgnm])
            qiv = tok1[0:n, 0, 1536:2048].rearrange("p (h d) -> p h d", h=8)
            kib = kin[0:n, :].rearrange("p (o d) -> p o d", o=1).to_broadcast([n, 8, 64])
            prod = sct[1]
            tt("dve", prod[0:n, :].rearrange("p (h d) -> p h d", h=8), qiv, kib, ALU.mult, [tok1, kin], [prod])
            k.emit("dve", lambda e: e.reduce_sum(out=sml[:, 0:8], in_=prod[0:n, :].rearrange("p (h d) -> p h d", h=8), axis=AX.X), [prod], [sml])
            ts("dve", sml[:, 0:8], sml[:, 0:8], 0.0, None, ALU.max, None, [sml], [sml])
            tt("dve", sml[:, 0:8], sml[:, 0:8], wis[0:n, 0, :], ALU.mult, [sml, wis], [sml])
            k.emit("dve", lambda e: e.reduce_sum(out=sml[:, 8:9], in_=sml[:, 0:8], axis=AX.X), [sml], [sml])
            qv = tok1[0:n, 0, 0:1024].rearrange("p (kv g d) -> p kv g d", kv=4, g=4)
            kv_ = tok1[0:n, 0, 1024:1280].rearrange("p (kv d) -> p kv d", kv=4)
            prod2 = qperm
            for g in range(4):
                tt("dve", prod2[0:n, :].rearrange("p (kv g d) -> p kv g d", kv=4, g=4)[:, :, g, :], qv[:, :, g, :], kv_, ALU.mult, [tok1], [prod2])
            k.emit("dve", lambda e: e.reduce_sum(out=sml[:, 16:32], in_=prod2[0:n, :].rearrange("p (h d) -> p h d", h=16), axis=AX.X), [prod2], [sml])
            scd = BufView(tok1, tok1[:, :, :].rearrange("p a b -> p (a b)")[0:ND, 0:2049])
            wkd = BufView(tok1, tok1[:, :, :].rearrange("p a b -> p (a b)")[0:ND, 2100:4149])
            memset("dve", scd[:, :], 0.0, [scd])
            cp("dve", scd[:, 2048:2049], sml[:, 8:9], [sml], [scd])
            stg = BufView(xtok, xtok[:, :, :].rearrange("p a b -> p (a b)").rearrange("p (i f) -> p i f", i=4))
            for b in range(ND):
                for g in range(4):
                    for i in range(4):
                        dyn_dma(stg[:, i, 0:64], c_dik, 16 * b + 4 * g + i, gst, [stg] if i in (0, 3) else [])
                    p = nxt("T")
                    for i in range(4):
                        cp("dve", kblk[:, 0:64], stg[:, i, 0:64], [stg], [kblk])
                        cp("dve", kblk[:, 64:128], stg[:, i, 0:64], [stg], [kblk])
                        k.emit("pe", lambda e, p=p, i=i: e.transpose(out=p[:, 128 * i:128 * i + 128], in_=kblk[:, :], identity=ident[:, :]), [kblk, ident], [p])
                    cp("act", kTc[:, 2, 512 * g:512 * g + 512], p[:, :], [p], [kTc])
                for h in range(8):
                    half, blk = h % 2, h // 2
                    for kb0 in range(0, 2048, 512):
                        p = nxt("S")
                        mm(p[0:n, 0:512], qiT[64 * half:64 * half + 64, blk, 0:n], kTc[64 * half:64 * half + 64, 2, kb0:kb0 + 512], True, True, [qiT, kTc], [p])
                        sc_ = sct[(h + kb0 // 512) % 2]
                        act(sc_[0:n, :], p[0:n, 0:512], AF.Relu, [p, absw], [sc_], scale=absw[0:n, 0, h:h + 1])
                        stt("dve", scd[:, kb0:kb0 + 512], sc_[0:n, :], sgnm[:, b, h:h + 1], scd[:, kb0:kb0 + 512], ALU.mult, ALU.add, [sc_, sgnm, scd], [scd])
            srcb = scd
            for r in range(32):
                k.emit("dve", lambda e, srcb=srcb: e.max(out=m8[0:n, :], in_=srcb[:, :]), [srcb], [m8])
                if r < 31:
                    k.emit("dve", lambda e, srcb=srcb: e.match_replace(out=wkd[:, :], in_to_replace=m8[0:n, :], in_values=srcb[:, :], imm_value=-1e30), [srcb, m8], [wkd])
                    srcb = wkd
            ts("dve", wkd[:, :], scd[:, :], m8[0:n, 7:8], None, ALU.is_ge, None, [scd, m8], [wkd])
            ts("dve", wkd[:, :], wkd[:, :], -1.0, 30000.0, ALU.add, ALU.mult, [wkd], [wkd])
            for g in range(4):
                p = nxt("T")
                for i in range(4):
                    pg = 4 * g + i
                    k.emit("pe", lambda e, p=p, i=i, pg=pg: e.transpose(out=p[:, ND * i:ND * i + ND], in_=wkd[:, 128 * pg:128 * pg + 128], identity=ident[0:ND, 0:ND]), [wkd, ident], [p])
                cp("act", maskTd[:, 4 * g:4 * g + 4, :], p[:, 0:4 * ND].rearrange("p (i t) -> p i t", i=4), [p], [maskTd])
            for h in range(16):
                stt("dve", sml[:, 32 + h:33 + h], sml[:, 16 + h:17 + h], 0.125, wkd[:, 2048:2049], ALU.mult, ALU.add, [sml, wkd], [sml])
            act(sml[:, 48:64], sml[:, 32:48], AF.Exp, [sml], [sml])
            for b in range(ND):
                ts("dve", pnm[:, b, :], sml[:, 48:64], ident[0:ND, b:b + 1], None, ALU.mult, None, [sml, ident], [pnm])
            stg2 = BufView(xtok, xtok[:, :, :].rearrange("p a b -> p (a b)").rearrange("p (i f) -> p i f", i=4))
            for b in range(ND):
                for g in range(4):
                    for i in range(4):
                        dyn_dma(stg2[:, i, :], c_dkv, 16 * b + 4 * g + i, gst, [stg2] if i in (0, 3) else [])
                    for i in range(4):
                        pg = 4 * g + i
                        p = nxt("T")
                        for blk in range(2):
                            k.emit("pe", lambda e, p=p, i=i, blk=blk: e.transpose(out=p[:, 128 * blk:128 * blk + 128], in_=stg2[:, i, 128 * blk:128 * blk + 128], identity=ident[:, :]), [stg2, ident], [p])
                        cp("act", kTc[:, 0:2, 128 * pg:128 * pg + 128], p[:, 0:256].rearrange("p (i t) -> p i t", i=2), [p], [kTc])
                        cp("dve", Vc[:, pg, 0:4, 0:64], stg2[:, i, 256:512].rearrange("p (h d) -> p h d", h=4), [stg2], [Vc])
                S = nxt("S")
                for j in range(16):
                    for a in range(2):
                        for b2 in range(2):
                            h0, pbs = 8 * a + 4 * b2, 64 * b2
                            for cc in range(4):
                                mm(S[:, 16 * j + h0 + cc:16 * j + h0 + cc + 1], kTc[pbs:pbs + 64, a, 128 * j:128 * j + 128],
                                   qT1[pbs:pbs + 64, 4 * a + cc, b:b + 1], True, True, [kTc, qT1], [S])
                PT = PTb[0]
                for j in range(16):
                    act(PT[:, 16 * j:16 * j + 16], S[:, 16 * j:16 * j + 16], AF.Exp, [S, maskTd], [PT], bias=maskTd[:, j, b:b + 1])
                O = nxt("O")
                for a in range(2):
                    for b2 in range(2):
                        h0, kvh = 8 * a + 4 * b2, 2 * a + b2
                        for j in range(16):
                            mm(O[0:65, h0:h0 + 4], Vc[:, j, kvh, :], PT[:, 16 * j + h0:16 * j + h0 + 4], j == 0, False, [Vc, PT], [O])
                        mm(O[0:65, h0:h0 + 4], Vn[:, kvh, :], pnm[:, b, h0:h0 + 4], False, True, [Vn, pnm], [O])
                k.emit("dve", lambda e, O=O: e.reciprocal(out=rl[64:65, 0:16], in_=O[64:65, 0:16]), [O], [rl])
                B = nxt("T")
                mm(B[0:64, 0:16], onesf[64:65, 0:64], rl[64:65, 0:16], True, True, [onesf, rl], [B])
                cp("act", bcs[:, 0:16], B[0:64, 0:16], [B], [bcs])
                tt("dve", attnT[:, 0:16, b:b + 1].rearrange("p h o -> p (h o)"), O[0:64, 0:16], bcs[:, 0:16], ALU.mult, [O, bcs], [attnT])
            wbD, wD = w_next()
            wbE, wE = w_next(keep_prev=True)
            for ob in range(8):
                p = nxt("mm")
                for h in range(16):
                    wsl, wbuf = (wD, wbD) if h < 8 else (wE, wbE)
                    mm(p[:, 0:n], wsl[:, h % 8, 128 * ob:128 * ob + 128], attnT[:, h, 0:n], h == 0, h == 15, [wbuf, attnT], [p])
                tt("dve", hT[:, ob, 0:n], hT[:, ob, 0:n], p[:, 0:n], ALU.add, [hT, p], [hT])
            rmsnorm(gffn, lambda dc: gffn[:, 1, dc:dc + 1], n)
            ffn(n)
            rmsnorm(gfin, lambda dc: gfin[:, dc:dc + 1], n)
            p = nxt("T")
            p2 = nxt("T")
            for dc in range(8):
                yt = tmp[dc % 4]
                stt("dve", yt[:, 0:n], hT[:, dc, 0:n], gfin[:, dc:dc + 1], rstd[:, 0:n], ALU.mult, ALU.mult, [hT, gfin, rstd], [yt])
                pp = p if dc < 4 else p2
                k.emit("pe", lambda e, pp=pp, dc=dc, yt=yt: e.transpose(out=pp[0:ND, 128 * (dc % 4):128 * (dc % 4) + 128], in_=yt[:, 0:ND], identity=ident[:, :]), [yt, ident], [pp])
            cp("act", kvst[0:n, 0:512], p[0:n, :], [p], [kvst])
            cp("act", kvst[0:n, 512:1024], p2[0:n, :], [p2], [kvst])
            k.dma("sp", y_s[:, :], kvst[0:n, :], gdo, reads=[kvst])

        for c in (range(NCH) if do_prompt else []):
            t0 = c * N
            k.dma("sp", xtok[:, :, :], x_p[t0:t0 + N, :].rearrange("(j p) d -> p j d", p=128), gio, writes=[xtok])
            for j in range(TPC):
                for g in range(2):
                    p = nxt("T")
                    for i in range(4):
                        dc = 4 * g + i
                        k.emit("pe", lambda e, p=p, i=i, j=j, dc=dc: e.transpose(out=p[:, 128 * i:128 * i + 128], in_=xtok[:, j, 128 * dc:128 * dc + 128], identity=ident[:, :]), [xtok, ident], [p])
                    cp("act", hT[:, 4 * g:4 * g + 4, 128 * j:128 * j + 128], p[:, :].rearrange("p (i t) -> p i t", i=4), [p], [hT])
            rmsnorm(gmix, lambda dc: gmix[:, 0, dc:dc + 1])
            wbA, wA = w_next()
            for hp in range(4):
                p = nxt("mm")
                for kc in range(8):
                    mm(p[:, 0:N], wA[:, kc, 128 * hp:128 * hp + 128], xnT[:, kc, :], kc == 0, kc == 7, [wbA, xnT], [p])
                ts("dve", qT[:, hp, :], p[:, 0:N], 0.125, None, ALU.mult, None, [p], [qT])
                p2 = nxt("mm")
                for kc in range(8):
                    mm(p2[:, 0:N], wA[:, kc, 512 + 128 * hp:512 + 128 * hp + 128], xnT[:, kc, :], kc == 0, kc == 7, [wbA, xnT], [p2])
                cp("dve", kTc[:, hp, t0:t0 + N], p2[:, 0:N], [p2], [kTc])
            wbB, wB = w_next(keep_prev=True)
            for j in range(3):
                cp("dve", wf3[:, :, 8 * j:8 * j + 8], wB[:, :, 512:520], [wbB], [wf3])
            for j in range(TPC):
                gt = c * TPC + j
                pk = nxt("mm")
                for kc in range(8):
                    mm(pk[:, 0:512], xnT[:, kc, 128 * j:128 * j + 128], wA[:, kc, 512:1024], kc == 0, kc == 7, [wbA, xnT], [pk])
                pv = nxt("mm")
                for kc in range(8):
                    mm(pv[:, 0:512], xnT[:, kc, 128 * j:128 * j + 128], wB[:, kc, 0:512], kc == 0, kc == 7, [wbB, xnT], [pv])
                cp("act", kvst[:, 0:512], pk[:, 0:512], [pk], [kvst])
                cp("dve", kvst[:, 512:1024], pv[:, 0:512], [pv], [kvst])
                cp("dve", Vc[:, gt, :, 0:64], kvst[:, 512:1024].rearrange("p (h d) -> p h d", h=8), [kvst], [Vc])
                k.dma("sp", fox_kv_p[t0 + 128 * j:t0 + 128 * j + 128, :], kvst[:, :], gkv, reads=[kvst])
            pf = nxt("mm")
            for kc in range(8):
                mm(pf[0:24, 0:N], wf3[:, kc, :], xnT[:, kc, :], kc == 0, kc == 7, [wf3, xnT], [pf])
            act(lfT[:, :], pf[0:24, 0:N], AF.Exp, [pf], [lfT], scale=-1.0, bias=nbf3[:, 0:1])
            act(lfT[:, :], lfT[:, :], AF.Ln, [lfT], [lfT], bias=1.0)
            ts("dve", lfT[:, :], lfT[:, :], -1.0, None, ALU.mult, None, [lfT], [lfT])
            k.emit("dve", lambda e: e.tensor_tensor_scan(out=cT[:, :], data0=onesf[0:24, 0:N], data1=lfT[:, :], initial=ccar[:, 0:1], op0=ALU.mult, op1=ALU.add), [onesf, lfT, ccar], [cT])
            cp("dve", ccar[:, 0:1], cT[:, N - 1:N], [cT], [ccar])
            cp("dve", hib[:, :], cT[:, :], [cT], [hib])
            tt("dve", r1[:, :], cT[:, :], hib[:, :], ALU.subtract, [cT, hib], [r1])
            cp("dve", midb[:, :], r1[:, :], [r1], [midb])
            tt("dve", r1[:, :], r1[:, :], midb[:, :], ALU.subtract, [r1, midb], [r1])
            cp("dve", lob[:, :], r1[:, :], [r1], [lob])
            ts("dve", cq3f[:, :], hib[:, :], m3[:, 0:1], None, ALU.mult, None, [hib, m3], [cq3f])
            stt("dve", cq3f[:, :], midb[:, :], m3[:, 1:2], cq3f[:, :], ALU.mult, ALU.add, [midb, m3, cq3f], [cq3f])
            stt("dve", cq3f[:, :], lob[:, :], m3[:, 2:3], cq3f[:, :], ALU.mult, ALU.add, [lob, m3, cq3f], [cq3f])
            cp("dve", cq3[:, :], cq3f[:, :], [cq3f], [cq3])
            for j in range(TPC):
                gt = c * TPC + j
                p = nxt("T")
                k.emit("pe", lambda e, p=p, j=j: e.transpose(out=p[:, 0:8], in_=lfT[0:8, 128 * j:128 * j + 128], identity=ident[0:8, 0:8]), [lfT, ident], [p])
                k.emit("pe", lambda e, p=p, j=j: e.transpose(out=p[:, 8:16], in_=cT[0:8, 128 * j:128 * j + 128], identity=ident[0:8, 0:8]), [cT, ident], [p])
                cp("dve", lfst[:, j, :], p[:, 0:8], [p], [lfst])
                ts("dve", negck[:, gt, :], p[:, 8:16], -1.0, None, ALU.mult, None, [p], [negck])
            k.dma("sp", fox_logf_p[t0:t0 + N, :].rearrange("(j p) h -> p j h", p=128), lfst[:, :, :], glf, reads=[lfst], **NCD)
            for h in range(8):
                hp, pb = h // 2, 64 * (h % 2)
                O = nxt("O")
                nk = (c + 1) * TPC

                def emitS(kt, h=h, hp=hp, pb=pb):
                    j = kt - c * TPC
                    qlo = 0 if j < 0 else 128 * j
                    S = nxt("S")
                    mm(S[:, qlo:N], kTc[pb:pb + 64, hp, 128 * kt:128 * kt + 128], qT[pb:pb + 64, hp, qlo:N], True, False, [kTc, qT], [S])
                    if j >= 0:
                        mm(S[:, qlo:qlo + 128], identb[:, :], cmask[:, :], False, False, [identb, cmask], [S])
                    mm(S[:, qlo:N], ind[:, h, :], cq3[:, qlo:N], False, True, [ind, cq3], [S])
                    return S, qlo

                cur = emitS(0)
                for kt in range(nk):
                    nx = emitS(kt + 1) if kt + 1 < nk else None
                    S, qlo = cur
                    PT = PTb[kt % 2]
                    act(PT[:, qlo:N], S[:, qlo:N], AF.Exp, [S, negck], [PT], bias=negck[:, kt, h:h + 1])
                    mm(O[0:65, qlo:N], Vc[:, kt, h, :], PT[:, qlo:N], kt == 0, kt == nk - 1, [Vc, PT], [O])
                    cur = nx
                k.emit("dve", lambda e, O=O: e.reciprocal(out=rl[64:65, :], in_=O[64:65, 0:N]), [O], [rl])
                B = nxt("T")
                mm(B[0:64, 0:N], onesf[64:65, 0:64], rl[64:65, :], True, True, [onesf, rl], [B])
                cp("act", bcs[:, :], B[0:64, 0:N], [B], [bcs])
                tt("dve", attnT[:, h, :], O[0:64, 0:N], bcs[:, :], ALU.mult, [O, bcs], [attnT])
            wbC, wC = w_next()
            for rb in range(4):
                xc, gt_, rr_, ii_, aa_, uu_, hs_, gl_ = tmp
                cp("dve", xrext[:, rb, 0:3], xrext[:, rb, N:N + 3], [xrext], [xrext])
                p = nxt("mm")
                for kc in range(8):
                    mm(p[:, 0:N], wC[:, kc, 128 * rb:128 * rb + 128], xnT[:, kc, :], kc == 0, kc == 7, [wbC, xnT], [p])
                cp("act", xrext[:, rb, 3:3 + N], p[:, 0:N], [p], [xrext])
                pg = nxt("mm")
                for kc in range(8):
                    mm(pg[:, 0:N], wC[:, kc, 512 + 128 * rb:512 + 128 * rb + 128], xnT[:, kc, :], kc == 0, kc == 7, [wbC, xnT], [pg])
                cp("act", gt_[:, :], pg[:, 0:N], [pg], [gt_])
                ts("dve", xc[:, :], xrext[:, rb, 3:3 + N], convw[:, 3, rb:rb + 1], convb[:, rb:rb + 1], ALU.mult, ALU.add, [xrext, convw, convb], [xc])
                for j in range(3):
                    stt("dve", xc[:, :], xrext[:, rb, j:j + N], convw[:, j, rb:rb + 1], xc[:, :], ALU.mult, ALU.add, [xrext, convw, xc], [xc])
                cp("dve", xcb[:, :], xc[:, :], [xc], [xcb])
                pr = nxt("mm")
                mm(pr[:, 0:N], gabd[:, rb, :], xcb[:, :], True, True, [gabd, xcb], [pr])
                pi = nxt("mm")
                mm(pi[:, 0:N], gxbd[:, rb, :], xcb[:, :], True, True, [gxbd, xcb], [pi])
                act(rr_[:, :], pr[:, 0:N], AF.Sigmoid, [pr, gab], [rr_], bias=gab[:, rb:rb + 1])
                act(ii_[:, :], pi[:, 0:N], AF.Sigmoid, [pi, gxb], [ii_], bias=gxb[:, rb:rb + 1])
                act(aa_[:, :], rr_[:, :], AF.Exp, [rr_, clam], [aa_], scale=clam[:, rb:rb + 1])
                tt("dve", uu_[:, :], aa_[:, :], aa_[:, :], ALU.mult, [aa_], [uu_])
                ts("dve", uu_[:, :], uu_[:, :], -1.0, 1.0, ALU.mult, ALU.add, [uu_], [uu_])
                act(uu_[:, :], uu_[:, :], AF.Sqrt, [uu_], [uu_])
                tt("dve", uu_[:, :], uu_[:, :], ii_[:, :], ALU.mult, [uu_, ii_], [uu_])
                tt("dve", uu_[:, :], uu_[:, :], xc[:, :], ALU.mult, [uu_, xc], [uu_])
                k.emit("dve", lambda e, rb=rb, aa_=aa_, uu_=uu_, hs_=hs_: e.tensor_tensor_scan(out=hs_[:, :], data0=aa_[:, :], data1=uu_[:, :], initial=hcar[:, rb:rb + 1], op0=ALU.mult, op1=ALU.add), [aa_, uu_, hcar], [hs_])
                cp("dve", hcar[:, rb:rb + 1], hs_[:, N - 1:N], [hs_], [hcar])
                tt("dve", gl_[:, :], gt_[:, :], gt_[:, :], ALU.mult, [gt_], [gl_])
                ts("dve", gl_[:, :], gl_[:, :], 0.044715, 1.0, ALU.mult, ALU.add, [gl_], [gl_])
                tt("dve", gl_[:, :], gl_[:, :], gt_[:, :], ALU.mult, [gl_, gt_], [gl_])
                act(gl_[:, :], gl_[:, :], AF.Tanh, [gl_], [gl_], scale=0.7978845608028654)
                stt("dve", gl_[:, :], gl_[:, :], 1.0, gt_[:, :], ALU.add, ALU.mult, [gl_, gt_], [gl_])
                stt("dve", lruT[:, rb, :], gl_[:, :], 0.5, hs_[:, :], ALU.mult, ALU.mult, [gl_, hs_], [lruT])
            if c == NCH - 1:
                for rb in range(4):
                    k.dma("sp", lru_conv_p[:, 128 * rb:128 * rb + 128].rearrange("j p -> p j"), xrext[:, rb, N:N + 3], gsm, reads=[xrext], **NCD)
                k.dma("sp", lru_h_p[0, :].rearrange("(c p) -> p c", p=128), hcar[:, :], gsm, reads=[hcar], **NCD)
            wbD, wD = w_next()
            wbE, wE = w_next(keep_prev=True)
            for ob in range(8):
                p = nxt("mm")
                for h in range(8):
                    mm(p[:, 0:N], wD[:, h, 128 * ob:128 * ob + 128], attnT[:, h, :], h == 0, False, [wbD, attnT], [p])
                for rb in range(4):
                    mm(p[:, 0:N], wE[:, rb, 128 * ob:128 * ob + 128], lruT[:, rb, :], False, rb == 3, [wbE, lruT], [p])
                tt("dve", hT[:, ob, :], hT[:, ob, :], p[:, 0:N], ALU.add, [hT, p], [hT])
            rmsnorm(gffn, lambda dc: gffn[:, 0, dc:dc + 1])
            ffn()
            k.dma("sp", h1_scr[c, :, :], hT[:, :, :].rearrange("p a b -> p (a b)"), gh1, reads=[hT], writes=[scrb[c]])

        if do_dec:
            layer0_decode()
        for c in (range(NCH) if do_prompt else []):
            t0 = c * N
            k.dma("sp", hT[:, :, :].rearrange("p a b -> p (a b)"), h1_scr[c, :, :], gh1r, reads=[scrb[c]], writes=[hT])
            rmsnorm(gmix, lambda dc: gmix[:, 1, dc:dc + 1])
            col0 = 0
            for pi_, wcols in enumerate((1024, 1024, 72)):
                wbX, wX = w_next()
                for j in range(TPC):
                    for cb in range(0, wcols, 512):
                        wdt = min(512, wcols - cb)
                        p = nxt("mm")
                        for kc in range(8):
                            mm(p[:, 0:wdt], xnT[:, kc, 128 * j:128 * j + 128], wX[:, kc, cb:cb + wdt], kc == 0, kc == 7, [wbX, xnT], [p])
                        cp("act", tok1[:, j, col0 + cb:col0 + cb + wdt], p[:, 0:wdt], [p], [tok1])
                col0 += wcols
            for j in range(TPC):
                gt = c * TPC + j
                kin = kin_t[j]
                cp("dve", Vc[:, gt, 0:4, 0:64], tok1[:, j, 1280:1536].rearrange("p (h d) -> p h d", h=4), [tok1], [Vc])
                k.emit("dve", lambda e, j=j: e.reduce_sum(out=lst[:, 0:1], in_=tok1[:, j, 2048:2112], axis=AX.X), [tok1], [lst])
                ts("dve", lst[:, 0:1], lst[:, 0:1], 1.0 / 64.0, None, ALU.mult, None, [lst], [lst])
                ts("dve", kin[:, :], tok1[:, j, 2048:2112], lst[:, 0:1], None, ALU.subtract, None, [tok1, lst], [kin])
                tt("dve", rt1[:, 0:8, :].rearrange("p a b -> p (a b)"), kin[:, :], kin[:, :], ALU.mult, [kin], [rt1])
                k.emit("dve", lambda e: e.reduce_sum(out=lst[:, 1:2], in_=rt1[:, 0:8, :].rearrange("p a b -> p (a b)"), axis=AX.X), [rt1], [lst])
                act(lst[:, 2:3], lst[:, 1:2], AF.Sqrt, [lst], [lst], scale=1.0 / 64.0, bias=1e-6)
                k.emit("dve", lambda e: e.reciprocal(out=lst[:, 3:4], in_=lst[:, 2:3]), [lst], [lst])
                ts("dve", kin[:, :], kin[:, :], lst[:, 3:4], None, ALU.mult, None, [kin, lst], [kin])
                tt("dve", kin[:, :], kin[:, :], lngb[:, 0:64], ALU.mult, [kin, lngb], [kin])
                tt("dve", kin[:, :], kin[:, :], lngb[:, 64:128], ALU.add, [kin, lngb], [kin])
                for (vw, H, bufv) in ((tok1[:, j, 0:1280].rearrange("p (h d) -> p h d", d=64), 20, tok1),
                                      (tok1[:, j, 1536:2048].rearrange("p (h d) -> p h d", d=64), 8, tok1),
                                      (kin[:, :].rearrange("p (h d) -> p h d", d=64), 1, kin)):
                    cs = cosT[:, gt:gt + 1, :].to_broadcast([128, H, 8])
                    sn = sinT[:, gt:gt + 1, :].to_broadcast([128, H, 8])
                    x1 = vw[:, :, 0:8]
                    x2 = vw[:, :, 8:16]
                    tt("dve", rt1[:, 0:H, :], x1, cs, ALU.mult, [bufv, cosT], [rt1])
                    tt("dve", rt2[:, 0:H, :], x2, sn, ALU.mult, [bufv, sinT], [rt2])
                    tt("dve", rt3[:, 0:H, :], x2, cs, ALU.mult, [bufv, cosT], [rt3])
                    tt("dve", rt4[:, 0:H, :], x1, sn, ALU.mult, [bufv, sinT], [rt4])
                    tt("dve", x1, rt1[:, 0:H, :], rt2[:, 0:H, :], ALU.subtract, [rt1, rt2], [bufv])
                    tt("dve", x2, rt3[:, 0:H, :], rt4[:, 0:H, :], ALU.add, [rt3, rt4], [bufv])
                k.dma("sp", dsa_kv_p[t0 + 128 * j:t0 + 128 * j + 128, :], tok1[:, j, 1024:1536], gdkv, reads=[tok1])
                k.dma("sp", dsa_idxk_p[t0 + 128 * j:t0 + 128 * j + 128, :], kin[:, :], gdik, reads=[kin])
            for j in range(TPC):
                gt = c * TPC + j
                for a in range(2):
                    src = tok1[:, j, 512 * a:512 * a + 512].rearrange("p (b cc d) -> p cc b d", b=2, cc=4)
                    dst = qperm[:, 512 * a:512 * a + 512].rearrange("p (cc b d) -> p cc b d", cc=4, b=2)
                    ts("dve", dst, src, 0.125, None, ALU.mult, None, [tok1], [qperm])
                for g in range(2):
                    p = nxt("T")
                    for i in range(4):
                        blk = 4 * g + i
                        k.emit("pe", lambda e, p=p, i=i, blk=blk: e.transpose(out=p[:, 128 * i:128 * i + 128], in_=qperm[:, 128 * blk:128 * blk + 128], identity=ident[:, :]), [qperm, ident], [p])
                    cp("act", qT1[:, 4 * g:4 * g + 4, 128 * j:128 * j + 128], p[:, :].rearrange("p (i t) -> p i t", i=4), [p], [qT1])
                p = nxt("T")
                for i in range(4):
                    k.emit("pe", lambda e, p=p, i=i, j=j: e.transpose(out=p[:, 128 * i:128 * i + 128], in_=tok1[:, j, 1536 + 128 * i:1536 + 128 * i + 128], identity=ident[:, :]), [tok1, ident], [p])
                cp("act", qiT[:, :, 128 * j:128 * j + 128], p[:, :].rearrange("p (i t) -> p i t", i=4), [p], [qiT])
                cp("dve", kblk[:, 0:64], kin_t[j][:, :], [kin_t[j]], [kblk])
                cp("dve", kblk[:, 64:128], kin_t[j][:, :], [kin_t[j]], [kblk])
                p = nxt("T")
                for i in range(2):
                    k.emit("pe", lambda e, p=p, i=i, j=j: e.transpose(out=p[:, 128 * i:128 * i + 128], in_=tok1[:, j, 1024 + 128 * i:1024 + 128 * i + 128], identity=ident[:, :]), [tok1, ident], [p])
                k.emit("pe", lambda e, p=p: e.transpose(out=p[:, 256:384], in_=kblk[:, :], identity=ident[:, :]), [kblk, ident], [p])
                cp("act", kTc[:, 0:3, 128 * gt:128 * gt + 128], p[:, 0:384].rearrange("p (i t) -> p i t", i=3), [p], [kTc])
                ts("dve", wis[:, j, :], tok1[:, j, 2112:2120], 512.0 ** -0.5, None, ALU.mult, None, [tok1], [wis])
                act(absw[:, j, :], wis[:, j, :], AF.Abs, [wis], [absw])
                k.emit("act", lambda e, j=j: e.sign(out=sgn[:, j, :], in_=wis[:, j, :]), [wis], [sgn])
            for j in range(TPC):
                gt = c * TPC + j
                L = 128 * (gt + 1)
                for h in range(8):
                    half, blk = h % 2, h // 2
                    for kb0 in range(0, L, 512):
                        wdt = min(512, L - kb0)
                        p = nxt("S")
                        mm(p[:, 0:wdt], qiT[64 * half:64 * half + 64, blk, 128 * j:128 * j + 128], kTc[64 * half:64 * half + 64, 2, kb0:kb0 + wdt], True, True, [qiT, kTc], [p])
                        sc_ = sct[(h + kb0 // 512) % 2]
                        act(sc_[:, 0:wdt], p[:, 0:wdt], AF.Relu, [p, absw], [sc_], scale=absw[:, j, h:h + 1])
                        if h == 0:
                            ts("dve", score[:, kb0:kb0 + wdt], sc_[:, 0:wdt], sgn[:, j, 0:1], None, ALU.mult, None, [sc_, sgn], [score])
                        else:
                            stt("dve", score[:, kb0:kb0 + wdt], sc_[:, 0:wdt], sgn[:, j, h:h + 1], score[:, kb0:kb0 + wdt], ALU.mult, ALU.add, [sc_, sgn, score], [score])
                tt("dve", score[:, L - 128:L], score[:, L - 128:L], cmq[:, :], ALU.add, [score, cmq], [score])
                if gt >= 2:
                    srcb = score
                    for r in range(32):
                        k.emit("dve", lambda e, srcb=srcb, L=L: e.max(out=m8[:, :], in_=srcb[:, 0:L]), [srcb], [m8])
                        if r < 31:
                            k.emit("dve", lambda e, srcb=srcb, L=L: e.match_replace(out=wk[:, 0:L], in_to_replace=m8[:, :], in_values=srcb[:, 0:L], imm_value=-1e30), [srcb, m8], [wk])
                            srcb = wk
                    ts("dve", wk[:, 0:L], score[:, 0:L], m8[:, 7:8], None, ALU.is_ge, None, [score, m8], [wk])
                else:
                    ts("dve", wk[:, 0:L], score[:, 0:L], -1e29, None, ALU.is_gt, None, [score], [wk])
                ts("dve", wk[:, 0:L], wk[:, 0:L], -1.0, 30000.0, ALU.add, ALU.mult, [wk], [wk])
                for kb0 in range(0, gt + 1, 4):
                    nb = min(4, gt + 1 - kb0)
                    p = nxt("T")
                    for i in range(nb):
                        k.emit("pe", lambda e, p=p, i=i, kb0=kb0: e.transpose(out=p[:, 128 * i:128 * i + 128], in_=wk[:, 128 * (kb0 + i):128 * (kb0 + i) + 128], identity=ident[:, :]), [wk, ident], [p])
                    cp("act", maskT[:, kb0:kb0 + nb, :], p[:, 0:128 * nb].rearrange("p (i t) -> p i t", i=nb), [p], [maskT])
                memset("dve", PT5[0][:, 0:1], 0.0, [hid, PT5[0], PT5[1]])
                for a in range(2):
                    for b in range(2):
                        pbs, kvh, h0 = 64 * b, 2 * a + b, 8 * a + 4 * b
                        O = nxt("O")

                        def emitS1(kb, a=a, pbs=pbs, j=j):
                            S = nxt("S")
                            for cc in range(4):
                                mm(S[:, 128 * cc:128 * cc + 128], kTc[pbs:pbs + 64, a, 128 * kb:128 * kb + 128],
                                   qT1[pbs:pbs + 64, 4 * a + cc, 128 * j:128 * j + 128], True, False, [kTc, qT1], [S])
                                mm(S[:, 128 * cc:128 * cc + 128], identb[:, :], maskT[:, kb, :], False, True, [identb, maskT], [S])
                            return S

                        cur = emitS1(0)
                        for kb in range(gt + 1):
                            nx = emitS1(kb + 1) if kb + 1 <= gt else None
                            S = cur
                            PT = PT5[kb % 2]
                            act(PT[:, :], S[:, 0:512], AF.Exp, [S], [PT])
                            mm(O[0:65, 0:512], Vc[:, kb, kvh, :], PT[:, :], kb == 0, kb == gt, [Vc, PT], [O])
                            cur = nx
                        for hf in range(2):
                            k.emit("dve", lambda e, O=O, hf=hf: e.reciprocal(out=rl[64:65, 0:256], in_=O[64:65, 256 * hf:256 * hf + 256]), [O], [rl])
                            B = nxt("T")
                            mm(B[0:64, 0:256], onesf[64:65, 0:64], rl[64:65, 0:256], True, True, [onesf, rl], [B])
                            cp("act", bcs[:, 0:256], B[0:64, 0:256], [B], [bcs])
                            tt("dve", attnT[:, h0 + 2 * hf:h0 + 2 * hf + 2, 128 * j:128 * j + 128],
                               O[0:64, 256 * hf:256 * hf + 256].rearrange("p (c q) -> p c q", c=2),
                               bcs[:, 0:256].rearrange("p (c q) -> p c q", c=2), ALU.mult, [O, bcs], [attnT])
                memset("dve", PT5[0][:, 0:1], 0.0, [hid, PT5[0], PT5[1]])
            wbD, wD = w_next()
            wbE, wE = w_next(keep_prev=True)
            for ob in range(8):
                p = nxt("mm")
                for h in range(16):
                    wsl, wbuf = (wD, wbD) if h < 8 else (wE, wbE)
                    mm(p[:, 0:N], wsl[:, h % 8, 128 * ob:128 * ob + 128], attnT[:, h, :], h == 0, h == 15, [wbuf, attnT], [p])
                tt("dve", hT[:, ob, :], hT[:, ob, :], p[:, 0:N], ALU.add, [hT, p], [hT])
            rmsnorm(gffn, lambda dc: gffn[:, 1, dc:dc + 1])
            ffn()
            rmsnorm(gfin, lambda dc: gfin[:, dc:dc + 1])
            for j in range(TPC):
                for g in range(2):
                    p = nxt("T")
                    for i in range(4):
                        dc = 4 * g + i
                        yt = tmp[i]
                        stt("dve", yt[:, 0:128], hT[:, dc, 128 * j:128 * j + 128], gfin[:, dc:dc + 1], rstd[:, 128 * j:128 * j + 128], ALU.mult, ALU.mult, [hT, gfin, rstd], [yt])
                        k.emit("pe", lambda e, p=p, i=i, yt=yt: e.transpose(out=p[:, 128 * i:128 * i + 128], in_=yt[:, 0:128], identity=ident[:, :]), [yt, ident], [p])
                    cp("act", xtok[:, j, 512 * g:512 * g + 512], p[:, :], [p], [xtok])
            k.dma("sp", y_p[t0:t0 + N, :].rearrange("(j p) d -> p j d", p=128), xtok[:, :, :], gy, reads=[xtok])

        if do_dec:
            layer1_decode()
        k.finish()
        k.replay()
    return nc


_OUT_SHAPES = None


def kernel(x_prompt, x_sample, cache_fox_kv, cache_fox_logf, state_lru_conv, state_lru_h,
           cache_dsa_kv, cache_dsa_idx_k, page_table, norm_mix, norm_ffn, norm_final,
           ab_w_in, ab_b_f, ab_conv_w, ab_conv_b, ab_gate_a_w, ab_gate_a_b, ab_gate_x_w, ab_gate_x_b,
           ab_lambda, ab_w_out, c_w_in, c_idx_norm_g, c_idx_norm_b, c_w_out, ffn_w1, ffn_w2):
    f = lambda a: np.ascontiguousarray(np.asarray(a), dtype=np.float32)
    shared = {
        "norm_mix": f(norm_mix), "norm_ffn": f(norm_ffn), "norm_final": f(norm_final).reshape(1, D),
        "ab_w_in": f(ab_w_in)[0], "ab_b_f": f(ab_b_f).reshape(1, 8), "ab_conv_w": f(ab_conv_w)[0],
        "ab_conv_b": f(ab_conv_b).reshape(1, 512), "ab_ga_w": f(ab_gate_a_w)[0], "ab_ga_b": f(ab_gate_a_b).reshape(1, 512),
        "ab_gx_w": f(ab_gate_x_w)[0], "ab_gx_b": f(ab_gate_x_b).reshape(1, 512), "ab_lambda": f(ab_lambda).reshape(1, 512),
        "ab_w_out": f(ab_w_out)[0], "c_w_in": f(c_w_in)[0], "c_idx_g": f(c_idx_norm_g).reshape(1, 64),
        "c_idx_b": f(c_idx_norm_b).reshape(1, 64), "c_w_out": f(c_w_out)[0], "ffn_w1": f(ffn_w1), "ffn_w2": f(ffn_w2),
    }
    xp = f(x_prompt)
    xs = f(x_sample).reshape(128, D)
    pt = np.ascontiguousarray(np.asarray(page_table), dtype=np.int32)
    convs = f(state_lru_conv)[0]
    hs = f(state_lru_h)[0]
    n_phys = int(np.asarray(cache_fox_kv).shape[1])
    shared["c_fkv"] = f(cache_fox_kv)[0].reshape(n_phys * 128, 1024)
    shared["c_flf"] = f(cache_fox_logf)[0].reshape(n_phys * 128, 8)
    shared["c_dkv"] = f(cache_dsa_kv)[0].reshape(n_phys * 128, 512)
    shared["c_dik"] = f(cache_dsa_idx_k)[0].reshape(n_phys * 128, 64)
    nc = build_nc(n_phys)
    in_maps = []
    for i in range(8):
        m = dict(shared)
        m["x_p"] = xp[i]
        m["x_s"] = xs[16 * i:16 * i + 16]
        m["pt"] = pt[16 * i:16 * i + 16].reshape(1, 256)
        m["conv_s"] = convs[16 * i:16 * i + 16].reshape(48, 512)
        m["h_s"] = hs[16 * i:16 * i + 16]
        in_maps.append(m)
    res = run_bass_kernel_spmd(nc, in_maps, core_ids=list(range(8)))
    R = res.results
    return assemble(R)


def assemble(R):
    B, DB = 8, 128
    cat = lambda name: np.concatenate([R[i][name] for i in range(8)], axis=0)
    st = lambda name: np.stack([R[i][name] for i in range(8)])
    y_prompt = st("y_p").reshape(B, T, D)
    y_sample = cat("y_s").reshape(DB, 1, D)
    fox_kv_p = st("fox_kv_p").reshape(1, B, T, 2, 8, 64)
    fox_logf_p = st("fox_logf_p").reshape(1, B, T, 8)
    lru_conv_p = st("lru_conv_p").reshape(1, B, 3, 512)
    lru_h_p = st("lru_h_p").reshape(1, B, 512)
    dsa_kv_p = st("dsa_kv_p").reshape(1, B, T, 2, 4, 64)
    dsa_idxk_p = st("dsa_idxk_p").reshape(1, B, T, 64)
    fox_kv_s = cat("fox_kv_s").reshape(1, DB, 1, 2, 8, 64)
    fox_logf_s = cat("fox_logf_s").reshape(1, DB, 1, 8)
    lru_conv_s = cat("lru_conv_s").reshape(1, DB, 3, 512)
    lru_h_s = cat("lru_h_s").reshape(1, DB, 512)
    dsa_kv_s = cat("dsa_kv_s").reshape(1, DB, 1, 2, 4, 64)
    dsa_idxk_s = cat("dsa_idxk_s").reshape(1, DB, 1, 64)
    return (y_prompt, y_sample, fox_kv_p, fox_logf_p, lru_conv_p, lru_h_p, dsa_kv_p, dsa_idxk_p,
            fox_kv_s, fox_logf_s, lru_conv_s, lru_h_s, dsa_kv_s, dsa_idxk_s)
```
